# Optimizing a Trainium2 kernel written in Bass

```python
import math
import jax, jax.numpy as jnp
from jax import lax
import numpy as np

D_MODEL = 4096
BATCH = 4
SEQ = 2048
DEPTH = 1
DEC_BATCH = 128
DEC_SEQ = 1
PAST_LEN = 16384
PAGE_SIZE = 128

RET_HEADS = 8
RET_DIM = 256
ML_HEADS = 8
ML_DIM = 256
RET_W = RET_HEADS * RET_DIM
ML_W = ML_HEADS * ML_DIM
MIX_W = RET_W + ML_W
IN_COLS = 4 * RET_W + 4 * ML_W + 2 * ML_HEADS
CHUNK = 128
CONV_W = 4
ROPE_BASE = 10000.0
PEER_HEADS = 8
N_KEYS = 128
N_EXPERTS = N_KEYS * N_KEYS
PEER_TOPK = 16
PEER_QDIM = 256
PEER_BLOCK = 64
PLE_DIM = 256
LN_EPS = 1e-5
DEEPNORM_ALPHA = (2.0 * DEPTH) ** 0.25
DEEPNORM_BETA = (8.0 * DEPTH) ** -0.25

kernel_name = "hybrid_retention_mlstm_peer_decoder_step"

F32 = jnp.float32


def layer_norm(x, g, b):
    xf = x.astype(F32)
    mu = xf.mean(-1, keepdims=True)
    var = jnp.mean(jnp.square(xf - mu), -1, keepdims=True)
    return ((xf - mu) * lax.rsqrt(var + LN_EPS)).astype(x.dtype) * g + b


def head_norm(h, g):
    mu = h.mean(-1, keepdims=True)
    var = jnp.mean(jnp.square(h - mu), -1, keepdims=True)
    return (h - mu) * lax.rsqrt(var + LN_EPS) * g.astype(F32)


def rope(x, pos):
    half = x.shape[-1] // 2
    inv = ROPE_BASE ** (-jnp.arange(half, dtype=F32) / half)
    ang = pos.astype(F32)[:, None] * inv[None]
    cos = jnp.cos(ang)[None, :, None, :].astype(x.dtype)
    sin = jnp.sin(ang)[None, :, None, :].astype(x.dtype)
    x1, x2 = x[..., :half], x[..., half:]
    return jnp.concatenate([x1 * cos - x2 * sin, x1 * sin + x2 * cos], -1)


def to_chunks(a, chunk):
    B, T = a.shape[:2]
    return jnp.moveaxis(a.reshape(B, T // chunk, chunk, *a.shape[2:]), 1, 0)


def from_chunks(a):
    a = jnp.moveaxis(a, 0, 1)
    return a.reshape(a.shape[0], a.shape[1] * a.shape[2], *a.shape[3:])


def retention(q, k, v, s0, chunk):
    H = q.shape[2]
    log_g = jnp.log1p(-(2.0 ** (-5.0 - jnp.arange(H, dtype=F32))))
    idx = jnp.arange(chunk, dtype=F32)
    diff = idx[:, None] - idx[None, :]
    causal = diff >= 0
    decay_in = jnp.where(causal[None], jnp.exp(log_g[:, None, None] * jnp.where(causal, diff, 0.0)[None]), 0.0)
    decay_q = jnp.exp(log_g[None, :] * (idx + 1.0)[:, None])[None, :, :, None]
    decay_k = jnp.exp(log_g[None, :] * (chunk - 1.0 - idx)[:, None])[None, :, :, None]
    decay_c = jnp.exp(log_g * chunk)[None, :, None, None]

    def step(s, inp):
        qc, kc, vc = inp
        sc = jnp.einsum('bihd,bjhd->bhij', qc, kc) * decay_in
        o = jnp.einsum('bhij,bjhe->bihe', sc, vc) + jnp.einsum('bihd,bhde->bihe', qc, s) * decay_q
        s = s * decay_c + jnp.einsum('bjhd,bjhe->bhde', kc * decay_k, vc)
        return s, o

    s, o = lax.scan(step, s0, (to_chunks(q, chunk), to_chunks(k, chunk), to_chunks(v, chunk)))
    return from_chunks(o), s


def mlstm(q, k, v, ig, lf, c0, n0, m0, chunk):
    idx = jnp.arange(chunk)
    causal = idx[:, None] >= idx[None, :]

    def step(carry, inp):
        c, n, m = carry
        qc, kc, vc, ic, fc = inp
        bt = jnp.cumsum(fc, axis=1).transpose(0, 2, 1)
        it = ic.transpose(0, 2, 1)
        dmat = jnp.where(causal, bt[..., :, None] - bt[..., None, :] + it[..., None, :], -jnp.inf)
        prior = bt + m[..., None]
        mt = jnp.maximum(prior, dmat.max(-1))
        w = jnp.exp(dmat - mt[..., None])
        wp = jnp.exp(prior - mt).transpose(0, 2, 1)
        qk = jnp.einsum('bihd,bjhd->bhij', qc, kc) * w
        num = jnp.einsum('bhij,bjhe->bihe', qk, vc) + jnp.einsum('bihd,bhde->bihe', qc, c) * wp[..., None]
        den = qk.sum(-1).transpose(0, 2, 1) + jnp.einsum('bihd,bhd->bih', qc, n) * wp
        h = num / jnp.maximum(jnp.abs(den), jnp.exp(-mt).transpose(0, 2, 1))[..., None]
        bl = bt[..., -1]
        m_new = mt[..., -1]
        wk = jnp.exp(bl[..., None] - bt + it - m_new[..., None])
        wc = jnp.exp(bl + m - m_new)
        c_new = c * wc[..., None, None] + jnp.einsum('bjhd,bhj,bjhe->bhde', kc, wk, vc)
        n_new = n * wc[..., None] + jnp.einsum('bjhd,bhj->bhd', kc, wk)
        return (c_new, n_new, m_new), h

    xs = tuple(to_chunks(a, chunk) for a in (q, k, v, ig, lf))
    (c, n, m), h = lax.scan(step, (c0, n0, m0), xs)
    return from_chunks(h), c, n, m


def causal_conv(x, buf, w, b):
    T = x.shape[1]
    xx = jnp.concatenate([buf.astype(x.dtype), x], 1)
    y = sum(xx[:, j:j + T] * w[j] for j in range(CONV_W)) + b
    return y, xx[:, -(CONV_W - 1):]


def token_mixers(x, pos, chunk, s_ret, conv_buf, c0, n0, m0, w_in, b_gate, conv_w, conv_b, g_ret_norm, g_ml_norm, w_out):
    B, T, _ = x.shape
    z = x @ w_in
    offs = [RET_W, 2 * RET_W, 3 * RET_W, 4 * RET_W, 4 * RET_W + 2 * ML_W, 4 * RET_W + 3 * ML_W, 4 * RET_W + 4 * ML_W]
    rq, rk, rv, rg, mqk, mv, mo, gates = jnp.split(z, offs, axis=-1)
    heads = lambda a, d: a.reshape(B, T, -1, d)
    rq = rope(heads(rq, RET_DIM), pos).astype(F32)
    rk = (rope(heads(rk, RET_DIM), pos) * RET_DIM ** -0.5).astype(F32)
    o_r, s_new = retention(rq, rk, heads(rv, RET_DIM).astype(F32), s_ret.astype(F32), chunk)
    o_r = head_norm(o_r, g_ret_norm).reshape(B, T, RET_W).astype(x.dtype) * jax.nn.silu(rg)
    qk_c, buf_new = causal_conv(mqk, conv_buf, conv_w, conv_b)
    mq, mk = jnp.split(jax.nn.silu(qk_c), 2, axis=-1)
    g = gates.astype(F32) + b_gate.astype(F32)
    ig, lf = g[..., :ML_HEADS], jax.nn.log_sigmoid(g[..., ML_HEADS:])
    h, c, n, m = mlstm(heads(mq, ML_DIM).astype(F32), (heads(mk, ML_DIM) * ML_DIM ** -0.5).astype(F32),
                       heads(mv, ML_DIM).astype(F32), ig, lf, c0.astype(F32), n0.astype(F32), m0.astype(F32), chunk)
    o_m = head_norm(h, g_ml_norm).reshape(B, T, ML_W).astype(x.dtype) * jax.nn.sigmoid(mo)
    y = jnp.concatenate([o_r, o_m], -1) @ w_out
    return y, s_new, buf_new, c, n, m


def peer(x, w_q, sub_keys, u_tab, v_tab):
    lead = x.shape[:-1]
    xt = x.reshape(-1, D_MODEL)
    nt = xt.shape[0]
    q = (xt @ w_q).reshape(nt, PEER_HEADS, 2, PEER_QDIM // 2).astype(F32)
    s = jnp.einsum('nhpd,hpkd->nhpk', q, sub_keys.astype(F32))
    sv, si = lax.top_k(s, PEER_TOPK)
    cand = (sv[:, :, 0, :, None] + sv[:, :, 1, None, :]).reshape(nt, PEER_HEADS, -1)
    cidx = (si[:, :, 0, :, None] * N_KEYS + si[:, :, 1, None, :]).reshape(nt, PEER_HEADS, -1)
    tv, ti = lax.top_k(cand, PEER_TOPK)
    eidx = jnp.take_along_axis(cidx, ti, -1).reshape(nt, -1)
    gw = jax.nn.softmax(tv, -1).reshape(nt, -1)
    nblk = -(-nt // PEER_BLOCK)
    pad = nblk * PEER_BLOCK - nt
    xb = jnp.pad(xt, ((0, pad), (0, 0))).reshape(nblk, PEER_BLOCK, D_MODEL)
    eb = jnp.pad(eidx, ((0, pad), (0, 0))).reshape(nblk, PEER_BLOCK, -1)
    gb = jnp.pad(gw, ((0, pad), (0, 0))).reshape(nblk, PEER_BLOCK, -1)

    def block(args):
        xs, es, gs = args
        a = jax.nn.gelu(jnp.einsum('nd,nkd->nk', xs, u_tab[es]))
        return jnp.einsum('nk,nkd->nd', a * gs.astype(a.dtype), v_tab[es])

    y = lax.map(block, (xb, eb, gb)).reshape(-1, D_MODEL)[:nt]
    return y.reshape(*lead, D_MODEL)


def trunk(x, p, pos, chunk, s_ret, s_conv, s_c, s_n, s_m, ln_emb_g, ln_emb_b, w_in, b_gate, conv_w, conv_b,
          g_ret_norm, g_ml_norm, w_out, ln1_g, ln1_b, w_peer_q, peer_sub_keys, peer_u, peer_v,
          w_ple_gate, w_ple_proj, ln2_g, ln2_b):
    x = layer_norm(x, ln_emb_g, ln_emb_b)
    rets, convs, cs, ns, ms = [], [], [], [], []
    for i in range(DEPTH):
        mix, sr, cb, c, n, m = token_mixers(x, pos, chunk, s_ret[i], s_conv[i], s_c[i], s_n[i], s_m[i], w_in[i], b_gate[i],
                                            conv_w[i], conv_b[i], g_ret_norm[i], g_ml_norm[i], w_out[i])
        x = layer_norm(DEEPNORM_ALPHA * x + mix, ln1_g[i], ln1_b[i])
        ch = peer(x, w_peer_q[i], peer_sub_keys[i], peer_u[i], peer_v[i])
        ple = jax.nn.sigmoid(x @ w_ple_gate[i]) * (p[i].astype(x.dtype) @ w_ple_proj[i])
        x = layer_norm(DEEPNORM_ALPHA * x + ch + ple, ln2_g[i], ln2_b[i])
        rets.append(sr); convs.append(cb); cs.append(c); ns.append(n); ms.append(m)
    return x, jnp.stack(rets), jnp.stack(convs), jnp.stack(cs), jnp.stack(ns), jnp.stack(ms)


def setup_inputs(seed: int = 0) -> dict:
    key = jax.random.key(seed)
    ks = jax.random.split(key, 32)
    nrm = lambda k, shape, scale: jax.random.normal(k, shape, F32) * scale
    H = ML_HEADS
    b_gate = jnp.concatenate([nrm(ks[10], (DEPTH, H), 0.1),
                              jnp.broadcast_to(jnp.linspace(3.0, 6.0, H, dtype=F32), (DEPTH, H)) + nrm(ks[11], (DEPTH, H), 0.1)], -1)
    return {
        "x_prompt": nrm(ks[0], (BATCH, SEQ, D_MODEL), 1.0),
        "x_sample": nrm(ks[1], (DEC_BATCH, DEC_SEQ, D_MODEL), 1.0),
        "state_ret": nrm(ks[2], (DEPTH, DEC_BATCH, RET_HEADS, RET_DIM, RET_DIM), 0.1),
        "state_conv": nrm(ks[3], (DEPTH, DEC_BATCH, CONV_W - 1, 2 * ML_W), 1.0),
        "state_mlstm_c": nrm(ks[4], (DEPTH, DEC_BATCH, ML_HEADS, ML_DIM, ML_DIM), 0.1),
        "state_mlstm_n": nrm(ks[5], (DEPTH, DEC_BATCH, ML_HEADS, ML_DIM), 0.5),
        "state_mlstm_m": nrm(ks[6], (DEPTH, DEC_BATCH, ML_HEADS), 1.0),
        "p_prompt": nrm(ks[7], (DEPTH, BATCH, SEQ, PLE_DIM), 1.0),
        "p_sample": nrm(ks[8], (DEPTH, DEC_BATCH, DEC_SEQ, PLE_DIM), 1.0),
        "ln_emb_g": 1.0 + nrm(ks[9], (D_MODEL,), 0.02),
        "ln_emb_b": nrm(ks[12], (D_MODEL,), 0.02),
        "w_in": nrm(ks[13], (DEPTH, D_MODEL, IN_COLS), D_MODEL ** -0.5),
        "b_gate": b_gate,
        "conv_w": nrm(ks[14], (DEPTH, CONV_W, 2 * ML_W), CONV_W ** -0.5),
        "conv_b": nrm(ks[15], (DEPTH, 2 * ML_W), 0.02),
        "g_ret_norm": 1.0 + nrm(ks[16], (DEPTH, RET_HEADS, RET_DIM), 0.02),
        "g_ml_norm": 1.0 + nrm(ks[17], (DEPTH, ML_HEADS, ML_DIM), 0.02),
        "w_out": nrm(ks[18], (DEPTH, MIX_W, D_MODEL), MIX_W ** -0.5 * DEEPNORM_BETA),
        "ln1_g": 1.0 + nrm(ks[19], (DEPTH, D_MODEL), 0.02),
        "ln1_b": nrm(ks[20], (DEPTH, D_MODEL), 0.02),
        "w_peer_q": nrm(ks[21], (DEPTH, D_MODEL, PEER_HEADS * PEER_QDIM), D_MODEL ** -0.5),
        "peer_sub_keys": nrm(ks[22], (DEPTH, PEER_HEADS, 2, N_KEYS, PEER_QDIM // 2), (PEER_QDIM // 2) ** -0.5),
        "peer_u": nrm(ks[23], (DEPTH, N_EXPERTS, D_MODEL), D_MODEL ** -0.5),
        "peer_v": nrm(ks[24], (DEPTH, N_EXPERTS, D_MODEL), DEEPNORM_BETA * PEER_HEADS ** -0.5),
        "w_ple_gate": nrm(ks[25], (DEPTH, D_MODEL, D_MODEL), D_MODEL ** -0.5),
        "w_ple_proj": nrm(ks[26], (DEPTH, PLE_DIM, D_MODEL), PLE_DIM ** -0.5 * DEEPNORM_BETA),
        "ln2_g": 1.0 + nrm(ks[27], (DEPTH, D_MODEL), 0.02),
        "ln2_b": nrm(ks[28], (DEPTH, D_MODEL), 0.02),
    }


def reference(x_prompt, x_sample, state_ret, state_conv, state_mlstm_c, state_mlstm_n, state_mlstm_m, p_prompt, p_sample,
              ln_emb_g, ln_emb_b, w_in, b_gate, conv_w, conv_b, g_ret_norm, g_ml_norm, w_out, ln1_g, ln1_b,
              w_peer_q, peer_sub_keys, peer_u, peer_v, w_ple_gate, w_ple_proj, ln2_g, ln2_b):
    weights = (ln_emb_g, ln_emb_b, w_in, b_gate, conv_w, conv_b, g_ret_norm, g_ml_norm, w_out, ln1_g, ln1_b,
               w_peer_q, peer_sub_keys, peer_u, peer_v, w_ple_gate, w_ple_proj, ln2_g, ln2_b)
    Bp, Tp = x_prompt.shape[:2]
    y_prompt, ret_p, conv_p, c_p, n_p, m_p = trunk(
        x_prompt, p_prompt, jnp.arange(Tp), CHUNK,
        jnp.zeros((DEPTH, Bp, RET_HEADS, RET_DIM, RET_DIM), F32),
        jnp.zeros((DEPTH, Bp, CONV_W - 1, 2 * ML_W), x_prompt.dtype),
        jnp.zeros((DEPTH, Bp, ML_HEADS, ML_DIM, ML_DIM), F32),
        jnp.zeros((DEPTH, Bp, ML_HEADS, ML_DIM), F32),
        jnp.zeros((DEPTH, Bp, ML_HEADS), F32),
        *weights)
    Ts = x_sample.shape[1]
    y_sample, ret_s, conv_s, c_s, n_s, m_s = trunk(
        x_sample, p_sample, PAST_LEN + jnp.arange(Ts), Ts,
        state_ret, state_conv, state_mlstm_c, state_mlstm_n, state_mlstm_m,
        *weights)
    return (y_prompt, y_sample, ret_p, conv_p, c_p, n_p, m_p, ret_s, conv_s, c_s, n_s, m_s)
```

```python
import math
import os
from contextlib import ExitStack
import numpy as np
import concourse.bass as bass
import concourse.mybir as mybir
from concourse.bass_utils import run_bass_kernel_spmd

F32 = mybir.dt.float32
BF16 = mybir.dt.bfloat16
ALU = mybir.AluOpType
AF = mybir.ActivationFunctionType
AX = mybir.AxisListType

SAME_ENGINE_SYNC = True
NCORES = 8
LN_EPS = 1e-5
LN16 = math.log(16.0)


class Buf:
    __slots__ = ("name", "w", "r")

    def __init__(self, name):
        self.name = name
        self.w = None
        self.r = {}


class Prog:
    ENGS = ("pe", "act", "dve", "pool", "sp")

    def __init__(self, nc, ctx):
        self.nc = nc
        self.ctx = ctx
        self.ops = {e: [] for e in self.ENGS}
        self.cnt = {}
        self.waited = {e: {} for e in self.ENGS}
        self.semh = {}
        self.nbuf = 0
        self.ntile = 0

    def buf(self, name=None):
        self.nbuf += 1
        return Buf(f"{name or 'b'}{self.nbuf}")

    def sb(self, name, shape, dtype, ctx=None):
        self.ntile += 1
        return (ctx or self.ctx).enter_context(self.nc.sbuf_tensor(f"{name}_{self.ntile}", list(shape), dtype))

    def ps(self, name, shape, dtype, ctx=None):
        self.ntile += 1
        return (ctx or self.ctx).enter_context(self.nc.psum_tensor(f"{name}_{self.ntile}", list(shape), dtype))

    def _sem(self, key):
        if key not in self.semh:
            nm = "s" + str(len(self.semh)) + "_" + "".join(ch for ch in str(key) if ch.isalnum())[:24]
            self.semh[key] = self.ctx.enter_context(self.nc.semaphore(nm))
            self.cnt[key] = 0
        return self.semh[key]

    def _collect(self, eng, reads, writes):
        waits = {}

        def need(k, v):
            if k[0] == "d":
                v = max(v, self.cnt[k])
            if k == eng and (eng == "pe" or not SAME_ENGINE_SYNC):
                return
            if self.waited[eng].get(k, 0) < v and waits.get(k, 0) < v:
                waits[k] = v

        for b in reads:
            if b.w is not None:
                need(*b.w)
        for b in writes:
            if b.w is not None:
                need(*b.w)
            for k, v in b.r.items():
                need(k, v)
        for k, v in waits.items():
            self.waited[eng][k] = v
        return list(waits.items())

    def op(self, eng, name, args, kw, reads=(), writes=()):
        self._sem(eng)
        waits = self._collect(eng, reads, writes)
        self.cnt[eng] += 1
        val = self.cnt[eng]
        self.ops[eng].append((waits, name, args, kw, eng, 1))
        for b in reads:
            b.r[eng] = val
        for b in writes:
            b.w = (eng, val)
            b.r = {}
        return val

    NDMASEM = 72

    def dma(self, q, out_ap, in_ap, tag, reads=(), writes=(), **kw):
        key0 = ("dw" if any(b is tag for b in writes) else "dr", tag.name)
        if not hasattr(self, "keymap"):
            self.keymap = {}
        if key0 not in self.keymap:
            assert len(self.keymap) < self.NDMASEM, "too many DMA buffers in one phase"
            self.keymap[key0] = ("d", len(self.keymap))
        key = self.keymap[key0]
        self._sem(key)
        waits = self._collect(q, reads, writes)
        self.cnt[key] += 16
        val = self.cnt[key]
        kw = dict(kw); kw["out"] = out_ap; kw["in_"] = in_ap
        self.ops[q].append((waits, "dma_start", (), kw, key, 16))
        for b in reads:
            b.r[key] = val
        for b in writes:
            b.w = (key, val)
            b.r = {}
        return val

    def barrier(self):
        for e in self.ENGS:
            waits = []
            for k, v in self.cnt.items():
                if k == e or v == 0:
                    continue
                if self.waited[e].get(k, 0) < v:
                    waits.append((k, v))
                    self.waited[e][k] = v
            if waits:
                self.ops[e].append((waits, None, None, None, None, 0))
        self.maxkeys = max(getattr(self, "maxkeys", 0), len(getattr(self, "keymap", {})))
        self.keymap = {}

    def emit(self):
        nc = self.nc
        self.barrier()
        with nc.Block() as block:
            def run(e, name):
                for waits, opn, args, kw, key, inc in self.ops[name]:
                    for k, v in waits:
                        e.wait_ge(self.semh[k], v)
                    if opn is not None:
                        getattr(e, opn)(*args, **kw).then_inc(self.semh[key], inc)

            @block.sync
            def _(e):
                run(e, "sp")

            @block.tensor
            def _(e):
                run(e, "pe")

            @block.scalar
            def _(e):
                run(e, "act")

            @block.vector
            def _(e):
                run(e, "dve")

            @block.gpsimd
            def _(e):
                run(e, "pool")


def pieces(n, m=512):
    k = (n + m - 1) // m
    base = (n + k - 1) // k
    out = []
    c = 0
    while c < n:
        out.append((c, min(n, c + base)))
        c += base
    return out


class Cfg:
    pass


def build(cfg):
    D, KT, H, NT, SD = cfg.D, cfg.KT, cfg.H, cfg.NT, cfg.SD
    NTO = NT + SD
    NCH = NT // 128
    RW = H * 256
    NCT = 8 * RW // 128
    NMT = 2 * RW // 128
    gam = [1.0 - 2.0 ** (-5.0 - h) for h in range(H)]
    NK, PH, PLE, ALPHA = cfg.NK, cfg.PH, cfg.PLE, cfg.ALPHA
    NE = NK * NK
    KTM = 2 * RW // 128
    CBS = min(512, D); NCB = D // CBS
    KG = min(int(os.environ.get("KKG", "4")), KT); NKG = KT // KG
    QW = PH * 256; CQ = min(512, QW); NCQ = QW // CQ
    EB = min(512, NE); NEB = NE // EB; ET = EB // 128
    NI = EB // NK
    PT_ = PLE // 128

    nc = bass.Bass("TRN2", target_bir_lowering=False)

    def din(name, shape, dt=F32):
        return nc.dram_tensor(name, list(shape), dt, kind="ExternalInput").ap()

    def dout(name, shape, dt=F32):
        return nc.dram_tensor(name, list(shape), dt, kind="ExternalOutput").ap()

    def dscr(name, shape, dt=F32):
        return nc.dram_tensor(name, list(shape), dt, kind="Internal").ap()

    xo = din("xo", [NT, D]); xp = din("xp", [NT, D]); xs = din("xs", [SD, D])
    flag_d = din("flag", [128, 1])
    lng_d = din("lng", [128, KT]); lnb_d = din("lnb", [128, KT])
    wm = din("wm", [NCT, 128, KT * 128])
    wg_d = din("wg", [128, KT * 2 * H])
    bg_d = din("bg", [H, 2])
    convw_d = din("convw", [128, NMT * 4]); convb_d = din("convb", [128, NMT])
    gnr_d = din("gnr", [128, 2 * H]); gnm_d = din("gnm", [128, 2 * H])
    tq_d = din("tq", [H, 2, 128, NTO]); tk_d = din("tk", [H, 2, 128, NTO]); tkp_d = din("tkp", [H, 2, 128, NT])
    ident_d = din("ident", [128, 128]); mask01_d = din("mask01", [128, 128]); maskneg_d = din("maskneg", [128, 128])
    sel_d = din("sel", [128, H * 128]); sdmask_d = din("sdmask", [128, SD * SD])
    sret = din("sret", [SD, H, 256, 256]); sconv = din("sconv", [SD * 3, 2 * RW])
    smc = din("smc", [SD, H, 256, 256]); smn = din("smn", [SD, H, 256]); smm = din("smm", [SD, H])

    o_retp = dout("o_retp", [H, 256, 256]); o_convp = dout("o_convp", [3, 2 * RW])
    o_cp = dout("o_cp", [H, 256, 256]); o_np = dout("o_np", [H, 256]); o_mp = dout("o_mp", [H, 1])
    o_rets = dout("o_rets", [SD, H, 256, 256]); o_convs = dout("o_convs", [SD * 3, 2 * RW])
    o_cs = dout("o_cs", [SD, H, 256, 256]); o_ns = dout("o_ns", [SD, H, 256]); o_ms = dout("o_ms", [SD, H])
    o_mix = dout("o_mix", [2 * RW, NTO], BF16)

    wo_d = din("wo", [NCB, 128, KTM * CBS])
    geb_d = din("geb", [128, D]); beb_d = din("beb", [128, D]); g1b_d = din("g1b", [128, D]); b1b_d = din("b1b", [128, D])
    g2b_d = din("g2b", [128, D]); b2b_d = din("b2b", [128, D])
    wq_d = din("wq", [NCQ, NKG, 128, KG * CQ]); skT_d = din("skT", [128, PH * 2 * NK])
    uT_d = din("uT", [NEB, NKG, 128, KG * EB]); v_d = din("vv", [NEB, NCB, 128, ET * CBS])
    wpg_d = din("wpg", [NCB, NKG, 128, KG * CBS]); wpp_d = din("wpp", [NCB, 128, PT_ * CBS])
    po = din("po", [NT, PLE]); ps_d = din("ps", [SD, PLE])
    y_o = dout("y_o", [NTO, D])
    y_d = dscr("y_d", [NTO, D]); x1_d = dscr("x1_d", [NTO, D])
    bstate_r = dscr("bstate_r", [H, 128, 512])
    bstate_c = dscr("bstate_c", [H, 128, 2 * 257])

    with ExitStack() as ctx:
        P = Prog(nc, ctx)

        def V(name, *a, r=(), w=(), **kw):
            P.op("dve", name, a, kw, r, w)

        def A(name, *a, r=(), w=(), **kw):
            P.op("act", name, a, kw, r, w)

        def G(name, *a, r=(), w=(), **kw):
            P.op("pool", name, a, kw, r, w)

        def T(name, *a, r=(), w=(), **kw):
            P.op("pe", name, a, kw, r, w)

        def load(dst, src, b, q="sp", **kw):
            P.dma(q, dst, src, b, writes=[b], **kw)

        def const(name, shape, src, dt=F32):
            t = P.sb(name, shape, dt); b = P.buf(name)
            load(t[:], src, b)
            return t, b

        ident, b_ident = const("ident", [128, 128], ident_d)
        mask01, b_m01 = const("mask01", [128, 128], mask01_d)
        maskneg, b_mneg = const("maskneg", [128, 128], maskneg_d)
        flag, b_flag = const("flag", [128, 1], flag_d)
        lng, b_lng = const("lng", [128, KT], lng_d)
        lnb, b_lnb = const("lnb", [128, KT], lnb_d)
        sel, b_sel = const("sel", [128, H * 128], sel_d)
        sdmask, b_sdm = const("sdmask", [128, SD * SD], sdmask_d)
        bg, b_bg = const("bg", [H, 2], bg_d)
        convw, b_cw = const("convw", [128, NMT * 4], convw_d)
        convb, b_cb = const("convb", [128, NMT], convb_d)
        gnr, b_gnr = const("gnr", [128, 2 * H], gnr_d)
        gnm, b_gnm = const("gnm", [128, 2 * H], gnm_d)
        identb = P.sb("identb", [128, 128], BF16); b_identb = P.buf("identb")
        V("tensor_copy", identb[:], ident[:], r=[b_ident], w=[b_identb])
        mhalf = P.sb("mhalf", [128, 1], F32); b_mhalf = P.buf("mhalf")
        V("memset", mhalf[:], -0.5, w=[b_mhalf])
        onesrow = P.sb("onesrow", [H, 128], F32); b_onesrow = P.buf("onesrow")
        V("memset", onesrow[:], 1.0, w=[b_onesrow])
        halo = P.sb("halo", [128, NMT * 3], F32); b_halo = P.buf("halo")
        mbound = P.sb("mbound", [H, 1], F32); b_mbound = P.buf("mbound")
        bufT = P.sb("bufT", [128, NMT, SD * 3], F32); b_bufT = P.buf("bufT")
        XT = P.sb("XT", [128, KT, NTO], BF16)
        b_XT = [P.buf("XT") for _ in range(NCH + 1)]

        def rstd_op(out_ap, var_ap, n, rb, wb):
            G("tensor_scalar", out_ap, var_ap, LN_EPS, None, ALU.add, r=rb, w=wb)
            G("tensor_tensor", out_ap, out_ap, mhalf[:n, :], ALU.pow, r=list(wb) + [b_mhalf], w=wb)

        def phase_ln(tiles):
            with ExitStack() as ph:
                xt = [P.sb("p1x", [128, D], F32, ph) for _ in range(2)]; bx = [P.buf("p1x") for _ in range(2)]
                pcs = pieces(D)
                nst = len(pcs)
                st = [P.sb("p1st", [128, nst * 6 + 8], F32, ph) for _ in range(2)]; bst = [P.buf("p1st") for _ in range(2)]
                pt = [P.ps("p1pt", [128, 4, 128], F32, ph) for _ in range(2)]; bpt = [P.buf("p1pt") for _ in range(2)]
                gi = 0
                for ti, (src, n, col0, xb) in enumerate(tiles):
                    x, b = xt[ti % 2], bx[ti % 2]
                    s, bs = st[ti % 2], bst[ti % 2]
                    load(x[:n, :], src, b)
                    for c, (c0, c1) in enumerate(pcs):
                        V("bn_stats", s[:n, c * 6:(c + 1) * 6], x[:n, c0:c1], r=[b], w=[bs])
                    o = nst * 6
                    V("bn_aggr", s[:n, o:o + 2], s[:n, 0:o], r=[bs], w=[bs])
                    rstd_op(s[:n, o + 2:o + 3], s[:n, o + 1:o + 2], n, [bs], [bs])
                    V("scalar_tensor_tensor", s[:n, o + 3:o + 4], s[:n, o:o + 1], -1.0, s[:n, o + 2:o + 3], ALU.mult, ALU.mult, r=[bs], w=[bs])
                    A("activation", x[:n, :], x[:n, :], AF.Identity, bias=s[:n, o + 3:o + 4], scale=s[:n, o + 2:o + 3], r=[b, bs], w=[b])
                    for k0 in range(0, KT, 4):
                        p_, bp = pt[gi % 2], bpt[gi % 2]; gi += 1
                        nk = min(4, KT - k0)
                        for j in range(nk):
                            kt = k0 + j
                            T("transpose", p_[:, j, :n], x[:n, kt * 128:(kt + 1) * 128], ident[:n, :n], r=[b, b_ident], w=[bp])
                        for j in range(nk):
                            kt = k0 + j
                            if j % 2 == 0:
                                V("tensor_scalar", XT[:, kt, col0:col0 + n], p_[:, j, :n], lng[:, kt:kt + 1], lnb[:, kt:kt + 1], ALU.mult, ALU.add,
                                  r=[bp, b_lng, b_lnb], w=[xb])
                            else:
                                A("activation", XT[:, kt, col0:col0 + n], p_[:, j, :n], AF.Identity, bias=lnb[:, kt:kt + 1], scale=lng[:, kt:kt + 1],
                                  r=[bp, b_lng, b_lnb], w=[xb])
            P.barrier()

        def phase_convbuf():
            if SD == 0:
                return
            with ExitStack() as ph:
                cvs = P.sb("cvs", [SD * 3, 2 * RW], F32, ph); bcvs = P.buf("cvs")
                pc = P.ps("pcv", [128, 512], F32, ph); bpc = P.buf("pcv")
                load(cvs[:, :], sconv, bcvs)
                for mti in range(NMT):
                    T("transpose", pc[:, 0:SD * 3], cvs[:, mti * 128:(mti + 1) * 128], ident[:SD * 3, :SD * 3], r=[bcvs, b_ident], w=[bpc])
                    A("activation", bufT[:, mti, :], pc[:, 0:SD * 3], AF.Identity, r=[bpc], w=[b_bufT])
                o3 = o_convs.rearrange("(s j) f -> s j f", j=3)
                c3 = cvs[:, :]
                for s in range(SD):
                    P.dma("sp", o_convs[s * 3:s * 3 + 2, :], cvs[s * 3 + 1:s * 3 + 3, :], bcvs, reads=[bcvs])
            P.barrier()

        def mixer_phase(mode):
            pre = mode == "pre"
            NC_ = NT if pre else NTO
            SDm = max(SD, 1)
            with ExitStack() as ph:
                NW = 3
                wt = [P.sb("wt", [128, KT, 128], BF16, ph) for _ in range(NW)]; bw = [P.buf("wt") for _ in range(NW)]
                pz = [P.ps("pz", [128, 512], F32, ph) for _ in range(2)]; bpz = [P.buf("pz") for _ in range(2)]
                pzst = {"i": 0}
                psc = P.ps("psc", [128, 4, 128], F32, ph); bsc = [P.buf("psc")] * 4
                pnd = P.ps("pnd", [128, 512], F32, ph); bnd = P.buf("pnd")
                pU = [P.ps("pU", [128, 512], F32, ph) for _ in range(2)]; bU = [P.buf("pU") for _ in range(2)]
                ptr = P.ps("ptr", [128, 1024], BF16, ph); btr = [P.buf("ptr")] * 4
                pms = P.ps("pms", [128, 512], F32, ph); bms = P.buf("pms")
                F = [P.sb("F", [128, NC_], BF16, ph) for _ in range(8)]; bF = [P.buf("F") for _ in range(8)]
                RA = P.sb("RA", [128, 3 + NC_], F32, ph); bRA = P.buf("RA")
                RB = P.sb("RB", [128, 3 + NC_], F32, ph); bRB = P.buf("RB")
                tmp = [P.sb("tmp", [128, 512], F32, ph) for _ in range(2)]; btmp = [P.buf("tmp") for _ in range(2)]
                tq = P.sb("tq", [128, 2, NC_], F32, ph); btq = P.buf("tq")
                tk = P.sb("tk", [128, 2, NC_], F32, ph); btk = P.buf("tk")
                RAW = [RA, RB]; bRAW = [bRA, bRB]
                CACC = tq[:, 0, :]; bCACC = btq
                PTt = P.sb("PT", [128, 128], BF16, ph); bPT = P.buf("PT")
                Et = P.sb("Et", [128, 128], F32, ph); bE = P.buf("Et")
                QP = P.sb("QP", [128, 2, 128], BF16, ph); bQP = P.buf("QP")
                vc = P.sb("vc", [128, 260], BF16, ph); bvc = P.buf("vc")
                kc = P.sb("kc", [128, 256], BF16, ph); bkc = P.buf("kc")
                on = P.sb("on", [128, 256], BF16, ph); bon = P.buf("on")
                hst = P.sb("hst", [128, 16], F32, ph); bhst = P.buf("hst")
                S32 = P.sb("S32", [128, 2, 257], F32, ph); bS32 = P.buf("S32")
                Sbf = P.sb("Sbf", [128, 2, 257], BF16, ph); bSbf = P.buf("Sbf")
                mixT = P.sb("mixT", [128, 2, NC_], BF16, ph); bmix = P.buf("mixT")
                SS = [P.sb("SS", [128, 2, 257], F32, ph) for _ in range(3)]; bSS = [P.buf("SS") for _ in range(3)]
                ksd = P.sb("ksd", [SDm, 256], BF16, ph); bksd = P.buf("ksd")
                kms = P.sb("kms", [SDm, 256], BF16, ph); bkms = P.buf("kms")
                vsd = P.sb("vsd", [SDm, 260], BF16, ph); bvsd = P.buf("vsd")
                qsd = P.sb("qsd", [128, 2, SDm], F32, ph); bqsd = P.buf("qsd")
                qms = P.sb("qms", [128, 2, SDm * SDm], F32, ph); bqms = P.buf("qms")
                nS = P.sb("nS", [128, 2, SDm], F32, ph); bnS = P.buf("nS")
                crw = [P.sb("crw", [SDm, 128], F32, ph) for _ in range(2)]; bcrw = [P.buf("crw") for _ in range(2)]
                crst = {"i": 0}
                NG = NC_
                grow = {nm: P.sb("g_" + nm, [128, NG], F32, ph) for nm in ("ig", "lf", "negM", "wp")}
                bgrow = {nm: P.buf("g_" + nm) for nm in grow}
                gt1 = P.sb("gt1", [H, 512], F32, ph); bgt1 = P.buf("gt1")
                gt2 = P.sb("gt2", [H, 512], F32, ph); bgt2 = P.buf("gt2")
                colA = P.sb("colA", [128, NCH, 3 * H], F32, ph); bcolA = P.buf("colA")
                wcrow = P.sb("wcrow", [128, NCH + 1], F32, ph); bwcrow = P.buf("wcrow")
                mrow = P.sb("mrow", [H, NCH + 2], F32, ph); bmrow = P.buf("mrow")
                wcbc = P.sb("wcbc", [128, H, NCH + 1], F32, ph); bwcbc = P.buf("wcbc")
                wgt = P.sb("wgt", [128, KT * 2 * H], BF16, ph); bwgt = P.buf("wgt")
                srow = {nm: P.sb("s_" + nm, [128, SDm], F32, ph) for nm in ("m", "mt", "w16", "wp", "emt", "t")}
                bsrow = P.buf("srow")
                scol = P.sb("scol", [SDm, 2 * H], F32, ph); bscol = P.buf("scol")
                wpbcS = P.sb("wpbcS", [128, H, SDm], F32, ph); bwpbcS = P.buf("wpbcS")
                convp = P.sb("convp", [128, 3, NMT], F32, ph); bconvp = P.buf("convp")

                xt_bufs = b_XT[:NCH] if pre else b_XT

                def head_cts(kind, h):
                    if kind == "ret":
                        bq, bk, bv, bgg = 2 * h, RW // 128 + 2 * h, 2 * RW // 128 + 2 * h, 3 * RW // 128 + 2 * h
                    else:
                        b0 = 4 * RW // 128
                        bq, bk, bv, bgg = b0 + 2 * h, b0 + RW // 128 + 2 * h, b0 + 2 * RW // 128 + 2 * h, b0 + 3 * RW // 128 + 2 * h
                    return bq, bk, bv, bgg

                order = []
                for h in range(H):
                    for kind in ("ret", "ml"):
                        bq, bk, bv, bgg = head_cts(kind, h)
                        if pre:
                            if kind == "ml":
                                order += [bq, bq + 1]
                            order += [bk, bk + 1, bv, bv + 1]
                        else:
                            order += [bq, bq + 1, bk, bk + 1, bv, bv + 1, bgg, bgg + 1]
                wst = {"issued": 0, "pos": 0}

                def wget(ct):
                    pos = wst["pos"]
                    assert order[pos] == ct, (order[pos], ct, pos)
                    while wst["issued"] <= min(pos + 2, len(order) - 1):
                        i = wst["issued"]
                        load(wt[i % NW][:].rearrange("p k c -> p (k c)"), wm[order[i]], bw[i % NW], q="pool")
                        wst["issued"] += 1
                    wst["pos"] += 1
                    return wt[pos % NW], bw[pos % NW]

                def ztile(ct, evac):
                    wtile, bwt = wget(ct)
                    for (c0, c1) in pieces(NC_):
                        i = pzst["i"] % 2; pzst["i"] += 1
                        p_, bp = pz[i], bpz[i]
                        n = c1 - c0
                        for kt in range(KT):
                            T("matmul", p_[:, :n], wtile[:, kt, :], XT[:, kt, c0:c1], start=(kt == 0), stop=(kt == KT - 1), r=[bwt] + xt_bufs, w=[bp])
                        evac(p_[:, :n], bp, c0, c1)

                def rope_pair(ct0, tab, btab, dst1, bd1, dst2, bd2):
                    def ev1(ps, bp, c0, c1):
                        V("tensor_tensor", RA[:, c0:c1], ps, tab[:, 0, c0:c1], ALU.mult, r=[bp, btab], w=[bRA])
                        V("tensor_tensor", RB[:, c0:c1], ps, tab[:, 1, c0:c1], ALU.mult, r=[bp, btab], w=[bRB])

                    def ev2(ps, bp, c0, c1):
                        n = c1 - c0
                        V("tensor_tensor", tmp[0][:, :n], ps, tab[:, 1, c0:c1], ALU.mult, r=[bp, btab], w=[btmp[0]])
                        V("tensor_tensor", tmp[1][:, :n], ps, tab[:, 0, c0:c1], ALU.mult, r=[bp, btab], w=[btmp[1]])
                        G("tensor_tensor", dst1[:, c0:c1], RA[:, c0:c1], tmp[0][:, :n], ALU.subtract, r=[bRA, btmp[0]], w=[bd1])
                        G("tensor_tensor", dst2[:, c0:c1], RB[:, c0:c1], tmp[1][:, :n], ALU.add, r=[bRB, btmp[1]], w=[bd2])
                    ztile(ct0, ev1)
                    ztile(ct0 + 1, ev2)

                def copy_tile(ct, dst, bd, func=AF.Identity):
                    def ev(ps, bp, c0, c1):
                        A("activation", dst[:, c0:c1], ps, func, r=[bp], w=[bd])
                    ztile(ct, ev)

                def tr_to_tokens(srcs, bsrcs, c0, n, slot, dst, bdst, scale=1.0, bscale=()):
                    for hh in range(2):
                        T("transpose", ptr[:n, slot * 256 + hh * 128: slot * 256 + (hh + 1) * 128], srcs[hh][:, c0:c0 + n], identb[:, :],
                          r=[bsrcs[hh], b_identb], w=[btr[slot]])
                    A("activation", dst[:n, 0:256], ptr[:n, slot * 256:(slot + 1) * 256], AF.Identity, scale=scale, r=[btr[slot]] + list(bscale), w=[bdst])

                def head_out(n, nd_ap, gcol, bg_, sg, bsg, c0, gi, r_ap=None):
                    V("bn_stats", hst[:n, 0:6], nd_ap, r=[bnd], w=[bhst])
                    V("bn_aggr", hst[:n, 6:8], hst[:n, 0:6], r=[bhst], w=[bhst])
                    if r_ap is not None:
                        V("tensor_tensor", hst[:n, 6:7], hst[:n, 6:7], r_ap, ALU.mult, r=[bhst], w=[bhst])
                        V("tensor_tensor", hst[:n, 7:8], hst[:n, 7:8], r_ap, ALU.mult, r=[bhst], w=[bhst])
                        V("tensor_tensor", hst[:n, 7:8], hst[:n, 7:8], r_ap, ALU.mult, r=[bhst], w=[bhst])
                    rstd_op(hst[:n, 8:9], hst[:n, 7:8], n, [bhst], [bhst])
                    V("scalar_tensor_tensor", hst[:n, 9:10], hst[:n, 6:7], -1.0, hst[:n, 8:9], ALU.mult, ALU.mult, r=[bhst], w=[bhst])
                    if r_ap is not None:
                        V("tensor_tensor", hst[:n, 8:9], hst[:n, 8:9], r_ap, ALU.mult, r=[bhst], w=[bhst])
                    A("activation", on[:n, :], nd_ap, AF.Identity, bias=hst[:n, 9:10], scale=hst[:n, 8:9], r=[bnd, bhst], w=[bon])
                    for hh in range(2):
                        T("transpose", ptr[:, 512 + hh * 128:512 + hh * 128 + n], on[:n, hh * 128:(hh + 1) * 128], identb[:n, :n], r=[bon, b_identb], w=[btr[2]])
                    for hh in range(2):
                        V("scalar_tensor_tensor", mixT[:, hh, c0:c0 + n], ptr[:, 512 + hh * 128:512 + hh * 128 + n], gcol[:, 2 * gi + hh:2 * gi + hh + 1],
                          sg[hh][:, c0:c0 + n], ALU.mult, ALU.mult, r=[btr[2], bg_, bsg[hh]], w=[bmix])

                def gate_stage():
                    ig, lf, negM, wpr = (grow[k] for k in ("ig", "lf", "negM", "wp"))
                    big, blf, bnegM, bwp = (bgrow[k] for k in ("ig", "lf", "negM", "wp"))
                    for k in grow:
                        G("memset", grow[k][:, :], 0.0, w=[bgrow[k]])
                    G("memset", wcrow[:, :], 0.0, w=[bwcrow])
                    load(wgt[:], wg_d, bwgt, q="pool")
                    wg3 = wgt[:].rearrange("p (k g) -> p k g", g=2 * H)
                    for (c0, c1) in pieces(NG):
                        n = c1 - c0
                        for gi, (dst, bd) in enumerate(((ig, big), (lf, blf))):
                            for kt in range(KT):
                                T("matmul", pms[:H, :n], wg3[:, kt, gi * H:(gi + 1) * H], XT[:, kt, c0:c1], start=(kt == 0), stop=(kt == KT - 1),
                                  r=[bwgt] + xt_bufs, w=[bms])
                            A("activation", dst[:H, c0:c1], pms[:H, :n], AF.Identity, bias=bg[:, gi:gi + 1], r=[bms, b_bg], w=[bd])
                        V("scalar_tensor_tensor", gt1[:, :n], lf[:H, c0:c1], -1.0, lf[:H, c0:c1], ALU.mult, ALU.min, r=[blf], w=[bgt1])
                        A("activation", gt1[:, :n], gt1[:, :n], AF.Exp, r=[bgt1], w=[bgt1])
                        A("activation", gt1[:, :n], gt1[:, :n], AF.Ln, bias=1.0, r=[bgt1], w=[bgt1])
                        V("tensor_scalar_min", lf[:H, c0:c1], lf[:H, c0:c1], 0.0, r=[blf], w=[blf])
                        V("tensor_tensor", lf[:H, c0:c1], lf[:H, c0:c1], gt1[:, :n], ALU.subtract, r=[blf, bgt1], w=[blf])
                    if pre:
                        V("memset", mrow[:, 0:1], 0.0, w=[bmrow])
                    else:
                        V("tensor_copy", mrow[:, 0:1], mbound[:, :], r=[b_mbound], w=[bmrow])
                    t1 = gt1[:, 0:128]; t2 = gt2[:, 0:128]; t3 = gt2[:, 128:256]
                    for c in range(NCH):
                        sl = slice(c * 128, (c + 1) * 128)
                        last = slice(c * 128 + 127, c * 128 + 128)
                        V("tensor_tensor_scan", t1, onesrow[:, :], lf[:H, sl], 0.0, ALU.mult, ALU.add, r=[blf, b_onesrow], w=[bgt1])
                        V("tensor_tensor", t2, ig[:H, sl], t1, ALU.subtract, r=[big, bgt1], w=[bgt2])
                        V("tensor_tensor_scan", negM[:H, sl], onesrow[:, :], t2, mrow[:, c:c + 1], ALU.mult, ALU.max, r=[bgt2, b_onesrow, bmrow], w=[bnegM])
                        V("tensor_tensor", mrow[:, c + 1:c + 2], gt1[:, 127:128], negM[:H, last], ALU.add, r=[bgt1, bnegM], w=[bmrow])
                        V("tensor_tensor", t1, t1, negM[:H, sl], ALU.add, r=[bgt1, bnegM], w=[bgt1])
                        A("activation", t1, t1, AF.Exp, scale=-1.0, r=[bgt1], w=[bgt1])
                        V("tensor_scalar", wcrow[:H, NCH:NCH + 1], negM[:H, last], -1.0, -LN16, ALU.mult, ALU.add, r=[bnegM], w=[bwcrow])
                        V("tensor_scalar_mul", negM[:H, sl], negM[:H, sl], -1.0, r=[bnegM], w=[bnegM])
                        A("activation", wpr[:H, sl], negM[:H, sl], AF.Exp, bias=mrow[:, c:c + 1], r=[bnegM, bmrow], w=[bwp])
                        V("tensor_copy", wcrow[:H, c:c + 1], wpr[:H, last], r=[bwp], w=[bwcrow])
                        A("activation", t3, t2, AF.Exp, bias=wcrow[:H, NCH:NCH + 1], r=[bgt2, bwcrow], w=[bgt2])
                        T("transpose", pms[:, 0:H], t2, ident[:H, :H], r=[bgt2, b_ident], w=[bms])
                        T("transpose", pms[:, H:2 * H], t3, ident[:H, :H], r=[bgt2, b_ident], w=[bms])
                        T("transpose", pms[:, 2 * H:3 * H], t1, ident[:H, :H], r=[bgt1, b_ident], w=[bms])
                        A("activation", colA[:, c, :], pms[:, 0:3 * H], AF.Identity, r=[bms], w=[bcolA])
                    for h in range(H):
                        T("matmul", pms[:, 0:NCH], sel[:, h * 128:(h + 1) * 128], wcrow[:, 0:NCH], start=True, stop=True, r=[b_sel, bwcrow], w=[bms])
                        A("activation", wcbc[:, h, 0:NCH], pms[:, 0:NCH], AF.Identity, r=[bms], w=[bwcbc])
                    if pre:
                        V("tensor_scalar_mul", mbound[:, :], mrow[:, NCH:NCH + 1], flag[:H, :], r=[bmrow, b_flag], w=[b_mbound])
                    elif SD > 0:
                        sm, smt, sw16, swp, semt, stt = (srow[k] for k in ("m", "mt", "w16", "wp", "emt", "t"))
                        for k in srow:
                            G("memset", srow[k][:, :], 0.0, w=[bsrow])
                        load(sm[:H, :], smm.rearrange("s h -> h s"), bsrow, allow_slow_non_contiguous=True)
                        cs = slice(NT, NTO)
                        V("tensor_tensor", stt[:H, :], lf[:H, cs], sm[:H, :], ALU.add, r=[blf, bsrow], w=[bsrow])
                        V("tensor_tensor", smt[:H, :], stt[:H, :], ig[:H, cs], ALU.max, r=[big, bsrow], w=[bsrow])
                        V("tensor_tensor", swp[:H, :], stt[:H, :], smt[:H, :], ALU.subtract, r=[bsrow], w=[bsrow])
                        A("activation", swp[:H, :], swp[:H, :], AF.Exp, r=[bsrow], w=[bsrow])
                        V("tensor_tensor", sw16[:H, :], ig[:H, cs], smt[:H, :], ALU.subtract, r=[big, bsrow], w=[bsrow])
                        V("tensor_scalar_add", sw16[:H, :], sw16[:H, :], -LN16, r=[bsrow], w=[bsrow])
                        A("activation", sw16[:H, :], sw16[:H, :], AF.Exp, r=[bsrow], w=[bsrow])
                        A("activation", semt[:H, :], smt[:H, :], AF.Exp, scale=-1.0, r=[bsrow], w=[bsrow])
                        P.dma("sp", o_ms.rearrange("s h -> h s"), smt[:H, :], bsrow, reads=[bsrow], allow_slow_non_contiguous=True)
                        T("transpose", pms[:SD, 0:H], sw16[:H, :], ident[:H, :H], r=[bsrow, b_ident], w=[bms])
                        T("transpose", pms[:SD, H:2 * H], semt[:H, :], ident[:H, :H], r=[bsrow, b_ident], w=[bms])
                        A("activation", scol[:SD, :], pms[:SD, 0:2 * H], AF.Identity, r=[bms], w=[bscol])
                        for h in range(H):
                            T("matmul", pms[:, 0:SD], sel[:, h * 128:(h + 1) * 128], swp[:, :], start=True, stop=True, r=[b_sel, bsrow], w=[bms])
                            A("activation", wpbcS[:, h, :], pms[:, 0:SD], AF.Identity, r=[bms], w=[bwpbcS])

                def run_head(kind, h):
                    ret = kind == "ret"
                    base_q, base_k, base_v, base_g = head_cts(kind, h)
                    q1, q2, k1, k2, v1, v2, g1, g2 = F
                    bq1, bq2, bk1, bk2, bv1, bv2, bg1, bg2 = bF
                    if ret:
                        if pre:
                            load(tk[:, :, :], tkp_d[h].rearrange("a p n -> p a n"), btk)
                        else:
                            load(tq[:, :, :], tq_d[h].rearrange("a p n -> p a n"), btq)
                            load(tk[:, :, :], tk_d[h].rearrange("a p n -> p a n"), btk)
                            rope_pair(base_q, tq, btq, q1, bq1, q2, bq2)
                        rope_pair(base_k, tk, btk, k1, bk1, k2, bk2)
                    else:
                        def conv_tile(ct, which, dst, bd):
                            mti = ct - 4 * RW // 128
                            R, bR = RAW[which % 2], bRAW[which % 2]

                            def ev(ps, bp, c0, c1):
                                A("activation", R[:, 3 + c0:3 + c1], ps, AF.Identity, r=[bp], w=[bR])
                            ztile(ct, ev)
                            if pre:
                                V("memset", R[:, 0:3], 0.0, w=[bR])
                                V("tensor_scalar_mul", halo[:, mti * 3:mti * 3 + 3], R[:, NT:NT + 3], flag[:, :], r=[bR, b_flag], w=[b_halo])
                            else:
                                V("tensor_copy", R[:, 0:3], halo[:, mti * 3:mti * 3 + 3], r=[b_halo], w=[bR])
                                G("tensor_copy", convp[:, :, mti], R[:, NT:NT + 3], r=[bR], w=[bconvp])
                            w4 = convw[:, mti * 4:mti * 4 + 4]
                            V("tensor_scalar", CACC[:, 0:NT], R[:, 3:3 + NT], w4[:, 3:4], convb[:, mti:mti + 1], ALU.mult, ALU.add, r=[bR, b_cw, b_cb], w=[bCACC])
                            for j in (2, 1, 0):
                                V("scalar_tensor_tensor", CACC[:, 0:NT], R[:, j:j + NT], w4[:, j:j + 1], CACC[:, 0:NT], ALU.mult, ALU.add, r=[bR, b_cw, bCACC], w=[bCACC])
                            if not pre and SD > 0:
                                V("tensor_scalar", CACC[:, NT:NTO], R[:, 3 + NT:3 + NTO], w4[:, 3:4], convb[:, mti:mti + 1], ALU.mult, ALU.add, r=[bR, b_cw, b_cb], w=[bCACC])
                                bt3 = bufT[:, mti, :].rearrange("p (s j) -> p s j", j=3)
                                for j in range(3):
                                    V("scalar_tensor_tensor", CACC[:, NT:NTO], bt3[:, :, j], w4[:, j:j + 1], CACC[:, NT:NTO], ALU.mult, ALU.add, r=[b_bufT, b_cw, bCACC], w=[bCACC])
                                i = crst["i"] % 2; crst["i"] += 1
                                T("transpose", pms[:SD, 0:128], R[:, 3 + NT:3 + NTO], ident[:, :], r=[bR, b_ident], w=[bms])
                                A("activation", crw[i][:SD, :], pms[:SD, 0:128], AF.Identity, r=[bms], w=[bcrw[i]])
                                o3 = o_convs.rearrange("(s j) f -> s j f", j=3)
                                P.dma("sp", o3[:, 2, mti * 128:(mti + 1) * 128], crw[i][:SD, :], bcrw[i], reads=[bcrw[i]])
                            A("activation", dst[:, :], CACC[:, 0:NC_], AF.Silu, r=[bCACC], w=[bd])
                        if not pre:
                            conv_tile(base_q, 0, q1, bq1)
                            conv_tile(base_q + 1, 1, q2, bq2)
                        else:
                            for ct in (base_q, base_q + 1):
                                mti = ct - 4 * RW // 128
                                wtile, bwt = wget(ct)
                                i = pzst["i"] % 2; pzst["i"] += 1
                                for kt in range(KT):
                                    T("matmul", pz[i][:, 0:3], wtile[:, kt, :], XT[:, kt, NT - 3:NT], start=(kt == 0), stop=(kt == KT - 1), r=[bwt] + xt_bufs, w=[bpz[i]])
                                V("tensor_scalar_mul", halo[:, mti * 3:mti * 3 + 3], pz[i][:, 0:3], flag[:, :], r=[bpz[i], b_flag], w=[b_halo])
                        conv_tile(base_k, 0, k1, bk1)
                        conv_tile(base_k + 1, 1, k2, bk2)
                    copy_tile(base_v, v1, bv1)
                    copy_tile(base_v + 1, v2, bv2)
                    if not pre:
                        gf = AF.Silu if ret else AF.Sigmoid
                        copy_tile(base_g, g1, bg1, gf)
                        copy_tile(base_g + 1, g2, bg2, gf)

                    W = 256 if ret else 257
                    if pre:
                        V("memset", S32[:, :, :], 0.0, w=[bS32])
                        V("memset", Sbf[:, :, :], 0.0, w=[bSbf])
                    else:
                        src = (bstate_r[h].rearrange("p (t e) -> p t e", t=2) if ret else bstate_c[h].rearrange("p (t e) -> p t e", t=2))
                        load(S32[:, :, 0:W], src, bS32)
                        A("activation", Sbf[:, :, 0:W], S32[:, :, 0:W], AF.Identity, r=[bS32], w=[bSbf])
                    if not ret:
                        V("memset", vc[:, 256:257], 1.0, w=[bvc])
                    g128 = gam[h] ** 128
                    selh = sel[:, h * 128:(h + 1) * 128]

                    for c in range(NCH):
                        c0 = c * 128
                        cs_ = slice(c0, c0 + 128)
                        if not pre:
                            T("matmul", psc[:, 0, :], k1[:, cs_], q1[:, cs_], start=True, stop=False, r=[bk1, bq1], w=[bsc[0]])
                            T("matmul", psc[:, 0, :], k2[:, cs_], q2[:, cs_], start=False, stop=True, r=[bk2, bq2], w=[bsc[0]])
                            if ret:
                                V("tensor_tensor", PTt[:, :], psc[:, 0, :], mask01[:, :], ALU.mult, r=[bsc[0], b_m01], w=[bPT])
                            else:
                                T("matmul", psc[:, 1, :], selh, grow["negM"][:, cs_], start=True, stop=False, r=[b_sel, bgrow["negM"]], w=[bsc[1]])
                                T("matmul", psc[:, 1, :], ident[:, :], maskneg[:, :], start=False, stop=True, r=[b_ident, b_mneg], w=[bsc[1]])
                                V("tensor_scalar", Et[:, :], psc[:, 1, :], colA[:, c, h:h + 1], 0.0, ALU.add, ALU.min, r=[bsc[1], bcolA], w=[bE])
                                A("activation", Et[:, :], Et[:, :], AF.Exp, r=[bE], w=[bE])
                                V("scalar_tensor_tensor", PTt[:, :], psc[:, 0, :], 1.0 / 16.0, Et[:, :], ALU.mult, ALU.mult, r=[bsc[0], bE], w=[bPT])
                                T("matmul", psc[:, 2, :], selh, grow["wp"][:, cs_], start=True, stop=True, r=[b_sel, bgrow["wp"]], w=[bsc[2]])
                                V("tensor_tensor", QP[:, 0, :], q1[:, cs_], psc[:, 2, :], ALU.mult, r=[bq1, bsc[2]], w=[bQP])
                                V("tensor_tensor", QP[:, 1, :], q2[:, cs_], psc[:, 2, :], ALU.mult, r=[bq2, bsc[2]], w=[bQP])
                        tr_to_tokens((v1, v2), (bv1, bv2), c0, 128, 0, vc, bvc)
                        if ret:
                            tr_to_tokens((k1, k2), (bk1, bk2), c0, 128, 1, kc, bkc, scale=g128)
                        else:
                            tr_to_tokens((k1, k2), (bk1, bk2), c0, 128, 1, kc, bkc, scale=colA[:, c, H + h:H + h + 1], bscale=[bcolA])
                        if not pre:
                            T("matmul", pnd[:, 0:W], PTt[:, :], vc[:, 0:W], start=True, stop=False, r=[bPT, bvc], w=[bnd])
                            if ret:
                                qa = (q1[:, cs_], q2[:, cs_]); bqa = (bq1, bq2)
                            else:
                                qa = (QP[:, 0, :], QP[:, 1, :]); bqa = (bQP, bQP)
                            for hh in range(2):
                                T("matmul", pnd[:, 0:W], qa[hh], Sbf[:, hh, 0:W], start=False, stop=(hh == 1), r=[bqa[hh], bSbf], w=[bnd])
                            if ret:
                                head_out(128, pnd[:, 0:256], gnr, b_gnr, (g1, g2), (bg1, bg2), c0, h)
                            else:
                                A("activation", hst[:, 10:11], pnd[:, 256:257], AF.Abs, r=[bnd], w=[bhst])
                                V("tensor_tensor", hst[:, 10:11], hst[:, 10:11], colA[:, c, 2 * H + h:2 * H + h + 1], ALU.max, r=[bhst, bcolA], w=[bhst])
                                V("reciprocal", hst[:, 11:12], hst[:, 10:11], r=[bhst], w=[bhst])
                                head_out(128, pnd[:, 0:256], gnm, b_gnm, (g1, g2), (bg1, bg2), c0, h, r_ap=hst[:, 11:12])
                        for hh in range(2):
                            T("matmul", pU[hh][:, 0:W], kc[:, hh * 128:(hh + 1) * 128], vc[:, 0:W], start=True, stop=True, r=[bkc, bvc], w=[bU[hh]])
                        for hh in range(2):
                            sc_ = g128 if ret else wcbc[:, h, c:c + 1]
                            V("scalar_tensor_tensor", S32[:, hh, 0:W], S32[:, hh, 0:W], sc_, pU[hh][:, 0:W], ALU.mult, ALU.add, r=[bS32, bU[hh], bwcbc], w=[bS32])
                        if c < NCH - 1 and not pre:
                            A("activation", Sbf[:, :, 0:W], S32[:, :, 0:W], AF.Identity, r=[bS32], w=[bSbf])
                    if pre:
                        V("tensor_scalar_mul", S32[:, :, 0:W], S32[:, :, 0:W], flag[:, :], r=[bS32, b_flag], w=[bS32])
                        dst = (bstate_r[h].rearrange("p (t e) -> p t e", t=2) if ret else bstate_c[h].rearrange("p (t e) -> p t e", t=2))
                        P.dma("sp", dst, S32[:, :, 0:W], bS32, reads=[bS32])
                        return
                    if ret:
                        P.dma("sp", o_retp[h].rearrange("(t p) e -> p t e", p=128), S32[:, :, 0:256], bS32, reads=[bS32])
                    else:
                        P.dma("sp", o_cp[h].rearrange("(t p) e -> p t e", p=128), S32[:, :, 0:256], bS32, reads=[bS32])
                        for t_ in range(2):
                            P.dma("sp", o_np[h:h + 1, t_ * 128:(t_ + 1) * 128].rearrange("o p -> p o"), S32[:, t_, 256:257], bS32, reads=[bS32], allow_slow_non_contiguous=True)
                    if SD > 0:
                        cs0 = NT
                        tr_to_tokens((v1, v2), (bv1, bv2), cs0, SD, 0, vsd, bvsd)
                        if ret:
                            tr_to_tokens((k1, k2), (bk1, bk2), cs0, SD, 1, ksd, bksd)
                        else:
                            V("memset", vsd[:SD, 256:257], 1.0, w=[bvsd])
                            tr_to_tokens((k1, k2), (bk1, bk2), cs0, SD, 1, ksd, bksd, scale=scol[:SD, h:h + 1], bscale=[bscol])
                            for t_ in range(2):
                                load(nS[:, t_, :], smn[:, h, t_ * 128:(t_ + 1) * 128].rearrange("s p -> p s"), bnS, allow_slow_non_contiguous=True)
                        V("tensor_copy", qsd[:, 0, :], q1[:, cs0:cs0 + SD], r=[bq1], w=[bqsd])
                        V("tensor_copy", qsd[:, 1, :], q2[:, cs0:cs0 + SD], r=[bq2], w=[bqsd])
                        for hh in range(2):
                            V("tensor_tensor", qms[:, hh, :].rearrange("p (s t) -> p s t", t=SD), qsd[:, hh, :].unsqueeze(2).to_broadcast([128, SD, SD]),
                              sdmask[:, :].rearrange("p (s t) -> p s t", t=SD), ALU.mult, r=[bqsd, b_sdm], w=[bqms])
                        for s in range(SD):
                            St, bSt = SS[s % 3], bSS[s % 3]
                            if ret:
                                load(St[:, :, 0:256], sret[s, h].rearrange("(t p) e -> p t e", p=128), bSt)
                            else:
                                load(St[:, :, 0:256], smc[s, h].rearrange("(t p) e -> p t e", p=128), bSt)
                                G("tensor_copy", St[:, :, 256], nS[:, :, s], r=[bnS], w=[bSt])
                            V("tensor_scalar_mul", kms[:SD, :], ksd[:SD, :], ident[:SD, s:s + 1], r=[bksd, b_ident], w=[bkms])
                            for hh in range(2):
                                T("matmul", pU[hh][:, 0:W], kms[:SD, hh * 128:(hh + 1) * 128], vsd[:SD, 0:W], start=True, stop=True, r=[bkms, bvsd], w=[bU[hh]])
                            for hh in range(2):
                                sc_ = gam[h] if ret else wpbcS[:, h, s:s + 1]
                                V("scalar_tensor_tensor", St[:, hh, 0:W], St[:, hh, 0:W], sc_, pU[hh][:, 0:W], ALU.mult, ALU.add, r=[bSt, bU[hh], bwpbcS], w=[bSt])
                            for hh in range(2):
                                T("matmul", pnd[:SD, 0:W], qms[:, hh, s * SD:(s + 1) * SD], St[:, hh, 0:W], start=(s == 0 and hh == 0), stop=(s == SD - 1 and hh == 1),
                                  r=[bqms, bSt], w=[bnd])
                            if ret:
                                P.dma("sp", o_rets[s, h].rearrange("(t p) e -> p t e", p=128), St[:, :, 0:256], bSt, reads=[bSt])
                            else:
                                P.dma("sp", o_cs[s, h].rearrange("(t p) e -> p t e", p=128), St[:, :, 0:256], bSt, reads=[bSt])
                                G("tensor_copy", nS[:, :, s], St[:, :, 256], r=[bSt], w=[bnS])
                        if ret:
                            head_out(SD, pnd[:SD, 0:256], gnr, b_gnr, (g1, g2), (bg1, bg2), cs0, h)
                        else:
                            for t_ in range(2):
                                P.dma("sp", o_ns[:, h, t_ * 128:(t_ + 1) * 128].rearrange("s p -> p s"), nS[:, t_, :], bnS, reads=[bnS], allow_slow_non_contiguous=True)
                            A("activation", hst[:SD, 10:11], pnd[:SD, 256:257], AF.Abs, r=[bnd], w=[bhst])
                            V("tensor_tensor", hst[:SD, 10:11], hst[:SD, 10:11], scol[:SD, H + h:H + h + 1], ALU.max, r=[bhst, bscol], w=[bhst])
                            V("reciprocal", hst[:SD, 11:12], hst[:SD, 10:11], r=[bhst], w=[bhst])
                            head_out(SD, pnd[:SD, 0:256], gnm, b_gnm, (g1, g2), (bg1, bg2), cs0, h, r_ap=hst[:SD, 11:12])
                    row0 = (0 if ret else RW) + h * 256
                    P.dma("sp", o_mix[row0:row0 + 256, :].rearrange("(t p) n -> p t n", p=128), mixT[:, :, :], bmix, reads=[bmix])

                gate_stage()
                for h in range(H):
                    run_head("ret", h)
                    run_head("ml", h)
                if not pre:
                    P.dma("sp", o_mp, mrow[:, NCH:NCH + 1], bmrow, reads=[bmrow])
                    for j_ in range(3):
                        P.dma("sp", o_convp[j_].rearrange("(t p) -> p t", p=128), convp[:, j_, :], bconvp, reads=[bconvp], allow_slow_non_contiguous=True)
            P.barrier()

        own_tiles = [(xo[c * 128:(c + 1) * 128, :], 128, c * 128, b_XT[c]) for c in range(NCH)]
        if SD > 0:
            own_tiles.append((xs[:, :], SD, NT, b_XT[NCH]))
        pre_tiles = [(xp[c * 128:(c + 1) * 128, :], 128, c * 128, b_XT[c]) for c in range(NCH)]
        all_tiles = [(c * 128, 128) for c in range(NCH)] + ([(NT, SD)] if SD > 0 else [])

        def phase_wout():
            with ExitStack() as ph:
                if KTM <= KT:
                    mixS = XT
                else:
                    mixS = P.sb("mixS", [128, KTM, NTO], BF16, ph)
                bmixS = P.buf("mixS")
                load(mixS[:, 0:KTM, :], o_mix.rearrange("(t p) n -> p t n", p=128), bmixS)
                wo = [P.sb("wo", [128, KTM, CBS], BF16, ph) for _ in range(2)]; bwo = [P.buf("wo") for _ in range(2)]
                pyo = [P.ps("pyo", [128, 512], F32, ph) for _ in range(2)]; bpyo = [P.buf("pyo") for _ in range(2)]
                ysb = [P.sb("ysb", [128, CBS], F32, ph) for _ in range(2)]; bysb = [P.buf("ysb") for _ in range(2)]
                it = 0
                for cb in range(NCB):
                    w_, bw_ = wo[cb % 2], bwo[cb % 2]
                    load(w_[:].rearrange("p k c -> p (k c)"), wo_d[cb], bw_, q="pool")
                    for (c0, n) in all_tiles:
                        p_, bp = pyo[it % 2], bpyo[it % 2]
                        y_, by = ysb[it % 2], bysb[it % 2]
                        it += 1
                        for kt in range(KTM):
                            T("matmul", p_[:n, :CBS], mixS[:, kt, c0:c0 + n], w_[:, kt, :], start=(kt == 0), stop=(kt == KTM - 1), r=[bmixS, bw_], w=[bp])
                        A("activation", y_[:n, :], p_[:n, :CBS], AF.Identity, r=[bp], w=[by])
                        P.dma("sp", y_d[c0:c0 + n, cb * CBS:(cb + 1) * CBS], y_[:n, :], by, reads=[by])
            P.barrier()

        def ln_rows(x, b, n, s, bs, pcs):
            nst = len(pcs)
            for c, (c0, c1) in enumerate(pcs):
                V("bn_stats", s[:n, c * 6:(c + 1) * 6], x[:n, c0:c1], r=[b], w=[bs])
            o = nst * 6
            V("bn_aggr", s[:n, o:o + 2], s[:n, 0:o], r=[bs], w=[bs])
            rstd_op(s[:n, o + 2:o + 3], s[:n, o + 1:o + 2], n, [bs], [bs])
            V("scalar_tensor_tensor", s[:n, o + 3:o + 4], s[:n, o:o + 1], -1.0, s[:n, o + 2:o + 3], ALU.mult, ALU.mult, r=[bs], w=[bs])
            A("activation", x[:n, :], x[:n, :], AF.Identity, bias=s[:n, o + 3:o + 4], scale=s[:n, o + 2:o + 3], r=[b, bs], w=[b])

        def phase_ln1():
            with ExitStack() as ph:
                geb, bgeb = P.sb("geb", [128, D], F32, ph), P.buf("geb"); load(geb[:], geb_d, bgeb)
                beb, bbeb = P.sb("beb", [128, D], F32, ph), P.buf("beb"); load(beb[:], beb_d, bbeb)
                g1b, bg1b = P.sb("g1b", [128, D], F32, ph), P.buf("g1b"); load(g1b[:], g1b_d, bg1b)
                b1b, bb1b = P.sb("b1b", [128, D], F32, ph), P.buf("b1b"); load(b1b[:], b1b_d, bb1b)
                xt_ = P.sb("l1x", [128, D], F32, ph); bx = P.buf("l1x")
                yt_ = P.sb("l1y", [128, D], F32, ph); by = P.buf("l1y")
                pcs = pieces(D)
                st = P.sb("l1st", [128, len(pcs) * 6 + 8], F32, ph); bst = P.buf("l1st")
                pt = [P.ps("l1pt", [128, 4, 128], F32, ph) for _ in range(2)]; bpt = [P.buf("l1pt") for _ in range(2)]
                gi = 0
                for ti, (c0, n) in enumerate(all_tiles):
                    src = xo[c0:c0 + n, :] if c0 < NT else xs[:, :]
                    load(xt_[:n, :], src, bx)
                    load(yt_[:n, :], y_d[c0:c0 + n, :], by)
                    ln_rows(xt_, bx, n, st, bst, pcs)
                    V("tensor_tensor", xt_[:n, :], xt_[:n, :], geb[:n, :], ALU.mult, r=[bx, bgeb], w=[bx])
                    G("tensor_tensor", xt_[:n, :], xt_[:n, :], beb[:n, :], ALU.add, r=[bx, bbeb], w=[bx])
                    V("scalar_tensor_tensor", yt_[:n, :], xt_[:n, :], ALPHA, yt_[:n, :], ALU.mult, ALU.add, r=[bx, by], w=[by])
                    ln_rows(yt_, by, n, st, bst, pcs)
                    V("tensor_tensor", yt_[:n, :], yt_[:n, :], g1b[:n, :], ALU.mult, r=[by, bg1b], w=[by])
                    G("tensor_tensor", yt_[:n, :], yt_[:n, :], b1b[:n, :], ALU.add, r=[by, bb1b], w=[by])
                    P.dma("sp", x1_d[c0:c0 + n, :], yt_[:n, :], by, reads=[by])
                    xb = b_XT[ti]
                    for k0 in range(0, KT, 4):
                        p_, bp = pt[gi % 2], bpt[gi % 2]; gi += 1
                        nk = min(4, KT - k0)
                        for j in range(nk):
                            T("transpose", p_[:, j, :n], yt_[:n, (k0 + j) * 128:(k0 + j + 1) * 128], ident[:n, :n], r=[by, b_ident], w=[bp])
                        if (k0 // 4) % 2 == 0:
                            A("activation", XT[:, k0:k0 + nk, c0:c0 + n], p_[:, 0:nk, :n], AF.Identity, r=[bp], w=[xb])
                        else:
                            V("tensor_copy", XT[:, k0:k0 + nk, c0:c0 + n], p_[:, 0:nk, :n], r=[bp], w=[xb])
            P.barrier()

        def phase_peer():
            TG = cfg.TG
            groups = [all_tiles[i:i + TG] for i in range(0, len(all_tiles), TG)]
            tile_idx = {c0: i for i, (c0, n) in enumerate(all_tiles)}
            HP = PH * 2
            for grp in groups:
                ng = len(grp)
                with ExitStack() as gph:
                    yacc = P.sb("yacc", [128, TG, D], F32, gph); byacc = [P.buf("yacc") for _ in range(TG)]
                    xbufs = [b_XT[tile_idx[c0]] for (c0, n) in grp]
                    s_ = P.sb("s_", [128, TG, HP * NK], F32, gph); bs_ = [P.buf("s_") for _ in range(TG)]
                    rt = P.sb("rt", [128, TG, 4 * PH], F32, gph); brt = [P.buf("rt") for _ in range(TG)]
                    with ExitStack() as ph:
                        phh = [P.ps("phh", [128, 512], F32, ph) for _ in range(TG)]; bphh = [P.buf("phh") for _ in range(TG)]
                        py = [P.ps("py", [128, 512], F32, ph) for _ in range(2)]; bpy = [P.buf("py") for _ in range(2)]
                        pmi = [P.ps("pmi", [128, 512], F32, ph) for _ in range(2)]; bpmi = [P.buf("pmi") for _ in range(2)]
                        wr = [P.sb("wr", [128, KG, 512], BF16, ph) for _ in range(4)]; bwr = [P.buf("wr") for _ in range(4)]
                        wrs = {"i": 0}
                        wpp = [P.sb("wpp", [128, PT_, CBS], BF16, ph) for _ in range(2)]; bwpp = [P.buf("wpp") for _ in range(2)]
                        skT, bskT = P.sb("skT", [128, HP * NK], F32, ph), P.buf("skT"); load(skT[:], skT_d, bskT)
                        sv = P.sb("sv", [128, HP * 16], F32, ph); bsv = P.buf("sv")
                        scr = P.sb("scr", [128, 512], F32, ph); bscr = P.buf("scr")
                        cand = [P.sb("cand", [128, 256], F32, ph) for _ in range(2)]; bcand = [P.buf("cand") for _ in range(2)]
                        tv = P.sb("tv", [128, PH * 24], F32, ph); btv = P.buf("tv")
                        qsb = P.sb("qsb", [128, 512], F32, ph); bqsb = P.buf("qsb")
                        qT = P.sb("qT", [128, 4, 128], F32, ph); bqT = P.buf("qT")
                        pTt = P.sb("pTt", [128, TG, PT_ * 128], BF16, ph); bpTt = [P.buf("pTt") for _ in range(TG)]
                        prow = P.sb("prow", [128, PLE], F32, ph); bprow = P.buf("prow")
                        sg = [P.sb("sg", [128, CBS], F32, ph) for _ in range(1)]; bsg = [P.buf("sg") for _ in range(1)]
                        rng = {"T": 0, "a": 0, "py": 0, "sg": 0, "mi": 0}

                        def wpiece(src_ap, ncols):
                            i = wrs["i"] % 4; wrs["i"] += 1
                            load(wr[i][:, :, 0:ncols], src_ap.rearrange("p (k c) -> p k c", c=ncols), bwr[i], q="pool")
                            return wr[i], bwr[i]

                        for gi, (c0, n) in enumerate(grp):
                            load(yacc[:n, gi, :], x1_d[c0:c0 + n, :], byacc[gi])
                            psrc = po[c0:c0 + n, :] if c0 < NT else ps_d[:, :]
                            load(prow[:n, :], psrc, bprow)
                            p_, bp = pmi[rng["mi"] % 2], bpmi[rng["mi"] % 2]; rng["mi"] += 1
                            for j in range(PT_):
                                T("transpose", p_[:, j * 128:j * 128 + n], prow[:n, j * 128:(j + 1) * 128], ident[:n, :n], r=[bprow, b_ident], w=[bp])
                            for j in range(PT_):
                                A("activation", pTt[:, gi, j * 128:j * 128 + n], p_[:, j * 128:j * 128 + n], AF.Identity, r=[bp], w=[bpTt[gi]])

                        for gi, (c0, n) in enumerate(grp):
                            pass
                        for cq in range(NCQ):
                            for kg in range(NKG):
                                wp_, bwp_ = wpiece(wq_d[cq, kg], CQ)
                                for gi, (c0, n) in enumerate(grp):
                                    for j in range(KG):
                                        kt = kg * KG + j
                                        T("matmul", phh[gi][:n, :CQ], XT[:, kt, c0:c0 + n], wp_[:, j, :CQ], start=(kt == 0), stop=(kt == KT - 1), r=[xbufs[gi], bwp_], w=[bphh[gi]])
                            for gi, (c0, n) in enumerate(grp):
                                A("activation", qsb[:n, :CQ], phh[gi][:n, :CQ], AF.Identity, r=[bphh[gi]], w=[bqsb])
                                nj = CQ // 128
                                p_, bp = pmi[rng["mi"] % 2], bpmi[rng["mi"] % 2]; rng["mi"] += 1
                                for j in range(nj):
                                    T("transpose", p_[:, j * 128:j * 128 + n], qsb[:n, j * 128:(j + 1) * 128], ident[:n, :n], r=[bqsb, b_ident], w=[bp])
                                for j in range(nj):
                                    A("activation", qT[:, j, :n], p_[:, j * 128:j * 128 + n], AF.Identity, r=[bp], w=[bqT])
                                hpb = 512 // NK if NK <= 512 else 1
                                for j0 in range(0, nj, hpb):
                                    p2, bp2 = pmi[rng["mi"] % 2], bpmi[rng["mi"] % 2]; rng["mi"] += 1
                                    jn = min(hpb, nj - j0)
                                    for j in range(j0, j0 + jn):
                                        hp = cq * nj + j
                                        T("matmul", p2[:n, (j - j0) * NK:(j - j0 + 1) * NK], qT[:, j, :n], skT[:, hp * NK:(hp + 1) * NK], start=True, stop=True, r=[bqT, bskT], w=[bp2])
                                    hp0 = cq * nj + j0
                                    A("activation", s_[:n, gi, hp0 * NK:(hp0 + jn) * NK], p2[:n, 0:jn * NK], AF.Identity, r=[bp2], w=[bs_[gi]])
                        for gi, (c0, n) in enumerate(grp):
                            s3 = s_[:, gi, :].rearrange("p (a k) -> p a k", k=NK)
                            sv3 = sv[:, :].rearrange("p (a k) -> p a k", k=16)
                            for hp in range(HP):
                                V("max", sv3[:n, hp, 0:8], s3[:n, hp, :], r=[bs_[gi]], w=[bsv])
                                V("match_replace", scr[:n, 0:NK], sv3[:n, hp, 0:8], s3[:n, hp, :], -1e30, r=[bs_[gi], bsv], w=[bscr])
                                V("max", sv3[:n, hp, 8:16], scr[:n, 0:NK], r=[bscr], w=[bsv])
                            tv3 = tv[:, :].rearrange("p (h k) -> p h k", k=24)
                            r4 = rt[:, gi, :].rearrange("p (a h) -> p a h", h=PH)
                            for h in range(PH):
                                V("tensor_tensor", cand[0][:n, :].rearrange("p (a b) -> p a b", b=16), sv3[:n, 2 * h, :].unsqueeze(2).to_broadcast([n, 16, 16]),
                                  sv3[:n, 2 * h + 1, :].unsqueeze(1).to_broadcast([n, 16, 16]), ALU.add, r=[bsv], w=[bcand[0]])
                                V("max", tv3[:n, h, 0:8], cand[0][:n, :], r=[bcand[0]], w=[btv])
                                V("match_replace", cand[1][:n, :], tv3[:n, h, 0:8], cand[0][:n, :], -1e30, r=[bcand[0], btv], w=[bcand[1]])
                                V("max", tv3[:n, h, 8:16], cand[1][:n, :], r=[bcand[1]], w=[btv])
                                V("match_replace", cand[0][:n, :], tv3[:n, h, 8:16], cand[1][:n, :], -1e30, r=[bcand[1], btv], w=[bcand[0]])
                                V("max", tv3[:n, h, 16:24], cand[0][:n, :], r=[bcand[0]], w=[btv])
                            V("tensor_tensor", r4[:n, 0, :], tv3[:n, :, 15], tv3[:n, :, 16], ALU.add, r=[btv], w=[brt[gi]])
                            V("tensor_scalar_mul", r4[:n, 0, :], r4[:n, 0, :], 0.5, r=[brt[gi]], w=[brt[gi]])
                            V("tensor_scalar_mul", r4[:n, 1, :], tv3[:n, :, 0], -1.0, r=[btv], w=[brt[gi]])
                            for h in range(PH):
                                A("activation", scr[:n, 0:16], tv3[:n, h, 0:16], AF.Exp, bias=r4[:n, 1, h:h + 1], accum_out=r4[:n, 2, h:h + 1], r=[btv, brt[gi]], w=[bscr, brt[gi]])
                            A("activation", r4[:n, 2, :], r4[:n, 2, :], AF.Ln, r=[brt[gi]], w=[brt[gi]])
                            V("tensor_tensor", r4[:n, 3, :], r4[:n, 1, :], r4[:n, 2, :], ALU.subtract, r=[brt[gi]], w=[brt[gi]])

                        for cb in range(NCB):
                            wq_, bwq_ = wpp[cb % 2], bwpp[cb % 2]
                            load(wq_[:].rearrange("p k c -> p (k c)"), wpp_d[cb], bwq_, q="pool")
                            for kg in range(NKG):
                                wp_, bwp_ = wpiece(wpg_d[cb, kg], CBS)
                                for gi, (c0, n) in enumerate(grp):
                                    for j in range(KG):
                                        kt = kg * KG + j
                                        T("matmul", phh[gi][:n, :CBS], XT[:, kt, c0:c0 + n], wp_[:, j, :CBS], start=(kt == 0), stop=(kt == KT - 1), r=[xbufs[gi], bwp_], w=[bphh[gi]])
                            for gi, (c0, n) in enumerate(grp):
                                sg_, bsg_ = sg[0], bsg[0]
                                p_, bp = py[rng["py"] % 2], bpy[rng["py"] % 2]; rng["py"] += 1
                                A("activation", sg_[:n, :], phh[gi][:n, :CBS], AF.Sigmoid, r=[bphh[gi]], w=[bsg_])
                                for j in range(PT_):
                                    T("matmul", p_[:n, :CBS], pTt[:, gi, j * 128:j * 128 + n], wq_[:, j, :], start=(j == 0), stop=(j == PT_ - 1), r=[bpTt[gi], bwq_], w=[bp])
                                V("tensor_tensor", sg_[:n, :], sg_[:n, :], p_[:n, :CBS], ALU.mult, r=[bsg_, bp], w=[bsg_])
                                ysl = yacc[:n, gi, cb * CBS:(cb + 1) * CBS]
                                V("scalar_tensor_tensor", ysl, ysl, ALPHA, sg_[:n, :], ALU.mult, ALU.add, r=[byacc[gi], bsg_], w=[byacc[gi]])

                    P.barrier()
                    with ExitStack() as ph:
                        NPY = max(2, min(3, 8 - TG - 2))
                        phh = [P.ps("phh", [128, 512], F32, ph) for _ in range(TG)]; bphh = [P.buf("phh") for _ in range(TG)]
                        py = [P.ps("py", [128, 512], F32, ph) for _ in range(NPY)]; bpy = [P.buf("py") for _ in range(NPY)]
                        paT = [P.ps("paT", [128, 1024], BF16, ph) for _ in range(2)]; bpaT = [P.buf("paT") for _ in range(2)]
                        NR = 8
                        ring = [P.sb("wring", [128, max(KG, ET), 512], BF16, ph) for _ in range(NR)]; bring = [P.buf("wring") for _ in range(NR)]
                        Tt = [P.sb("Tt", [128, EB], F32, ph) for _ in range(2)]; bTt = [P.buf("Tt") for _ in range(2)]
                        Ex = [P.sb("Ex", [128, EB], BF16, ph) for _ in range(2)]; bEx = [P.buf("Ex") for _ in range(2)]
                        Mm = [P.sb("Mm", [128, EB], BF16, ph) for _ in range(2)]; bMm = [P.buf("Mm") for _ in range(2)]
                        Gacc = [[P.sb("Gacc", [128, EB], BF16, ph) for _ in range(TG)] for _ in range(2)]
                        bGacc = [[P.buf("Gacc") for _ in range(TG)] for _ in range(2)]
                        asb = [P.sb("asb", [128, EB], BF16, ph) for _ in range(2)]; basb = [P.buf("asb") for _ in range(2)]
                        abf = [P.sb("abf", [128, EB], BF16, ph) for _ in range(2)]; babf = [P.buf("abf") for _ in range(2)]
                        aT = [P.sb("aT", [128, ET, 128], BF16, ph) for _ in range(TG)]; baT = [P.buf("aT") for _ in range(TG)]
                        rng = {"T": 0, "a": 0, "py": 0, "sg": 0, "mi": 0}

                        seq = []
                        for eb in range(NEB):
                            seq += [("u", eb, kg) for kg in range(NKG)]
                            seq += [("v", eb, cb) for cb in range(NCB)]
                        wst = {"pos": 0}

                        def wnext():
                            j = wst["pos"]; wst["pos"] += 1
                            kind, eb_, x = seq[j]
                            t_, b_ = ring[j % NR], bring[j % NR]
                            if kind == "u":
                                load(t_[:, 0:KG, 0:EB], uT_d[eb_, x].rearrange("p (k c) -> p k c", c=EB), b_, q="pool")
                            else:
                                load(t_[:, 0:ET, 0:CBS], v_d[eb_, x].rearrange("p (k c) -> p k c", c=CBS), b_, q="pool")
                            return t_, b_

                        def gate_units(eb, par):
                            units = []
                            i0 = eb * NI
                            for gi, (c0, n) in enumerate(grp):
                                for h in range(PH):
                                    def unit(gi=gi, n=n, h=h):
                                        s3 = s_[:, gi, :].rearrange("p (a k) -> p a k", k=NK)
                                        r4 = rt[:, gi, :].rearrange("p (a h) -> p a h", h=PH)
                                        GA, bGA = Gacc[par][gi], bGacc[par][gi]
                                        ti = rng["T"] % 2; rng["T"] += 1
                                        V("tensor_tensor", Tt[ti][:n, :].rearrange("p (i k) -> p i k", k=NK), s3[:n, 2 * h, i0:i0 + NI].unsqueeze(2).to_broadcast([n, NI, NK]),
                                          s3[:n, 2 * h + 1, :].unsqueeze(1).to_broadcast([n, NI, NK]), ALU.add, r=[bs_[gi]], w=[bTt[ti]])
                                        A("activation", Ex[ti][:n, :], Tt[ti][:n, :], AF.Exp, bias=r4[:n, 3, h:h + 1], r=[bTt[ti], brt[gi]], w=[bEx[ti]])
                                        dst, bdst = (GA, bGA) if h == 0 else (Mm[ti], bMm[ti])
                                        V("scalar_tensor_tensor", dst[:n, :], Tt[ti][:n, :], r4[:n, 0, h:h + 1], Ex[ti][:n, :], ALU.is_ge, ALU.mult, r=[bTt[ti], bEx[ti], brt[gi]], w=[bdst])
                                        if h > 0:
                                            V("tensor_tensor", GA[:n, :], GA[:n, :], Mm[ti][:n, :], ALU.add, r=[bGA, bMm[ti]], w=[bGA])
                                    units.append(unit)
                            return units

                        for u_ in gate_units(0, 0):
                            u_()
                        for eb in range(NEB):
                            par = eb % 2
                            nxt = gate_units(eb + 1, 1 - par) if eb + 1 < NEB else []
                            UQ = max(0, -(-(len(nxt) - NCB) // NKG))
                            for kg in range(NKG):
                                wp_, bwp_ = wnext()
                                for gi, (c0, n) in enumerate(grp):
                                    for j in range(KG):
                                        kt = kg * KG + j
                                        T("matmul", phh[gi][:n, :EB], XT[:, kt, c0:c0 + n], wp_[:, j, :EB], start=(kt == 0), stop=(kt == KT - 1), r=[xbufs[gi], bwp_], w=[bphh[gi]])
                                for _ in range(UQ):
                                    if nxt:
                                        nxt.pop(0)()
                            for gi, (c0, n) in enumerate(grp):
                                ai = rng["a"] % 2; rng["a"] += 1
                                A("activation", asb[ai][:n, :], phh[gi][:n, :EB], AF.Gelu, r=[bphh[gi]], w=[basb[ai]])
                                V("tensor_tensor", abf[ai][:n, :], asb[ai][:n, :], Gacc[par][gi][:n, :], ALU.mult, r=[basb[ai], bGacc[par][gi]], w=[babf[ai]])
                                pa, bpa = paT[gi % 2], bpaT[gi % 2]
                                for j in range(ET):
                                    T("transpose", pa[:, j * 128:j * 128 + n], abf[ai][:n, j * 128:(j + 1) * 128], identb[:n, :n], r=[babf[ai], b_identb], w=[bpa])
                                for j in range(ET):
                                    A("activation", aT[gi][:, j, :n], pa[:, j * 128:j * 128 + n], AF.Identity, r=[bpa], w=[baT[gi]])
                            for cb in range(NCB):
                                vr_, bvr_ = wnext()
                                for gi, (c0, n) in enumerate(grp):
                                    p_, bp = py[rng["py"] % NPY], bpy[rng["py"] % NPY]; rng["py"] += 1
                                    for j in range(ET):
                                        T("matmul", p_[:n, :CBS], aT[gi][:, j, :n], vr_[:, j, :CBS], start=(j == 0), stop=(j == ET - 1), r=[baT[gi], bvr_], w=[bp])
                                    ysl = yacc[:n, gi, cb * CBS:(cb + 1) * CBS]
                                    V("tensor_tensor", ysl, ysl, p_[:n, :CBS], ALU.add, r=[byacc[gi], bp], w=[byacc[gi]])
                                if nxt:
                                    nxt.pop(0)()
                            while nxt:
                                nxt.pop(0)()
                    P.barrier()
                    with ExitStack() as ph2:
                        g2b, bg2b = P.sb("g2b", [128, D], F32, ph2), P.buf("g2b"); load(g2b[:], g2b_d, bg2b)
                        b2b, bb2b = P.sb("b2b", [128, D], F32, ph2), P.buf("b2b"); load(b2b[:], b2b_d, bb2b)
                        pcs = pieces(D)
                        st = P.sb("l2st", [128, len(pcs) * 6 + 8], F32, ph2); bst = P.buf("l2st")
                        for gi, (c0, n) in enumerate(grp):
                            yv = yacc[:, gi, :]
                            ln_rows(yv, byacc[gi], n, st, bst, pcs)
                            V("tensor_tensor", yv[:n, :], yv[:n, :], g2b[:n, :], ALU.mult, r=[byacc[gi], bg2b], w=[byacc[gi]])
                            G("tensor_tensor", yv[:n, :], yv[:n, :], b2b[:n, :], ALU.add, r=[byacc[gi], bb2b], w=[byacc[gi]])
                            P.dma("sp", y_o[c0:c0 + n, :], yv[:n, :], byacc[gi], reads=[byacc[gi]])
                    P.barrier()

        stop = int(os.environ.get("KSTOP", "99"))
        if stop >= 1:
            phase_convbuf()
        if stop >= 2:
            phase_ln(pre_tiles)
        if stop >= 3:
            mixer_phase("pre")
        if stop >= 4:
            phase_ln(own_tiles)
        if stop >= 5:
            mixer_phase("own")
        if stop >= 6:
            phase_wout()
        if stop >= 7:
            phase_ln1()
        if stop >= 8:
            phase_peer()
        P.emit()
    return nc


ROPE_BASE = 10000.0
PAST_LEN = 16384


def col_layout(v, ntile):
    return np.ascontiguousarray(np.asarray(v, np.float32).reshape(ntile, 128).T)


def rope_tables(pos, H, dq_fn, dk_fn, idx):
    half = 128
    inv = (np.float32(ROPE_BASE) ** (-np.arange(half, dtype=np.float32) / np.float32(half))).astype(np.float32)
    ang = pos.astype(np.float32)[:, None] * inv[None]
    cos = np.cos(ang).astype(np.float32).T
    sin = np.sin(ang).astype(np.float32).T
    tq = np.empty((H, 2, 128, len(pos)), np.float32)
    tk = np.empty((H, 2, 128, len(pos)), np.float32)
    for h in range(H):
        dq = dq_fn(h, idx)[None, :]
        dk = dk_fn(h, idx)[None, :]
        tq[h, 0] = cos * dq; tq[h, 1] = sin * dq
        tk[h, 0] = cos * dk; tk[h, 1] = sin * dk
    return tq, tk


_NC_CACHE = {}


def kernel(x_prompt, x_sample, state_ret, state_conv, state_mlstm_c, state_mlstm_n, state_mlstm_m, p_prompt, p_sample,
           ln_emb_g, ln_emb_b, w_in, b_gate, conv_w, conv_b, g_ret_norm, g_ml_norm, w_out, ln1_g, ln1_b,
           w_peer_q, peer_sub_keys, peer_u, peer_v, w_ple_gate, w_ple_proj, ln2_g, ln2_b, _debug=None):
    f32 = np.float32
    B, SEQ, D = x_prompt.shape
    DB = x_sample.shape[0]
    H = state_ret.shape[2]
    cps = NCORES // B
    assert cps == 2
    cfg = Cfg()
    cfg.D = D; cfg.KT = D // 128; cfg.H = H
    cfg.NT = SEQ // cps; cfg.SD = DB // NCORES
    NT, SD, KT = cfg.NT, cfg.SD, cfg.KT
    NTO = NT + SD
    RW = H * 256
    NCT = 8 * RW // 128
    NMT = 2 * RW // 128
    cfg.NK = peer_sub_keys.shape[3]; cfg.PH = peer_sub_keys.shape[1]; cfg.PLE = p_prompt.shape[-1]
    cfg.ALPHA = float((2.0 * w_in.shape[0]) ** 0.25)
    ntiles = NT // 128 + (1 if SD > 0 else 0)
    cfg.TG = 3 if ntiles >= 3 and ntiles % 3 == 0 else 2
    cfg.TG = int(os.environ.get("KTG", cfg.TG))
    NK, PH, PLE = cfg.NK, cfg.PH, cfg.PLE
    NE = NK * NK
    KTM = 2 * RW // 128
    CBS = min(512, D); NCB = D // CBS
    KG = min(int(os.environ.get("KKG", "4")), KT); NKG = KT // KG
    QW = PH * 256; CQ = min(512, QW); NCQ = QW // CQ
    EB = min(512, NE); NEB = NE // EB; ET = EB // 128
    PT_ = PLE // 128
    key = (D, H, NT, SD, NK, PH, PLE)
    if key not in _NC_CACHE:
        _NC_CACHE[key] = build(cfg)
    nc = _NC_CACHE[key]

    w_in0 = np.asarray(w_in[0], f32)
    wm = np.ascontiguousarray(w_in0[:, :NCT * 128].reshape(KT, 128, NCT, 128).transpose(2, 1, 0, 3)).reshape(NCT, 128, KT * 128)
    wg = np.ascontiguousarray(w_in0[:, NCT * 128:].reshape(KT, 128, 2 * H).transpose(1, 0, 2)).reshape(128, KT * 2 * H)
    bg = np.ascontiguousarray(np.asarray(b_gate[0], f32).reshape(2, H).T)
    cw = np.asarray(conv_w[0], f32)
    convw = np.ascontiguousarray(cw.reshape(4, NMT, 128).transpose(2, 1, 0)).reshape(128, NMT * 4)
    convb = col_layout(conv_b[0], NMT)
    gnr = col_layout(np.asarray(g_ret_norm[0], f32).reshape(-1), 2 * H)
    gnm = col_layout(np.asarray(g_ml_norm[0], f32).reshape(-1), 2 * H)
    lng = col_layout(ln_emb_g, KT); lnb = col_layout(ln_emb_b, KT)
    ident = np.eye(128, dtype=f32)
    jj, ii = np.meshgrid(np.arange(128), np.arange(128), indexing="ij")
    mask01 = (jj <= ii).astype(f32)
    maskneg = np.where(jj <= ii, 0.0, -30000.0).astype(f32)
    sel = np.zeros((128, H, 128), f32)
    for h in range(H):
        sel[h, h, :] = 1.0
    sel = sel.reshape(128, H * 128)
    sdmask = np.zeros((128, SD, SD), f32)
    for s in range(SD):
        sdmask[:, s, s] = 1.0
    sdmask = sdmask.reshape(128, SD * SD)

    wo = np.ascontiguousarray(np.asarray(w_out[0], f32).reshape(KTM, 128, NCB, CBS).transpose(2, 1, 0, 3)).reshape(NCB, 128, KTM * CBS)
    bc = lambda v: np.ascontiguousarray(np.broadcast_to(np.asarray(v, f32).reshape(1, D), (128, D)))
    geb, beb, g1b, b1b, g2b, b2b = bc(ln_emb_g), bc(ln_emb_b), bc(ln1_g[0]), bc(ln1_b[0]), bc(ln2_g[0]), bc(ln2_b[0])
    wq = np.ascontiguousarray(np.asarray(w_peer_q[0], f32).reshape(NKG, KG, 128, NCQ, CQ).transpose(3, 0, 2, 1, 4)).reshape(NCQ, NKG, 128, KG * CQ)
    skT = np.ascontiguousarray(np.asarray(peer_sub_keys[0], f32).transpose(3, 0, 1, 2)).reshape(128, PH * 2 * NK)
    uT = np.ascontiguousarray(np.asarray(peer_u[0], f32).reshape(NEB, EB, NKG, KG, 128).transpose(0, 2, 4, 3, 1)).reshape(NEB, NKG, 128, KG * EB)
    vv = np.ascontiguousarray(np.asarray(peer_v[0], f32).reshape(NEB, ET, 128, NCB, CBS).transpose(0, 3, 2, 1, 4)).reshape(NEB, NCB, 128, ET * CBS)
    wpg = np.ascontiguousarray(np.asarray(w_ple_gate[0], f32).reshape(NKG, KG, 128, NCB, CBS).transpose(3, 0, 2, 1, 4)).reshape(NCB, NKG, 128, KG * CBS)
    wpp = np.ascontiguousarray(np.asarray(w_ple_proj[0], f32).reshape(PT_, 128, NCB, CBS).transpose(2, 1, 0, 3)).reshape(NCB, 128, PT_ * CBS)
    ps_all = np.asarray(p_sample[0], f32).reshape(DB, PLE)

    gam = [np.float64(1.0 - 2.0 ** (-5.0 - h)) for h in range(H)]
    idx_own = np.concatenate([np.arange(NT) % 128, np.zeros(SD, np.int64)])
    is_s = np.concatenate([np.zeros(NT, bool), np.ones(SD, bool)])

    def dq_own(h, idx):
        return np.where(is_s, 1.0, gam[h] ** (idx + 1.0)).astype(f32)

    def dk_own(h, idx):
        return np.where(is_s, 1.0 / 16.0, gam[h] ** (-(idx + 1.0)) / 16.0).astype(f32)

    def dk_pre(h, idx):
        return (gam[h] ** (-(idx + 1.0)) / 16.0).astype(f32)

    in_maps = []
    tabs = {}
    for half in range(cps):
        pos_own = np.concatenate([half * NT + np.arange(NT), np.full(SD, PAST_LEN)])
        tq, tk = rope_tables(pos_own, H, dq_own, dk_own, idx_own)
        pos_pre = np.arange(NT)
        _, tkp = rope_tables(pos_pre, H, dk_pre, dk_pre, np.arange(NT) % 128)
        tabs[half] = (tq, tk, tkp)
    xs_all = np.asarray(x_sample, f32).reshape(DB, D)
    for c in range(NCORES):
        b, half = c // cps, c % cps
        xb = np.asarray(x_prompt[b], f32)
        xo = xb[half * NT:(half + 1) * NT]
        xp = xb[0:NT] if half == 1 else xo
        s0 = c * SD
        tq, tk, tkp = tabs[half]
        in_maps.append(dict(
            xo=np.ascontiguousarray(xo), xp=np.ascontiguousarray(xp), xs=np.ascontiguousarray(xs_all[s0:s0 + SD]),
            flag=np.full((128, 1), float(half), f32), lng=lng, lnb=lnb, wm=wm, wg=wg, bg=bg, convw=convw, convb=convb,
            gnr=gnr, gnm=gnm, tq=tq, tk=tk, tkp=tkp, ident=ident, mask01=mask01, maskneg=maskneg, sel=sel, sdmask=sdmask,
            sret=np.ascontiguousarray(state_ret[0, s0:s0 + SD]), sconv=np.ascontiguousarray(np.asarray(state_conv[0, s0:s0 + SD], f32).reshape(SD * 3, 2 * RW)),
            smc=np.ascontiguousarray(state_mlstm_c[0, s0:s0 + SD]), smn=np.ascontiguousarray(state_mlstm_n[0, s0:s0 + SD]),
            smm=np.ascontiguousarray(state_mlstm_m[0, s0:s0 + SD]),
            wo=wo, geb=geb, beb=beb, g1b=g1b, b1b=b1b, g2b=g2b, b2b=b2b, wq=wq, skT=skT, uT=uT, vv=vv, wpg=wpg, wpp=wpp,
            po=np.ascontiguousarray(np.asarray(p_prompt[0, b], f32)[half * NT:(half + 1) * NT]), ps=np.ascontiguousarray(ps_all[s0:s0 + SD]),
        ))
    res = run_bass_kernel_spmd(nc, in_maps, core_ids=list(range(NCORES))).results

    last = [b * cps + cps - 1 for b in range(B)]
    ret_p = np.stack([res[c]["o_retp"] for c in last])[None]
    conv_p = np.stack([res[c]["o_convp"] for c in last])[None]
    c_p = np.stack([res[c]["o_cp"] for c in last])[None]
    n_p = np.stack([res[c]["o_np"] for c in last])[None]
    m_p = np.stack([res[c]["o_mp"][:, 0] for c in last])[None]
    ret_s = np.concatenate([res[c]["o_rets"] for c in range(NCORES)])[None]
    conv_s = np.concatenate([res[c]["o_convs"].reshape(SD, 3, 2 * RW) for c in range(NCORES)])[None]
    c_s = np.concatenate([res[c]["o_cs"] for c in range(NCORES)])[None]
    n_s = np.concatenate([res[c]["o_ns"] for c in range(NCORES)])[None]
    m_s = np.concatenate([res[c]["o_ms"] for c in range(NCORES)])[None]
    if _debug is not None:
        _debug["mix"] = [np.asarray(res[c]["o_mix"]) for c in range(NCORES)]
    y_prompt = np.zeros((B, SEQ, D), f32)
    y_sample = np.zeros((DB, 1, D), f32)
    for c in range(NCORES):
        b, half = c // cps, c % cps
        yo = np.asarray(res[c]["y_o"])
        y_prompt[b, half * NT:(half + 1) * NT] = yo[:NT]
        y_sample[c * SD:(c + 1) * SD, 0] = yo[NT:]
    return (y_prompt, y_sample, ret_p.astype(f32), conv_p.astype(f32), c_p.astype(f32), n_p.astype(f32), m_p.astype(f32),
            ret_s.astype(f32), conv_s.astype(f32), c_s.astype(f32), n_s.astype(f32), m_s.astype(f32))
```

```python
import math
import os
from contextlib import ExitStack
import numpy as np
import concourse.bass as bass
import concourse.mybir as mybir
from concourse.bass_utils import run_bass_kernel_spmd

F32 = mybir.dt.float32
BF16 = mybir.dt.bfloat16
ALU = mybir.AluOpType
AF = mybir.ActivationFunctionType
AX = mybir.AxisListType

SAME_ENGINE_SYNC = True
NCORES = 8
LN_EPS = 1e-5
LN16 = math.log(16.0)


class Buf:
    __slots__ = ("name", "w", "r")

    def __init__(self, name):
        self.name = name
        self.w = None
        self.r = {}


class Prog:
    ENGS = ("pe", "act", "dve", "pool", "sp")

    def __init__(self, nc, ctx):
        self.nc = nc
        self.ctx = ctx
        self.ops = {e: [] for e in self.ENGS}
        self.cnt = {}
        self.waited = {e: {} for e in self.ENGS}
        self.semh = {}
        self.nbuf = 0
        self.ntile = 0

    def buf(self, name=None):
        self.nbuf += 1
        return Buf(f"{name or 'b'}{self.nbuf}")

    def sb(self, name, shape, dtype, ctx=None):
        self.ntile += 1
        return (ctx or self.ctx).enter_context(self.nc.sbuf_tensor(f"{name}_{self.ntile}", list(shape), dtype))

    def ps(self, name, shape, dtype, ctx=None):
        self.ntile += 1
        return (ctx or self.ctx).enter_context(self.nc.psum_tensor(f"{name}_{self.ntile}", list(shape), dtype))

    def _sem(self, key):
        if key not in self.semh:
            nm = "s" + str(len(self.semh)) + "_" + "".join(ch for ch in str(key) if ch.isalnum())[:24]
            self.semh[key] = self.ctx.enter_context(self.nc.semaphore(nm))
            self.cnt[key] = 0
        return self.semh[key]

    def _collect(self, eng, reads, writes):
        waits = {}

        def need(k, v):
            if k[0] == "d":
                v = max(v, self.cnt[k])
            if k == eng and (eng == "pe" or not SAME_ENGINE_SYNC):
                return
            if self.waited[eng].get(k, 0) < v and waits.get(k, 0) < v:
                waits[k] = v

        for b in reads:
            if b.w is not None:
                need(*b.w)
        for b in writes:
            if b.w is not None:
                need(*b.w)
            for k, v in b.r.items():
                need(k, v)
        for k, v in waits.items():
            self.waited[eng][k] = v
        return list(waits.items())

    def op(self, eng, name, args, kw, reads=(), writes=()):
        self._sem(eng)
        waits = self._collect(eng, reads, writes)
        self.cnt[eng] += 1
        val = self.cnt[eng]
        self.ops[eng].append((waits, name, args, kw, eng, 1))
        for b in reads:
            b.r[eng] = val
        for b in writes:
            b.w = (eng, val)
            b.r = {}
        return val

    NDMASEM = 72

    def dma(self, q, out_ap, in_ap, tag, reads=(), writes=(), **kw):
        key0 = ("dw" if any(b is tag for b in writes) else "dr", tag.name)
        if not hasattr(self, "keymap"):
            self.keymap = {}
        if key0 not in self.keymap:
            assert len(self.keymap) < self.NDMASEM, "too many DMA buffers in one phase"
            self.keymap[key0] = ("d", len(self.keymap))
        key = self.keymap[key0]
        self._sem(key)
        waits = self._collect(q, reads, writes)
        self.cnt[key] += 16
        val = self.cnt[key]
        kw = dict(kw); kw["out"] = out_ap; kw["in_"] = in_ap
        self.ops[q].append((waits, "dma_start", (), kw, key, 16))
        for b in reads:
            b.r[key] = val
        for b in writes:
            b.w = (key, val)
            b.r = {}
        return val

    def barrier(self):
        for e in self.ENGS:
            waits = []
            for k, v in self.cnt.items():
                if k == e or v == 0:
                    continue
                if self.waited[e].get(k, 0) < v:
                    waits.append((k, v))
                    self.waited[e][k] = v
            if waits:
                self.ops[e].append((waits, None, None, None, None, 0))
        self.maxkeys = max(getattr(self, "maxkeys", 0), len(getattr(self, "keymap", {})))
        self.keymap = {}

    def emit(self):
        nc = self.nc
        self.barrier()
        with nc.Block() as block:
            def run(e, name):
                for waits, opn, args, kw, key, inc in self.ops[name]:
                    for k, v in waits:
                        e.wait_ge(self.semh[k], v)
                    if opn is not None:
                        getattr(e, opn)(*args, **kw).then_inc(self.semh[key], inc)

            @block.sync
            def _(e):
                run(e, "sp")

            @block.tensor
            def _(e):
                run(e, "pe")

            @block.scalar
            def _(e):
                run(e, "act")

            @block.vector
            def _(e):
                run(e, "dve")

            @block.gpsimd
            def _(e):
                run(e, "pool")


def pieces(n, m=512):
    k = (n + m - 1) // m
    base = (n + k - 1) // k
    out = []
    c = 0
    while c < n:
        out.append((c, min(n, c + base)))
        c += base
    return out


class Cfg:
    pass


def build(cfg):
    D, KT, H, NT, SD = cfg.D, cfg.KT, cfg.H, cfg.NT, cfg.SD
    NTO = NT + SD
    NCH = NT // 128
    RW = H * 256
    NCT = 8 * RW // 128
    NMT = 2 * RW // 128
    gam = [1.0 - 2.0 ** (-5.0 - h) for h in range(H)]
    NK, PH, PLE, ALPHA = cfg.NK, cfg.PH, cfg.PLE, cfg.ALPHA
    NE = NK * NK
    KTM = 2 * RW // 128
    CBS = min(512, D); NCB = D // CBS
    KG = min(int(os.environ.get("KKG", "4")), KT); NKG = KT // KG
    QW = PH * 256; CQ = min(512, QW); NCQ = QW // CQ
    EB = min(512, NE); NEB = NE // EB; ET = EB // 128
    NI = EB // NK
    PT_ = PLE // 128

    nc = bass.Bass("TRN2", target_bir_lowering=False)

    def din(name, shape, dt=F32):
        return nc.dram_tensor(name, list(shape), dt, kind="ExternalInput").ap()

    def dout(name, shape, dt=F32):
        return nc.dram_tensor(name, list(shape), dt, kind="ExternalOutput").ap()

    def dscr(name, shape, dt=F32):
        return nc.dram_tensor(name, list(shape), dt, kind="Internal").ap()

    xo = din("xo", [NT, D]); xp = din("xp", [NT, D]); xs = din("xs", [SD, D])
    flag_d = din("flag", [128, 1])
    lng_d = din("lng", [128, KT]); lnb_d = din("lnb", [128, KT])
    wm = din("wm", [NCT, 128, KT * 128])
    wg_d = din("wg", [128, KT * 2 * H])
    bg_d = din("bg", [H, 2])
    convw_d = din("convw", [128, NMT * 4]); convb_d = din("convb", [128, NMT])
    gnr_d = din("gnr", [128, 2 * H]); gnm_d = din("gnm", [128, 2 * H])
    tq_d = din("tq", [H, 2, 128, NTO]); tk_d = din("tk", [H, 2, 128, NTO]); tkp_d = din("tkp", [H, 2, 128, NT])
    ident_d = din("ident", [128, 128]); mask01_d = din("mask01", [128, 128]); maskneg_d = din("maskneg", [128, 128])
    sel_d = din("sel", [128, H * 128]); sdmask_d = din("sdmask", [128, SD * SD])
    sret = din("sret", [SD, H, 256, 256]); sconv = din("sconv", [SD * 3, 2 * RW])
    smc = din("smc", [SD, H, 256, 256]); smn = din("smn", [SD, H, 256]); smm = din("smm", [SD, H])

    o_retp = dout("o_retp", [H, 256, 256]); o_convp = dout("o_convp", [3, 2 * RW])
    o_cp = dout("o_cp", [H, 256, 256]); o_np = dout("o_np", [H, 256]); o_mp = dout("o_mp", [H, 1])
    o_rets = dout("o_rets", [SD, H, 256, 256]); o_convs = dout("o_convs", [SD * 3, 2 * RW])
    o_cs = dout("o_cs", [SD, H, 256, 256]); o_ns = dout("o_ns", [SD, H, 256]); o_ms = dout("o_ms", [SD, H])
    o_mix = dout("o_mix", [2 * RW, NTO], BF16)

    wo_d = din("wo", [NCB, 128, KTM * CBS])
    geb_d = din("geb", [128, D]); beb_d = din("beb", [128, D]); g1b_d = din("g1b", [128, D]); b1b_d = din("b1b", [128, D])
    g2b_d = din("g2b", [128, D]); b2b_d = din("b2b", [128, D])
    wq_d = din("wq", [NCQ, NKG, 128, KG * CQ]); skT_d = din("skT", [128, PH * 2 * NK])
    uT_d = din("uT", [NEB, NKG, 128, KG * EB]); v_d = din("vv", [NEB, NCB, 128, ET * CBS])
    wpg_d = din("wpg", [NCB, NKG, 128, KG * CBS]); wpp_d = din("wpp", [NCB, 128, PT_ * CBS])
    po = din("po", [NT, PLE]); ps_d = din("ps", [SD, PLE])
    y_o = dout("y_o", [NTO, D])
    y_d = dscr("y_d", [NTO, D]); x1_d = dscr("x1_d", [NTO, D])
    bstate_r = dscr("bstate_r", [H, 128, 512])
    bstate_c = dscr("bstate_c", [H, 128, 2 * 257])

    with ExitStack() as ctx:
        P = Prog(nc, ctx)

        def V(name, *a, r=(), w=(), **kw):
            P.op("dve", name, a, kw, r, w)

        def A(name, *a, r=(), w=(), **kw):
            P.op("act", name, a, kw, r, w)

        def G(name, *a, r=(), w=(), **kw):
            P.op("pool", name, a, kw, r, w)

        def T(name, *a, r=(), w=(), **kw):
            P.op("pe", name, a, kw, r, w)

        def load(dst, src, b, q="sp", **kw):
            P.dma(q, dst, src, b, writes=[b], **kw)

        def const(name, shape, src, dt=F32):
            t = P.sb(name, shape, dt); b = P.buf(name)
            load(t[:], src, b)
            return t, b

        ident, b_ident = const("ident", [128, 128], ident_d)
        mask01, b_m01 = const("mask01", [128, 128], mask01_d)
        maskneg, b_mneg = const("maskneg", [128, 128], maskneg_d)
        flag, b_flag = const("flag", [128, 1], flag_d)
        lng, b_lng = const("lng", [128, KT], lng_d)
        lnb, b_lnb = const("lnb", [128, KT], lnb_d)
        sel, b_sel = const("sel", [128, H * 128], sel_d)
        sdmask, b_sdm = const("sdmask", [128, SD * SD], sdmask_d)
        bg, b_bg = const("bg", [H, 2], bg_d)
        convw, b_cw = const("convw", [128, NMT * 4], convw_d)
        convb, b_cb = const("convb", [128, NMT], convb_d)
        gnr, b_gnr = const("gnr", [128, 2 * H], gnr_d)
        gnm, b_gnm = const("gnm", [128, 2 * H], gnm_d)
        identb = P.sb("identb", [128, 128], BF16); b_identb = P.buf("identb")
        V("tensor_copy", identb[:], ident[:], r=[b_ident], w=[b_identb])
        mhalf = P.sb("mhalf", [128, 1], F32); b_mhalf = P.buf("mhalf")
        V("memset", mhalf[:], -0.5, w=[b_mhalf])
        onesrow = P.sb("onesrow", [H, 128], F32); b_onesrow = P.buf("onesrow")
        V("memset", onesrow[:], 1.0, w=[b_onesrow])
        halo = P.sb("halo", [128, NMT * 3], F32); b_halo = P.buf("halo")
        mbound = P.sb("mbound", [H, 1], F32); b_mbound = P.buf("mbound")
        bufT = P.sb("bufT", [128, NMT, SD * 3], F32); b_bufT = P.buf("bufT")
        XT = P.sb("XT", [128, KT, NTO], BF16)
        b_XT = [P.buf("XT") for _ in range(NCH + 1)]

        def rstd_op(out_ap, var_ap, n, rb, wb):
            G("tensor_scalar", out_ap, var_ap, LN_EPS, None, ALU.add, r=rb, w=wb)
            G("tensor_tensor", out_ap, out_ap, mhalf[:n, :], ALU.pow, r=list(wb) + [b_mhalf], w=wb)

        def phase_ln(tiles):
            with ExitStack() as ph:
                xt = [P.sb("p1x", [128, D], F32, ph) for _ in range(2)]; bx = [P.buf("p1x") for _ in range(2)]
                pcs = pieces(D)
                nst = len(pcs)
                st = [P.sb("p1st", [128, nst * 6 + 8], F32, ph) for _ in range(2)]; bst = [P.buf("p1st") for _ in range(2)]
                pt = [P.ps("p1pt", [128, 4, 128], F32, ph) for _ in range(2)]; bpt = [P.buf("p1pt") for _ in range(2)]
                gi = 0
                for ti, (src, n, col0, xb) in enumerate(tiles):
                    x, b = xt[ti % 2], bx[ti % 2]
                    s, bs = st[ti % 2], bst[ti % 2]
                    load(x[:n, :], src, b)
                    for c, (c0, c1) in enumerate(pcs):
                        V("bn_stats", s[:n, c * 6:(c + 1) * 6], x[:n, c0:c1], r=[b], w=[bs])
                    o = nst * 6
                    V("bn_aggr", s[:n, o:o + 2], s[:n, 0:o], r=[bs], w=[bs])
                    rstd_op(s[:n, o + 2:o + 3], s[:n, o + 1:o + 2], n, [bs], [bs])
                    V("scalar_tensor_tensor", s[:n, o + 3:o + 4], s[:n, o:o + 1], -1.0, s[:n, o + 2:o + 3], ALU.mult, ALU.mult, r=[bs], w=[bs])
                    A("activation", x[:n, :], x[:n, :], AF.Identity, bias=s[:n, o + 3:o + 4], scale=s[:n, o + 2:o + 3], r=[b, bs], w=[b])
                    for k0 in range(0, KT, 4):
                        p_, bp = pt[gi % 2], bpt[gi % 2]; gi += 1
                        nk = min(4, KT - k0)
                        for j in range(nk):
                            kt = k0 + j
                            T("transpose", p_[:, j, :n], x[:n, kt * 128:(kt + 1) * 128], ident[:n, :n], r=[b, b_ident], w=[bp])
                        for j in range(nk):
                            kt = k0 + j
                            if j % 2 == 0:
                                V("tensor_scalar", XT[:, kt, col0:col0 + n], p_[:, j, :n], lng[:, kt:kt + 1], lnb[:, kt:kt + 1], ALU.mult, ALU.add,
                                  r=[bp, b_lng, b_lnb], w=[xb])
                            else:
                                A("activation", XT[:, kt, col0:col0 + n], p_[:, j, :n], AF.Identity, bias=lnb[:, kt:kt + 1], scale=lng[:, kt:kt + 1],
                                  r=[bp, b_lng, b_lnb], w=[xb])
            P.barrier()

        def phase_convbuf():
            if SD == 0:
                return
            with ExitStack() as ph:
                cvs = P.sb("cvs", [SD * 3, 2 * RW], F32, ph); bcvs = P.buf("cvs")
                pc = P.ps("pcv", [128, 512], F32, ph); bpc = P.buf("pcv")
                load(cvs[:, :], sconv, bcvs)
                for mti in range(NMT):
                    T("transpose", pc[:, 0:SD * 3], cvs[:, mti * 128:(mti + 1) * 128], ident[:SD * 3, :SD * 3], r=[bcvs, b_ident], w=[bpc])
                    A("activation", bufT[:, mti, :], pc[:, 0:SD * 3], AF.Identity, r=[bpc], w=[b_bufT])
                o3 = o_convs.rearrange("(s j) f -> s j f", j=3)
                c3 = cvs[:, :]
                for s in range(SD):
                    P.dma("sp", o_convs[s * 3:s * 3 + 2, :], cvs[s * 3 + 1:s * 3 + 3, :], bcvs, reads=[bcvs])
            P.barrier()

        def mixer_phase(mode):
            pre = mode == "pre"
            NC_ = NT if pre else NTO
            SDm = max(SD, 1)
            with ExitStack() as ph:
                NW = 3
                wt = [P.sb("wt", [128, KT, 128], BF16, ph) for _ in range(NW)]; bw = [P.buf("wt") for _ in range(NW)]
                pz = [P.ps("pz", [128, 512], F32, ph) for _ in range(2)]; bpz = [P.buf("pz") for _ in range(2)]
                pzst = {"i": 0}
                psc = P.ps("psc", [128, 4, 128], F32, ph); bsc = [P.buf("psc")] * 4
                pnd = P.ps("pnd", [128, 512], F32, ph); bnd = P.buf("pnd")
                pU = [P.ps("pU", [128, 512], F32, ph) for _ in range(2)]; bU = [P.buf("pU") for _ in range(2)]
                ptr = P.ps("ptr", [128, 1024], BF16, ph); btr = [P.buf("ptr")] * 4
                pms = P.ps("pms", [128, 512], F32, ph); bms = P.buf("pms")
                F = [P.sb("F", [128, NC_], BF16, ph) for _ in range(8)]; bF = [P.buf("F") for _ in range(8)]
                RA = P.sb("RA", [128, 3 + NC_], F32, ph); bRA = P.buf("RA")
                RB = P.sb("RB", [128, 3 + NC_], F32, ph); bRB = P.buf("RB")
                tmp = [P.sb("tmp", [128, 512], F32, ph) for _ in range(2)]; btmp = [P.buf("tmp") for _ in range(2)]
                tq = P.sb("tq", [128, 2, NC_], F32, ph); btq = P.buf("tq")
                tk = P.sb("tk", [128, 2, NC_], F32, ph); btk = P.buf("tk")
                RAW = [RA, RB]; bRAW = [bRA, bRB]
                CACC = tq[:, 0, :]; bCACC = btq
                PTt = P.sb("PT", [128, 128], BF16, ph); bPT = P.buf("PT")
                Et = P.sb("Et", [128, 128], F32, ph); bE = P.buf("Et")
                QP = P.sb("QP", [128, 2, 128], BF16, ph); bQP = P.buf("QP")
                vc = P.sb("vc", [128, 260], BF16, ph); bvc = P.buf("vc")
                kc = P.sb("kc", [128, 256], BF16, ph); bkc = P.buf("kc")
                on = P.sb("on", [128, 256], BF16, ph); bon = P.buf("on")
                hst = P.sb("hst", [128, 16], F32, ph); bhst = P.buf("hst")
                S32 = P.sb("S32", [128, 2, 257], F32, ph); bS32 = P.buf("S32")
                Sbf = P.sb("Sbf", [128, 2, 257], BF16, ph); bSbf = P.buf("Sbf")
                mixT = P.sb("mixT", [128, 2, NC_], BF16, ph); bmix = P.buf("mixT")
                SS = [P.sb("SS", [128, 2, 257], F32, ph) for _ in range(3)]; bSS = [P.buf("SS") for _ in range(3)]
                ksd = P.sb("ksd", [SDm, 256], BF16, ph); bksd = P.buf("ksd")
                kms = P.sb("kms", [SDm, 256], BF16, ph); bkms = P.buf("kms")
                vsd = P.sb("vsd", [SDm, 260], BF16, ph); bvsd = P.buf("vsd")
                qsd = P.sb("qsd", [128, 2, SDm], F32, ph); bqsd = P.buf("qsd")
                qms = P.sb("qms", [128, 2, SDm * SDm], F32, ph); bqms = P.buf("qms")
                nS = P.sb("nS", [128, 2, SDm], F32, ph); bnS = P.buf("nS")
                crw = [P.sb("crw", [SDm, 128], F32, ph) for _ in range(2)]; bcrw = [P.buf("crw") for _ in range(2)]
                crst = {"i": 0}
                NG = NC_
                grow = {nm: P.sb("g_" + nm, [128, NG], F32, ph) for nm in ("ig", "lf", "negM", "wp")}
                bgrow = {nm: P.buf("g_" + nm) for nm in grow}
                gt1 = P.sb("gt1", [H, 512], F32, ph); bgt1 = P.buf("gt1")
                gt2 = P.sb("gt2", [H, 512], F32, ph); bgt2 = P.buf("gt2")
                colA = P.sb("colA", [128, NCH, 3 * H], F32, ph); bcolA = P.buf("colA")
                wcrow = P.sb("wcrow", [128, NCH + 1], F32, ph); bwcrow = P.buf("wcrow")
                mrow = P.sb("mrow", [H, NCH + 2], F32, ph); bmrow = P.buf("mrow")
                wcbc = P.sb("wcbc", [128, H, NCH + 1], F32, ph); bwcbc = P.buf("wcbc")
                wgt = P.sb("wgt", [128, KT * 2 * H], BF16, ph); bwgt = P.buf("wgt")
                srow = {nm: P.sb("s_" + nm, [128, SDm], F32, ph) for nm in ("m", "mt", "w16", "wp", "emt", "t")}
                bsrow = P.buf("srow")
                scol = P.sb("scol", [SDm, 2 * H], F32, ph); bscol = P.buf("scol")
                wpbcS = P.sb("wpbcS", [128, H, SDm], F32, ph); bwpbcS = P.buf("wpbcS")
                convp = P.sb("convp", [128, 3, NMT], F32, ph); bconvp = P.buf("convp")

                xt_bufs = b_XT[:NCH] if pre else b_XT

                def head_cts(kind, h):
                    if kind == "ret":
                        bq, bk, bv, bgg = 2 * h, RW // 128 + 2 * h, 2 * RW // 128 + 2 * h, 3 * RW // 128 + 2 * h
                    else:
                        b0 = 4 * RW // 128
                        bq, bk, bv, bgg = b0 + 2 * h, b0 + RW // 128 + 2 * h, b0 + 2 * RW // 128 + 2 * h, b0 + 3 * RW // 128 + 2 * h
                    return bq, bk, bv, bgg

                order = []
                for h in range(H):
                    for kind in ("ret", "ml"):
                        bq, bk, bv, bgg = head_cts(kind, h)
                        if pre:
                            if kind == "ml":
                                order += [bq, bq + 1]
                            order += [bk, bk + 1, bv, bv + 1]
                        else:
                            order += [bq, bq + 1, bk, bk + 1, bv, bv + 1, bgg, bgg + 1]
                wst = {"issued": 0, "pos": 0}

                def wget(ct):
                    pos = wst["pos"]
                    assert order[pos] == ct, (order[pos], ct, pos)
                    while wst["issued"] <= min(pos + 2, len(order) - 1):
                        i = wst["issued"]
                        load(wt[i % NW][:].rearrange("p k c -> p (k c)"), wm[order[i]], bw[i % NW], q="pool")
                        wst["issued"] += 1
                    wst["pos"] += 1
                    return wt[pos % NW], bw[pos % NW]

                def ztile(ct, evac):
                    wtile, bwt = wget(ct)
                    for (c0, c1) in pieces(NC_):
                        i = pzst["i"] % 2; pzst["i"] += 1
                        p_, bp = pz[i], bpz[i]
                        n = c1 - c0
                        for kt in range(KT):
                            T("matmul", p_[:, :n], wtile[:, kt, :], XT[:, kt, c0:c1], start=(kt == 0), stop=(kt == KT - 1), r=[bwt] + xt_bufs, w=[bp])
                        evac(p_[:, :n], bp, c0, c1)

                def rope_pair(ct0, tab, btab, dst1, bd1, dst2, bd2):
                    def ev1(ps, bp, c0, c1):
                        V("tensor_tensor", RA[:, c0:c1], ps, tab[:, 0, c0:c1], ALU.mult, r=[bp, btab], w=[bRA])
                        V("tensor_tensor", RB[:, c0:c1], ps, tab[:, 1, c0:c1], ALU.mult, r=[bp, btab], w=[bRB])

                    def ev2(ps, bp, c0, c1):
                        n = c1 - c0
                        V("tensor_tensor", tmp[0][:, :n], ps, tab[:, 1, c0:c1], ALU.mult, r=[bp, btab], w=[btmp[0]])
                        V("tensor_tensor", tmp[1][:, :n], ps, tab[:, 0, c0:c1], ALU.mult, r=[bp, btab], w=[btmp[1]])
                        G("tensor_tensor", dst1[:, c0:c1], RA[:, c0:c1], tmp[0][:, :n], ALU.subtract, r=[bRA, btmp[0]], w=[bd1])
                        G("tensor_tensor", dst2[:, c0:c1], RB[:, c0:c1], tmp[1][:, :n], ALU.add, r=[bRB, btmp[1]], w=[bd2])
                    ztile(ct0, ev1)
                    ztile(ct0 + 1, ev2)

                def copy_tile(ct, dst, bd, func=AF.Identity):
                    def ev(ps, bp, c0, c1):
                        A("activation", dst[:, c0:c1], ps, func, r=[bp], w=[bd])
                    ztile(ct, ev)

                def tr_to_tokens(srcs, bsrcs, c0, n, slot, dst, bdst, scale=1.0, bscale=()):
                    for hh in range(2):
                        T("transpose", ptr[:n, slot * 256 + hh * 128: slot * 256 + (hh + 1) * 128], srcs[hh][:, c0:c0 + n], identb[:, :],
                          r=[bsrcs[hh], b_identb], w=[btr[slot]])
                    A("activation", dst[:n, 0:256], ptr[:n, slot * 256:(slot + 1) * 256], AF.Identity, scale=scale, r=[btr[slot]] + list(bscale), w=[bdst])

                def head_out(n, nd_ap, gcol, bg_, sg, bsg, c0, gi, r_ap=None):
                    V("bn_stats", hst[:n, 0:6], nd_ap, r=[bnd], w=[bhst])
                    V("bn_aggr", hst[:n, 6:8], hst[:n, 0:6], r=[bhst], w=[bhst])
                    if r_ap is not None:
                        V("tensor_tensor", hst[:n, 6:7], hst[:n, 6:7], r_ap, ALU.mult, r=[bhst], w=[bhst])
                        V("tensor_tensor", hst[:n, 7:8], hst[:n, 7:8], r_ap, ALU.mult, r=[bhst], w=[bhst])
                        V("tensor_tensor", hst[:n, 7:8], hst[:n, 7:8], r_ap, ALU.mult, r=[bhst], w=[bhst])
                    rstd_op(hst[:n, 8:9], hst[:n, 7:8], n, [bhst], [bhst])
                    V("scalar_tensor_tensor", hst[:n, 9:10], hst[:n, 6:7], -1.0, hst[:n, 8:9], ALU.mult, ALU.mult, r=[bhst], w=[bhst])
                    if r_ap is not None:
                        V("tensor_tensor", hst[:n, 8:9], hst[:n, 8:9], r_ap, ALU.mult, r=[bhst], w=[bhst])
                    A("activation", on[:n, :], nd_ap, AF.Identity, bias=hst[:n, 9:10], scale=hst[:n, 8:9], r=[bnd, bhst], w=[bon])
                    for hh in range(2):
                        T("transpose", ptr[:, 512 + hh * 128:512 + hh * 128 + n], on[:n, hh * 128:(hh + 1) * 128], identb[:n, :n], r=[bon, b_identb], w=[btr[2]])
                    for hh in range(2):
                        V("scalar_tensor_tensor", mixT[:, hh, c0:c0 + n], ptr[:, 512 + hh * 128:512 + hh * 128 + n], gcol[:, 2 * gi + hh:2 * gi + hh + 1],
                          sg[hh][:, c0:c0 + n], ALU.mult, ALU.mult, r=[btr[2], bg_, bsg[hh]], w=[bmix])

                def gate_stage():
                    ig, lf, negM, wpr = (grow[k] for k in ("ig", "lf", "negM", "wp"))
                    big, blf, bnegM, bwp = (bgrow[k] for k in ("ig", "lf", "negM", "wp"))
                    for k in grow:
                        G("memset", grow[k][:, :], 0.0, w=[bgrow[k]])
                    G("memset", wcrow[:, :], 0.0, w=[bwcrow])
                    load(wgt[:], wg_d, bwgt, q="pool")
                    wg3 = wgt[:].rearrange("p (k g) -> p k g", g=2 * H)
                    for (c0, c1) in pieces(NG):
                        n = c1 - c0
                        for gi, (dst, bd) in enumerate(((ig, big), (lf, blf))):
                            for kt in range(KT):
                                T("matmul", pms[:H, :n], wg3[:, kt, gi * H:(gi + 1) * H], XT[:, kt, c0:c1], start=(kt == 0), stop=(kt == KT - 1),
                                  r=[bwgt] + xt_bufs, w=[bms])
                            A("activation", dst[:H, c0:c1], pms[:H, :n], AF.Identity, bias=bg[:, gi:gi + 1], r=[bms, b_bg], w=[bd])
                        V("scalar_tensor_tensor", gt1[:, :n], lf[:H, c0:c1], -1.0, lf[:H, c0:c1], ALU.mult, ALU.min, r=[blf], w=[bgt1])
                        A("activation", gt1[:, :n], gt1[:, :n], AF.Exp, r=[bgt1], w=[bgt1])
                        A("activation", gt1[:, :n], gt1[:, :n], AF.Ln, bias=1.0, r=[bgt1], w=[bgt1])
                        V("tensor_scalar_min", lf[:H, c0:c1], lf[:H, c0:c1], 0.0, r=[blf], w=[blf])
                        V("tensor_tensor", lf[:H, c0:c1], lf[:H, c0:c1], gt1[:, :n], ALU.subtract, r=[blf, bgt1], w=[blf])
                    if pre:
                        V("memset", mrow[:, 0:1], 0.0, w=[bmrow])
                    else:
                        V("tensor_copy", mrow[:, 0:1], mbound[:, :], r=[b_mbound], w=[bmrow])
                    t1 = gt1[:, 0:128]; t2 = gt2[:, 0:128]; t3 = gt2[:, 128:256]
                    for c in range(NCH):
                        sl = slice(c * 128, (c + 1) * 128)
                        last = slice(c * 128 + 127, c * 128 + 128)
                        V("tensor_tensor_scan", t1, onesrow[:, :], lf[:H, sl], 0.0, ALU.mult, ALU.add, r=[blf, b_onesrow], w=[bgt1])
                        V("tensor_tensor", t2, ig[:H, sl], t1, ALU.subtract, r=[big, bgt1], w=[bgt2])
                        V("tensor_tensor_scan", negM[:H, sl], onesrow[:, :], t2, mrow[:, c:c + 1], ALU.mult, ALU.max, r=[bgt2, b_onesrow, bmrow], w=[bnegM])
                        V("tensor_tensor", mrow[:, c + 1:c + 2], gt1[:, 127:128], negM[:H, last], ALU.add, r=[bgt1, bnegM], w=[bmrow])
                        V("tensor_tensor", t1, t1, negM[:H, sl], ALU.add, r=[bgt1, bnegM], w=[bgt1])
                        A("activation", t1, t1, AF.Exp, scale=-1.0, r=[bgt1], w=[bgt1])
                        V("tensor_scalar", wcrow[:H, NCH:NCH + 1], negM[:H, last], -1.0, -LN16, ALU.mult, ALU.add, r=[bnegM], w=[bwcrow])
                        V("tensor_scalar_mul", negM[:H, sl], negM[:H, sl], -1.0, r=[bnegM], w=[bnegM])
                        A("activation", wpr[:H, sl], negM[:H, sl], AF.Exp, bias=mrow[:, c:c + 1], r=[bnegM, bmrow], w=[bwp])
                        V("tensor_copy", wcrow[:H, c:c + 1], wpr[:H, last], r=[bwp], w=[bwcrow])
                        A("activation", t3, t2, AF.Exp, bias=wcrow[:H, NCH:NCH + 1], r=[bgt2, bwcrow], w=[bgt2])
                        T("transpose", pms[:, 0:H], t2, ident[:H, :H], r=[bgt2, b_ident], w=[bms])
                        T("transpose", pms[:, H:2 * H], t3, ident[:H, :H], r=[bgt2, b_ident], w=[bms])
                        T("transpose", pms[:, 2 * H:3 * H], t1, ident[:H, :H], r=[bgt1, b_ident], w=[bms])
                        A("activation", colA[:, c, :], pms[:, 0:3 * H], AF.Identity, r=[bms], w=[bcolA])
                    for h in range(H):
                        T("matmul", pms[:, 0:NCH], sel[:, h * 128:(h + 1) * 128], wcrow[:, 0:NCH], start=True, stop=True, r=[b_sel, bwcrow], w=[bms])
                        A("activation", wcbc[:, h, 0:NCH], pms[:, 0:NCH], AF.Identity, r=[bms], w=[bwcbc])
                    if pre:
                        V("tensor_scalar_mul", mbound[:, :], mrow[:, NCH:NCH + 1], flag[:H, :], r=[bmrow, b_flag], w=[b_mbound])
                    elif SD > 0:
                        sm, smt, sw16, swp, semt, stt = (srow[k] for k in ("m", "mt", "w16", "wp", "emt", "t"))
                        for k in srow:
                            G("memset", srow[k][:, :], 0.0, w=[bsrow])
                        load(sm[:H, :], smm.rearrange("s h -> h s"), bsrow, allow_slow_non_contiguous=True)
                        cs = slice(NT, NTO)
                        V("tensor_tensor", stt[:H, :], lf[:H, cs], sm[:H, :], ALU.add, r=[blf, bsrow], w=[bsrow])
                        V("tensor_tensor", smt[:H, :], stt[:H, :], ig[:H, cs], ALU.max, r=[big, bsrow], w=[bsrow])
                        V("tensor_tensor", swp[:H, :], stt[:H, :], smt[:H, :], ALU.subtract, r=[bsrow], w=[bsrow])
                        A("activation", swp[:H, :], swp[:H, :], AF.Exp, r=[bsrow], w=[bsrow])
                        V("tensor_tensor", sw16[:H, :], ig[:H, cs], smt[:H, :], ALU.subtract, r=[big, bsrow], w=[bsrow])
                        V("tensor_scalar_add", sw16[:H, :], sw16[:H, :], -LN16, r=[bsrow], w=[bsrow])
                        A("activation", sw16[:H, :], sw16[:H, :], AF.Exp, r=[bsrow], w=[bsrow])
                        A("activation", semt[:H, :], smt[:H, :], AF.Exp, scale=-1.0, r=[bsrow], w=[bsrow])
                        P.dma("sp", o_ms.rearrange("s h -> h s"), smt[:H, :], bsrow, reads=[bsrow], allow_slow_non_contiguous=True)
                        T("transpose", pms[:SD, 0:H], sw16[:H, :], ident[:H, :H], r=[bsrow, b_ident], w=[bms])
                        T("transpose", pms[:SD, H:2 * H], semt[:H, :], ident[:H, :H], r=[bsrow, b_ident], w=[bms])
                        A("activation", scol[:SD, :], pms[:SD, 0:2 * H], AF.Identity, r=[bms], w=[bscol])
                        for h in range(H):
                            T("matmul", pms[:, 0:SD], sel[:, h * 128:(h + 1) * 128], swp[:, :], start=True, stop=True, r=[b_sel, bsrow], w=[bms])
                            A("activation", wpbcS[:, h, :], pms[:, 0:SD], AF.Identity, r=[bms], w=[bwpbcS])

                def run_head(kind, h):
                    ret = kind == "ret"
                    base_q, base_k, base_v, base_g = head_cts(kind, h)
                    q1, q2, k1, k2, v1, v2, g1, g2 = F
                    bq1, bq2, bk1, bk2, bv1, bv2, bg1, bg2 = bF
                    if ret:
                        if pre:
                            load(tk[:, :, :], tkp_d[h].rearrange("a p n -> p a n"), btk)
                        else:
                            load(tq[:, :, :], tq_d[h].rearrange("a p n -> p a n"), btq)
                            load(tk[:, :, :], tk_d[h].rearrange("a p n -> p a n"), btk)
                            rope_pair(base_q, tq, btq, q1, bq1, q2, bq2)
                        rope_pair(base_k, tk, btk, k1, bk1, k2, bk2)
                    else:
                        def conv_tile(ct, which, dst, bd):
                            mti = ct - 4 * RW // 128
                            R, bR = RAW[which % 2], bRAW[which % 2]

                            def ev(ps, bp, c0, c1):
                                A("activation", R[:, 3 + c0:3 + c1], ps, AF.Identity, r=[bp], w=[bR])
                            ztile(ct, ev)
                            if pre:
                                V("memset", R[:, 0:3], 0.0, w=[bR])
                                V("tensor_scalar_mul", halo[:, mti * 3:mti * 3 + 3], R[:, NT:NT + 3], flag[:, :], r=[bR, b_flag], w=[b_halo])
                            else:
                                V("tensor_copy", R[:, 0:3], halo[:, mti * 3:mti * 3 + 3], r=[b_halo], w=[bR])
                                G("tensor_copy", convp[:, :, mti], R[:, NT:NT + 3], r=[bR], w=[bconvp])
                            w4 = convw[:, mti * 4:mti * 4 + 4]
                            V("tensor_scalar", CACC[:, 0:NT], R[:, 3:3 + NT], w4[:, 3:4], convb[:, mti:mti + 1], ALU.mult, ALU.add, r=[bR, b_cw, b_cb], w=[bCACC])
                            for j in (2, 1, 0):
                                V("scalar_tensor_tensor", CACC[:, 0:NT], R[:, j:j + NT], w4[:, j:j + 1], CACC[:, 0:NT], ALU.mult, ALU.add, r=[bR, b_cw, bCACC], w=[bCACC])
                            if not pre and SD > 0:
                                V("tensor_scalar", CACC[:, NT:NTO], R[:, 3 + NT:3 + NTO], w4[:, 3:4], convb[:, mti:mti + 1], ALU.mult, ALU.add, r=[bR, b_cw, b_cb], w=[bCACC])
                                bt3 = bufT[:, mti, :].rearrange("p (s j) -> p s j", j=3)
                                for j in range(3):
                                    V("scalar_tensor_tensor", CACC[:, NT:NTO], bt3[:, :, j], w4[:, j:j + 1], CACC[:, NT:NTO], ALU.mult, ALU.add, r=[b_bufT, b_cw, bCACC], w=[bCACC])
                                i = crst["i"] % 2; crst["i"] += 1
                                T("transpose", pms[:SD, 0:128], R[:, 3 + NT:3 + NTO], ident[:, :], r=[bR, b_ident], w=[bms])
                                A("activation", crw[i][:SD, :], pms[:SD, 0:128], AF.Identity, r=[bms], w=[bcrw[i]])
                                o3 = o_convs.rearrange("(s j) f -> s j f", j=3)
                                P.dma("sp", o3[:, 2, mti * 128:(mti + 1) * 128], crw[i][:SD, :], bcrw[i], reads=[bcrw[i]])
                            A("activation", dst[:, :], CACC[:, 0:NC_], AF.Silu, r=[bCACC], w=[bd])
                        if not pre:
                            conv_tile(base_q, 0, q1, bq1)
                            conv_tile(base_q + 1, 1, q2, bq2)
                        else:
                            for ct in (base_q, base_q + 1):
                                mti = ct - 4 * RW // 128
                                wtile, bwt = wget(ct)
                                i = pzst["i"] % 2; pzst["i"] += 1
                                for kt in range(KT):
                                    T("matmul", pz[i][:, 0:3], wtile[:, kt, :], XT[:, kt, NT - 3:NT], start=(kt == 0), stop=(kt == KT - 1), r=[bwt] + xt_bufs, w=[bpz[i]])
                                V("tensor_scalar_mul", halo[:, mti * 3:mti * 3 + 3], pz[i][:, 0:3], flag[:, :], r=[bpz[i], b_flag], w=[b_halo])
                        conv_tile(base_k, 0, k1, bk1)
                        conv_tile(base_k + 1, 1, k2, bk2)
                    copy_tile(base_v, v1, bv1)
                    copy_tile(base_v + 1, v2, bv2)
                    if not pre:
                        gf = AF.Silu if ret else AF.Sigmoid
                        copy_tile(base_g, g1, bg1, gf)
                        copy_tile(base_g + 1, g2, bg2, gf)

                    W = 256 if ret else 257
                    if pre:
                        V("memset", S32[:, :, :], 0.0, w=[bS32])
                        V("memset", Sbf[:, :, :], 0.0, w=[bSbf])
                    else:
                        src = (bstate_r[h].rearrange("p (t e) -> p t e", t=2) if ret else bstate_c[h].rearrange("p (t e) -> p t e", t=2))
                        load(S32[:, :, 0:W], src, bS32)
                        A("activation", Sbf[:, :, 0:W], S32[:, :, 0:W], AF.Identity, r=[bS32], w=[bSbf])
                    if not ret:
                        V("memset", vc[:, 256:257], 1.0, w=[bvc])
                    g128 = gam[h] ** 128
                    selh = sel[:, h * 128:(h + 1) * 128]

                    for c in range(NCH):
                        c0 = c * 128
                        cs_ = slice(c0, c0 + 128)
                        if not pre:
                            T("matmul", psc[:, 0, :], k1[:, cs_], q1[:, cs_], start=True, stop=False, r=[bk1, bq1], w=[bsc[0]])
                            T("matmul", psc[:, 0, :], k2[:, cs_], q2[:, cs_], start=False, stop=True, r=[bk2, bq2], w=[bsc[0]])
                            if ret:
                                V("tensor_tensor", PTt[:, :], psc[:, 0, :], mask01[:, :], ALU.mult, r=[bsc[0], b_m01], w=[bPT])
                            else:
                                T("matmul", psc[:, 1, :], selh, grow["negM"][:, cs_], start=True, stop=False, r=[b_sel, bgrow["negM"]], w=[bsc[1]])
                                T("matmul", psc[:, 1, :], ident[:, :], maskneg[:, :], start=False, stop=True, r=[b_ident, b_mneg], w=[bsc[1]])
                                V("tensor_scalar", Et[:, :], psc[:, 1, :], colA[:, c, h:h + 1], 0.0, ALU.add, ALU.min, r=[bsc[1], bcolA], w=[bE])
                                A("activation", Et[:, :], Et[:, :], AF.Exp, r=[bE], w=[bE])
                                V("scalar_tensor_tensor", PTt[:, :], psc[:, 0, :], 1.0 / 16.0, Et[:, :], ALU.mult, ALU.mult, r=[bsc[0], bE], w=[bPT])
                                T("matmul", psc[:, 2, :], selh, grow["wp"][:, cs_], start=True, stop=True, r=[b_sel, bgrow["wp"]], w=[bsc[2]])
                                V("tensor_tensor", QP[:, 0, :], q1[:, cs_], psc[:, 2, :], ALU.mult, r=[bq1, bsc[2]], w=[bQP])
                                V("tensor_tensor", QP[:, 1, :], q2[:, cs_], psc[:, 2, :], ALU.mult, r=[bq2, bsc[2]], w=[bQP])
                        tr_to_tokens((v1, v2), (bv1, bv2), c0, 128, 0, vc, bvc)
                        if ret:
                            tr_to_tokens((k1, k2), (bk1, bk2), c0, 128, 1, kc, bkc, scale=g128)
                        else:
                            tr_to_tokens((k1, k2), (bk1, bk2), c0, 128, 1, kc, bkc, scale=colA[:, c, H + h:H + h + 1], bscale=[bcolA])
                        if not pre:
                            T("matmul", pnd[:, 0:W], PTt[:, :], vc[:, 0:W], start=True, stop=False, r=[bPT, bvc], w=[bnd])
                            if ret:
                                qa = (q1[:, cs_], q2[:, cs_]); bqa = (bq1, bq2)
                            else:
                                qa = (QP[:, 0, :], QP[:, 1, :]); bqa = (bQP, bQP)
                            for hh in range(2):
                                T("matmul", pnd[:, 0:W], qa[hh], Sbf[:, hh, 0:W], start=False, stop=(hh == 1), r=[bqa[hh], bSbf], w=[bnd])
                            if ret:
                                head_out(128, pnd[:, 0:256], gnr, b_gnr, (g1, g2), (bg1, bg2), c0, h)
                            else:
                                A("activation", hst[:, 10:11], pnd[:, 256:257], AF.Abs, r=[bnd], w=[bhst])
                                V("tensor_tensor", hst[:, 10:11], hst[:, 10:11], colA[:, c, 2 * H + h:2 * H + h + 1], ALU.max, r=[bhst, bcolA], w=[bhst])
                                V("reciprocal", hst[:, 11:12], hst[:, 10:11], r=[bhst], w=[bhst])
                                head_out(128, pnd[:, 0:256], gnm, b_gnm, (g1, g2), (bg1, bg2), c0, h, r_ap=hst[:, 11:12])
                        for hh in range(2):
                            T("matmul", pU[hh][:, 0:W], kc[:, hh * 128:(hh + 1) * 128], vc[:, 0:W], start=True, stop=True, r=[bkc, bvc], w=[bU[hh]])
                        for hh in range(2):
                            sc_ = g128 if ret else wcbc[:, h, c:c + 1]
                            V("scalar_tensor_tensor", S32[:, hh, 0:W], S32[:, hh, 0:W], sc_, pU[hh][:, 0:W], ALU.mult, ALU.add, r=[bS32, bU[hh], bwcbc], w=[bS32])
                        if c < NCH - 1 and not pre:
                            A("activation", Sbf[:, :, 0:W], S32[:, :, 0:W], AF.Identity, r=[bS32], w=[bSbf])
                    if pre:
                        V("tensor_scalar_mul", S32[:, :, 0:W], S32[:, :, 0:W], flag[:, :], r=[bS32, b_flag], w=[bS32])
                        dst = (bstate_r[h].rearrange("p (t e) -> p t e", t=2) if ret else bstate_c[h].rearrange("p (t e) -> p t e", t=2))
                        P.dma("sp", dst, S32[:, :, 0:W], bS32, reads=[bS32])
                        return
                    if ret:
                        P.dma("sp", o_retp[h].rearrange("(t p) e -> p t e", p=128), S32[:, :, 0:256], bS32, reads=[bS32])
                    else:
                        P.dma("sp", o_cp[h].rearrange("(t p) e -> p t e", p=128), S32[:, :, 0:256], bS32, reads=[bS32])
                        for t_ in range(2):
                            P.dma("sp", o_np[h:h + 1, t_ * 128:(t_ + 1) * 128].rearrange("o p -> p o"), S32[:, t_, 256:257], bS32, reads=[bS32], allow_slow_non_contiguous=True)
                    if SD > 0:
                        cs0 = NT
                        tr_to_tokens((v1, v2), (bv1, bv2), cs0, SD, 0, vsd, bvsd)
                        if ret:
                            tr_to_tokens((k1, k2), (bk1, bk2), cs0, SD, 1, ksd, bksd)
                        else:
                            V("memset", vsd[:SD, 256:257], 1.0, w=[bvsd])
                            tr_to_tokens((k1, k2), (bk1, bk2), cs0, SD, 1, ksd, bksd, scale=scol[:SD, h:h + 1], bscale=[bscol])
                            for t_ in range(2):
                                load(nS[:, t_, :], smn[:, h, t_ * 128:(t_ + 1) * 128].rearrange("s p -> p s"), bnS, allow_slow_non_contiguous=True)
                        V("tensor_copy", qsd[:, 0, :], q1[:, cs0:cs0 + SD], r=[bq1], w=[bqsd])
                        V("tensor_copy", qsd[:, 1, :], q2[:, cs0:cs0 + SD], r=[bq2], w=[bqsd])
                        for hh in range(2):
                            V("tensor_tensor", qms[:, hh, :].rearrange("p (s t) -> p s t", t=SD), qsd[:, hh, :].unsqueeze(2).to_broadcast([128, SD, SD]),
                              sdmask[:, :].rearrange("p (s t) -> p s t", t=SD), ALU.mult, r=[bqsd, b_sdm], w=[bqms])
                        for s in range(SD):
                            St, bSt = SS[s % 3], bSS[s % 3]
                            if ret:
                                load(St[:, :, 0:256], sret[s, h].rearrange("(t p) e -> p t e", p=128), bSt)
                            else:
                                load(St[:, :, 0:256], smc[s, h].rearrange("(t p) e -> p t e", p=128), bSt)
                                G("tensor_copy", St[:, :, 256], nS[:, :, s], r=[bnS], w=[bSt])
                            V("tensor_scalar_mul", kms[:SD, :], ksd[:SD, :], ident[:SD, s:s + 1], r=[bksd, b_ident], w=[bkms])
                            for hh in range(2):
                                T("matmul", pU[hh][:, 0:W], kms[:SD, hh * 128:(hh + 1) * 128], vsd[:SD, 0:W], start=True, stop=True, r=[bkms, bvsd], w=[bU[hh]])
                            for hh in range(2):
                                sc_ = gam[h] if ret else wpbcS[:, h, s:s + 1]
                                V("scalar_tensor_tensor", St[:, hh, 0:W], St[:, hh, 0:W], sc_, pU[hh][:, 0:W], ALU.mult, ALU.add, r=[bSt, bU[hh], bwpbcS], w=[bSt])
                            for hh in range(2):
                                T("matmul", pnd[:SD, 0:W], qms[:, hh, s * SD:(s + 1) * SD], St[:, hh, 0:W], start=(s == 0 and hh == 0), stop=(s == SD - 1 and hh == 1),
                                  r=[bqms, bSt], w=[bnd])
                            if ret:
                                P.dma("sp", o_rets[s, h].rearrange("(t p) e -> p t e", p=128), St[:, :, 0:256], bSt, reads=[bSt])
                            else:
                                P.dma("sp", o_cs[s, h].rearrange("(t p) e -> p t e", p=128), St[:, :, 0:256], bSt, reads=[bSt])
                                G("tensor_copy", nS[:, :, s], St[:, :, 256], r=[bSt], w=[bnS])
                        if ret:
                            head_out(SD, pnd[:SD, 0:256], gnr, b_gnr, (g1, g2), (bg1, bg2), cs0, h)
                        else:
                            for t_ in range(2):
                                P.dma("sp", o_ns[:, h, t_ * 128:(t_ + 1) * 128].rearrange("s p -> p s"), nS[:, t_, :], bnS, reads=[bnS], allow_slow_non_contiguous=True)
                            A("activation", hst[:SD, 10:11], pnd[:SD, 256:257], AF.Abs, r=[bnd], w=[bhst])
                            V("tensor_tensor", hst[:SD, 10:11], hst[:SD, 10:11], scol[:SD, H + h:H + h + 1], ALU.max, r=[bhst, bscol], w=[bhst])
                            V("reciprocal", hst[:SD, 11:12], hst[:SD, 10:11], r=[bhst], w=[bhst])
                            head_out(SD, pnd[:SD, 0:256], gnm, b_gnm, (g1, g2), (bg1, bg2), cs0, h, r_ap=hst[:SD, 11:12])
                    row0 = (0 if ret else RW) + h * 256
                    P.dma("sp", o_mix[row0:row0 + 256, :].rearrange("(t p) n -> p t n", p=128), mixT[:, :, :], bmix, reads=[bmix])

                gate_stage()
                for h in range(H):
                    run_head("ret", h)
                    run_head("ml", h)
                if not pre:
                    P.dma("sp", o_mp, mrow[:, NCH:NCH + 1], bmrow, reads=[bmrow])
                    for j_ in range(3):
                        P.dma("sp", o_convp[j_].rearrange("(t p) -> p t", p=128), convp[:, j_, :], bconvp, reads=[bconvp], allow_slow_non_contiguous=True)
            P.barrier()

        own_tiles = [(xo[c * 128:(c + 1) * 128, :], 128, c * 128, b_XT[c]) for c in range(NCH)]
        if SD > 0:
            own_tiles.append((xs[:, :], SD, NT, b_XT[NCH]))
        pre_tiles = [(xp[c * 128:(c + 1) * 128, :], 128, c * 128, b_XT[c]) for c in range(NCH)]
        all_tiles = [(c * 128, 128) for c in range(NCH)] + ([(NT, SD)] if SD > 0 else [])

        def phase_wout():
            with ExitStack() as ph:
                if KTM <= KT:
                    mixS = XT
                else:
                    mixS = P.sb("mixS", [128, KTM, NTO], BF16, ph)
                bmixS = P.buf("mixS")
                load(mixS[:, 0:KTM, :], o_mix.rearrange("(t p) n -> p t n", p=128), bmixS)
                wo = [P.sb("wo", [128, KTM, CBS], BF16, ph) for _ in range(2)]; bwo = [P.buf("wo") for _ in range(2)]
                pyo = [P.ps("pyo", [128, 512], F32, ph) for _ in range(2)]; bpyo = [P.buf("pyo") for _ in range(2)]
                ysb = [P.sb("ysb", [128, CBS], F32, ph) for _ in range(2)]; bysb = [P.buf("ysb") for _ in range(2)]
                it = 0
                for cb in range(NCB):
                    w_, bw_ = wo[cb % 2], bwo[cb % 2]
                    load(w_[:].rearrange("p k c -> p (k c)"), wo_d[cb], bw_, q="pool")
                    for (c0, n) in all_tiles:
                        p_, bp = pyo[it % 2], bpyo[it % 2]
                        y_, by = ysb[it % 2], bysb[it % 2]
                        it += 1
                        for kt in range(KTM):
                            T("matmul", p_[:n, :CBS], mixS[:, kt, c0:c0 + n], w_[:, kt, :], start=(kt == 0), stop=(kt == KTM - 1), r=[bmixS, bw_], w=[bp])
                        A("activation", y_[:n, :], p_[:n, :CBS], AF.Identity, r=[bp], w=[by])
                        P.dma("sp", y_d[c0:c0 + n, cb * CBS:(cb + 1) * CBS], y_[:n, :], by, reads=[by])
            P.barrier()

        def ln_rows(x, b, n, s, bs, pcs):
            nst = len(pcs)
            for c, (c0, c1) in enumerate(pcs):
                V("bn_stats", s[:n, c * 6:(c + 1) * 6], x[:n, c0:c1], r=[b], w=[bs])
            o = nst * 6
            V("bn_aggr", s[:n, o:o + 2], s[:n, 0:o], r=[bs], w=[bs])
            rstd_op(s[:n, o + 2:o + 3], s[:n, o + 1:o + 2], n, [bs], [bs])
            V("scalar_tensor_tensor", s[:n, o + 3:o + 4], s[:n, o:o + 1], -1.0, s[:n, o + 2:o + 3], ALU.mult, ALU.mult, r=[bs], w=[bs])
            A("activation", x[:n, :], x[:n, :], AF.Identity, bias=s[:n, o + 3:o + 4], scale=s[:n, o + 2:o + 3], r=[b, bs], w=[b])

        def phase_ln1():
            with ExitStack() as ph:
                geb, bgeb = P.sb("geb", [128, D], F32, ph), P.buf("geb"); load(geb[:], geb_d, bgeb)
                beb, bbeb = P.sb("beb", [128, D], F32, ph), P.buf("beb"); load(beb[:], beb_d, bbeb)
                g1b, bg1b = P.sb("g1b", [128, D], F32, ph), P.buf("g1b"); load(g1b[:], g1b_d, bg1b)
                b1b, bb1b = P.sb("b1b", [128, D], F32, ph), P.buf("b1b"); load(b1b[:], b1b_d, bb1b)
                xt_ = P.sb("l1x", [128, D], F32, ph); bx = P.buf("l1x")
                yt_ = P.sb("l1y", [128, D], F32, ph); by = P.buf("l1y")
                pcs = pieces(D)
                st = P.sb("l1st", [128, len(pcs) * 6 + 8], F32, ph); bst = P.buf("l1st")
                pt = [P.ps("l1pt", [128, 4, 128], F32, ph) for _ in range(2)]; bpt = [P.buf("l1pt") for _ in range(2)]
                gi = 0
                for ti, (c0, n) in enumerate(all_tiles):
                    src = xo[c0:c0 + n, :] if c0 < NT else xs[:, :]
                    load(xt_[:n, :], src, bx)
                    load(yt_[:n, :], y_d[c0:c0 + n, :], by)
                    ln_rows(xt_, bx, n, st, bst, pcs)
                    V("tensor_tensor", xt_[:n, :], xt_[:n, :], geb[:n, :], ALU.mult, r=[bx, bgeb], w=[bx])
                    G("tensor_tensor", xt_[:n, :], xt_[:n, :], beb[:n, :], ALU.add, r=[bx, bbeb], w=[bx])
                    V("scalar_tensor_tensor", yt_[:n, :], xt_[:n, :], ALPHA, yt_[:n, :], ALU.mult, ALU.add, r=[bx, by], w=[by])
                    ln_rows(yt_, by, n, st, bst, pcs)
                    V("tensor_tensor", yt_[:n, :], yt_[:n, :], g1b[:n, :], ALU.mult, r=[by, bg1b], w=[by])
                    G("tensor_tensor", yt_[:n, :], yt_[:n, :], b1b[:n, :], ALU.add, r=[by, bb1b], w=[by])
                    P.dma("sp", x1_d[c0:c0 + n, :], yt_[:n, :], by, reads=[by])
                    xb = b_XT[ti]
                    for k0 in range(0, KT, 4):
                        p_, bp = pt[gi % 2], bpt[gi % 2]; gi += 1
                        nk = min(4, KT - k0)
                        for j in range(nk):
                            T("transpose", p_[:, j, :n], yt_[:n, (k0 + j) * 128:(k0 + j + 1) * 128], ident[:n, :n], r=[by, b_ident], w=[bp])
                        if (k0 // 4) % 2 == 0:
                            A("activation", XT[:, k0:k0 + nk, c0:c0 + n], p_[:, 0:nk, :n], AF.Identity, r=[bp], w=[xb])
                        else:
                            V("tensor_copy", XT[:, k0:k0 + nk, c0:c0 + n], p_[:, 0:nk, :n], r=[bp], w=[xb])
            P.barrier()

        def phase_peer():
            TG = cfg.TG
            groups = [all_tiles[i:i + TG] for i in range(0, len(all_tiles), TG)]
            tile_idx = {c0: i for i, (c0, n) in enumerate(all_tiles)}
            HP = PH * 2
            for grp in groups:
                ng = len(grp)
                with ExitStack() as gph:
                    yacc = P.sb("yacc", [128, TG, D], F32, gph); byacc = [P.buf("yacc") for _ in range(TG)]
                    xbufs = [b_XT[tile_idx[c0]] for (c0, n) in grp]
                    s_ = P.sb("s_", [128, TG, HP * NK], F32, gph); bs_ = [P.buf("s_") for _ in range(TG)]
                    rt = P.sb("rt", [128, TG, 4 * PH], F32, gph); brt = [P.buf("rt") for _ in range(TG)]
                    with ExitStack() as ph:
                        phh = [P.ps("phh", [128, 512], F32, ph) for _ in range(TG)]; bphh = [P.buf("phh") for _ in range(TG)]
                        py = [P.ps("py", [128, 512], F32, ph) for _ in range(2)]; bpy = [P.buf("py") for _ in range(2)]
                        pmi = [P.ps("pmi", [128, 512], F32, ph) for _ in range(2)]; bpmi = [P.buf("pmi") for _ in range(2)]
                        wr = [P.sb("wr", [128, KG, 512], BF16, ph) for _ in range(4)]; bwr = [P.buf("wr") for _ in range(4)]
                        wrs = {"i": 0}
                        wpp = [P.sb("wpp", [128, PT_, CBS], BF16, ph) for _ in range(2)]; bwpp = [P.buf("wpp") for _ in range(2)]
                        skT, bskT = P.sb("skT", [128, HP * NK], F32, ph), P.buf("skT"); load(skT[:], skT_d, bskT)
                        sv = P.sb("sv", [128, HP * 16], F32, ph); bsv = P.buf("sv")
                        scr = P.sb("scr", [128, 512], F32, ph); bscr = P.buf("scr")
                        cand = [P.sb("cand", [128, 256], F32, ph) for _ in range(2)]; bcand = [P.buf("cand") for _ in range(2)]
                        tv = P.sb("tv", [128, PH * 24], F32, ph); btv = P.buf("tv")
                        qsb = P.sb("qsb", [128, 512], F32, ph); bqsb = P.buf("qsb")
                        qT = P.sb("qT", [128, 4, 128], F32, ph); bqT = P.buf("qT")
                        pTt = P.sb("pTt", [128, TG, PT_ * 128], BF16, ph); bpTt = [P.buf("pTt") for _ in range(TG)]
                        prow = P.sb("prow", [128, PLE], F32, ph); bprow = P.buf("prow")
                        sg = [P.sb("sg", [128, CBS], F32, ph) for _ in range(1)]; bsg = [P.buf("sg") for _ in range(1)]
                        rng = {"T": 0, "a": 0, "py": 0, "sg": 0, "mi": 0}

                        def wpiece(src_ap, ncols):
                            i = wrs["i"] % 4; wrs["i"] += 1
                            load(wr[i][:, :, 0:ncols], src_ap.rearrange("p (k c) -> p k c", c=ncols), bwr[i], q="pool")
                            return wr[i], bwr[i]

                        for gi, (c0, n) in enumerate(grp):
                            load(yacc[:n, gi, :], x1_d[c0:c0 + n, :], byacc[gi])
                            psrc = po[c0:c0 + n, :] if c0 < NT else ps_d[:, :]
                            load(prow[:n, :], psrc, bprow)
                            p_, bp = pmi[rng["mi"] % 2], bpmi[rng["mi"] % 2]; rng["mi"] += 1
                            for j in range(PT_):
                                T("transpose", p_[:, j * 128:j * 128 + n], prow[:n, j * 128:(j + 1) * 128], ident[:n, :n], r=[bprow, b_ident], w=[bp])
                            for j in range(PT_):
                                A("activation", pTt[:, gi, j * 128:j * 128 + n], p_[:, j * 128:j * 128 + n], AF.Identity, r=[bp], w=[bpTt[gi]])

                        for gi, (c0, n) in enumerate(grp):
                            pass
                        for cq in range(NCQ):
                            for kg in range(NKG):
                                wp_, bwp_ = wpiece(wq_d[cq, kg], CQ)
                                for gi, (c0, n) in enumerate(grp):
                                    for j in range(KG):
                                        kt = kg * KG + j
                                        T("matmul", phh[gi][:n, :CQ], XT[:, kt, c0:c0 + n], wp_[:, j, :CQ], start=(kt == 0), stop=(kt == KT - 1), r=[xbufs[gi], bwp_], w=[bphh[gi]])
                            for gi, (c0, n) in enumerate(grp):
                                A("activation", qsb[:n, :CQ], phh[gi][:n, :CQ], AF.Identity, r=[bphh[gi]], w=[bqsb])
                                nj = CQ // 128
                                p_, bp = pmi[rng["mi"] % 2], bpmi[rng["mi"] % 2]; rng["mi"] += 1
                                for j in range(nj):
                                    T("transpose", p_[:, j * 128:j * 128 + n], qsb[:n, j * 128:(j + 1) * 128], ident[:n, :n], r=[bqsb, b_ident], w=[bp])
                                for j in range(nj):
                                    A("activation", qT[:, j, :n], p_[:, j * 128:j * 128 + n], AF.Identity, r=[bp], w=[bqT])
                                hpb = 512 // NK if NK <= 512 else 1
                                for j0 in range(0, nj, hpb):
                                    p2, bp2 = pmi[rng["mi"] % 2], bpmi[rng["mi"] % 2]; rng["mi"] += 1
                                    jn = min(hpb, nj - j0)
                                    for j in range(j0, j0 + jn):
                                        hp = cq * nj + j
                                        T("matmul", p2[:n, (j - j0) * NK:(j - j0 + 1) * NK], qT[:, j, :n], skT[:, hp * NK:(hp + 1) * NK], start=True, stop=True, r=[bqT, bskT], w=[bp2])
                                    hp0 = cq * nj + j0
                                    A("activation", s_[:n, gi, hp0 * NK:(hp0 + jn) * NK], p2[:n, 0:jn * NK], AF.Identity, r=[bp2], w=[bs_[gi]])
                        for gi, (c0, n) in enumerate(grp):
                            s3 = s_[:, gi, :].rearrange("p (a k) -> p a k", k=NK)
                            sv3 = sv[:, :].rearrange("p (a k) -> p a k", k=16)
                            for hp in range(HP):
                                V("max", sv3[:n, hp, 0:8], s3[:n, hp, :], r=[bs_[gi]], w=[bsv])
                                V("match_replace", scr[:n, 0:NK], sv3[:n, hp, 0:8], s3[:n, hp, :], -1e30, r=[bs_[gi], bsv], w=[bscr])
                                V("max", sv3[:n, hp, 8:16], scr[:n, 0:NK], r=[bscr], w=[bsv])
                            tv3 = tv[:, :].rearrange("p (h k) -> p h k", k=24)
                            r4 = rt[:, gi, :].rearrange("p (a h) -> p a h", h=PH)
                            for h in range(PH):
                                V("tensor_tensor", cand[0][:n, :].rearrange("p (a b) -> p a b", b=16), sv3[:n, 2 * h, :].unsqueeze(2).to_broadcast([n, 16, 16]),
                                  sv3[:n, 2 * h + 1, :].unsqueeze(1).to_broadcast([n, 16, 16]), ALU.add, r=[bsv], w=[bcand[0]])
                                V("max", tv3[:n, h, 0:8], cand[0][:n, :], r=[bcand[0]], w=[btv])
                                V("match_replace", cand[1][:n, :], tv3[:n, h, 0:8], cand[0][:n, :], -1e30, r=[bcand[0], btv], w=[bcand[1]])
                                V("max", tv3[:n, h, 8:16], cand[1][:n, :], r=[bcand[1]], w=[btv])
                                V("match_replace", cand[0][:n, :], tv3[:n, h, 8:16], cand[1][:n, :], -1e30, r=[bcand[1], btv], w=[bcand[0]])
                                V("max", tv3[:n, h, 16:24], cand[0][:n, :], r=[bcand[0]], w=[btv])
                            V("tensor_tensor", r4[:n, 0, :], tv3[:n, :, 15], tv3[:n, :, 16], ALU.add, r=[btv], w=[brt[gi]])
                            V("tensor_scalar_mul", r4[:n, 0, :], r4[:n, 0, :], 0.5, r=[brt[gi]], w=[brt[gi]])
                            V("tensor_scalar_mul", r4[:n, 1, :], tv3[:n, :, 0], -1.0, r=[btv], w=[brt[gi]])
                            for h in range(PH):
                                A("activation", scr[:n, 0:16], tv3[:n, h, 0:16], AF.Exp, bias=r4[:n, 1, h:h + 1], accum_out=r4[:n, 2, h:h + 1], r=[btv, brt[gi]], w=[bscr, brt[gi]])
                            A("activation", r4[:n, 2, :], r4[:n, 2, :], AF.Ln, r=[brt[gi]], w=[brt[gi]])
                            V("tensor_tensor", r4[:n, 3, :], r4[:n, 1, :], r4[:n, 2, :], ALU.subtract, r=[brt[gi]], w=[brt[gi]])

                        for cb in range(NCB):
                            wq_, bwq_ = wpp[cb % 2], bwpp[cb % 2]
                            load(wq_[:].rearrange("p k c -> p (k c)"), wpp_d[cb], bwq_, q="pool")
                            for kg in range(NKG):
                                wp_, bwp_ = wpiece(wpg_d[cb, kg], CBS)
                                for gi, (c0, n) in enumerate(grp):
                                    for j in range(KG):
                                        kt = kg * KG + j
                                        T("matmul", phh[gi][:n, :CBS], XT[:, kt, c0:c0 + n], wp_[:, j, :CBS], start=(kt == 0), stop=(kt == KT - 1), r=[xbufs[gi], bwp_], w=[bphh[gi]])
                            for gi, (c0, n) in enumerate(grp):
                                sg_, bsg_ = sg[0], bsg[0]
                                p_, bp = py[rng["py"] % 2], bpy[rng["py"] % 2]; rng["py"] += 1
                                A("activation", sg_[:n, :], phh[gi][:n, :CBS], AF.Sigmoid, r=[bphh[gi]], w=[bsg_])
                                for j in range(PT_):
                                    T("matmul", p_[:n, :CBS], pTt[:, gi, j * 128:j * 128 + n], wq_[:, j, :], start=(j == 0), stop=(j == PT_ - 1), r=[bpTt[gi], bwq_], w=[bp])
                                V("tensor_tensor", sg_[:n, :], sg_[:n, :], p_[:n, :CBS], ALU.mult, r=[bsg_, bp], w=[bsg_])
                                ysl = yacc[:n, gi, cb * CBS:(cb + 1) * CBS]
                                V("scalar_tensor_tensor", ysl, ysl, ALPHA, sg_[:n, :], ALU.mult, ALU.add, r=[byacc[gi], bsg_], w=[byacc[gi]])

                    P.barrier()
                    with ExitStack() as ph:
                        NPY = max(2, min(3, 8 - TG - 2))
                        phh = [P.ps("phh", [128, 512], F32, ph) for _ in range(TG)]; bphh = [P.buf("phh") for _ in range(TG)]
                        py = [P.ps("py", [128, 512], F32, ph) for _ in range(NPY)]; bpy = [P.buf("py") for _ in range(NPY)]
                        paT = [P.ps("paT", [128, 1024], BF16, ph) for _ in range(2)]; bpaT = [P.buf("paT") for _ in range(2)]
                        NR = 8
                        ring = [P.sb("wring", [128, max(KG, ET), 512], BF16, ph) for _ in range(NR)]; bring = [P.buf("wring") for _ in range(NR)]
                        Tt = [P.sb("Tt", [128, EB], F32, ph) for _ in range(2)]; bTt = [P.buf("Tt") for _ in range(2)]
                        Ex = [P.sb("Ex", [128, EB], BF16, ph) for _ in range(2)]; bEx = [P.buf("Ex") for _ in range(2)]
                        Mm = [P.sb("Mm", [128, EB], BF16, ph) for _ in range(2)]; bMm = [P.buf("Mm") for _ in range(2)]
                        Gacc = [[P.sb("Gacc", [128, EB], BF16, ph) for _ in range(TG)] for _ in range(2)]
                        bGacc = [[P.buf("Gacc") for _ in range(TG)] for _ in range(2)]
                        asb = [P.sb("asb", [128, EB], BF16, ph) for _ in range(2)]; basb = [P.buf("asb") for _ in range(2)]
                        abf = [P.sb("abf", [128, EB], BF16, ph) for _ in range(2)]; babf = [P.buf("abf") for _ in range(2)]
                        aT = [P.sb("aT", [128, ET, 128], BF16, ph) for _ in range(TG)]; baT = [P.buf("aT") for _ in range(TG)]
                        rng = {"T": 0, "a": 0, "py": 0, "sg": 0, "mi": 0}

                        seq = []
                        for eb in range(NEB):
                            seq += [("u", eb, kg) for kg in range(NKG)]
                            seq += [("v", eb, cb) for cb in range(NCB)]
                        wst = {"pos": 0}

                        def wnext():
                            j = wst["pos"]; wst["pos"] += 1
                            kind, eb_, x = seq[j]
                            t_, b_ = ring[j % NR], bring[j % NR]
                            if kind == "u":
                                load(t_[:, 0:KG, 0:EB], uT_d[eb_, x].rearrange("p (k c) -> p k c", c=EB), b_, q="pool")
                            else:
                                load(t_[:, 0:ET, 0:CBS], v_d[eb_, x].rearrange("p (k c) -> p k c", c=CBS), b_, q="pool")
                            return t_, b_

                        def gate_units(eb, par):
                            units = []
                            i0 = eb * NI
                            for gi, (c0, n) in enumerate(grp):
                                for h in range(PH):
                                    st = {}

                                    def unit_a(gi=gi, n=n, h=h, st=st):
                                        s3 = s_[:, gi, :].rearrange("p (a k) -> p a k", k=NK)
                                        r4 = rt[:, gi, :].rearrange("p (a h) -> p a h", h=PH)
                                        ti = rng["T"] % 2; rng["T"] += 1
                                        st["ti"] = ti
                                        V("tensor_tensor", Tt[ti][:n, :].rearrange("p (i k) -> p i k", k=NK), s3[:n, 2 * h, i0:i0 + NI].unsqueeze(2).to_broadcast([n, NI, NK]),
                                          s3[:n, 2 * h + 1, :].unsqueeze(1).to_broadcast([n, NI, NK]), ALU.add, r=[bs_[gi]], w=[bTt[ti]])
                                        A("activation", Ex[ti][:n, :], Tt[ti][:n, :], AF.Exp, bias=r4[:n, 3, h:h + 1], r=[bTt[ti], brt[gi]], w=[bEx[ti]])

                                    def unit_b(gi=gi, n=n, h=h, st=st):
                                        r4 = rt[:, gi, :].rearrange("p (a h) -> p a h", h=PH)
                                        GA, bGA = Gacc[par][gi], bGacc[par][gi]
                                        ti = st["ti"]
                                        dst, bdst = (GA, bGA) if h == 0 else (Mm[ti], bMm[ti])
                                        V("scalar_tensor_tensor", dst[:n, :], Tt[ti][:n, :], r4[:n, 0, h:h + 1], Ex[ti][:n, :], ALU.is_ge, ALU.mult, r=[bTt[ti], bEx[ti], brt[gi]], w=[bdst])
                                        if h > 0:
                                            V("tensor_tensor", GA[:n, :], GA[:n, :], Mm[ti][:n, :], ALU.add, r=[bGA, bMm[ti]], w=[bGA])
                                    units.append((unit_a, unit_b))
                            return units

                        pend = {"b": None}

                        def run_unit(u_):
                            u_[0]()
                            if pend["b"] is not None:
                                pend["b"]()
                            pend["b"] = u_[1]

                        def flush_units():
                            if pend["b"] is not None:
                                pend["b"]()
                                pend["b"] = None

                        for u_ in gate_units(0, 0):
                            run_unit(u_)
                        flush_units()
                        for eb in range(NEB):
                            par = eb % 2
                            nxt = gate_units(eb + 1, 1 - par) if eb + 1 < NEB else []
                            UQ = max(0, -(-(len(nxt) - NCB) // NKG))
                            for kg in range(NKG):
                                wp_, bwp_ = wnext()
                                for gi, (c0, n) in enumerate(grp):
                                    for j in range(KG):
                                        kt = kg * KG + j
                                        T("matmul", phh[gi][:n, :EB], XT[:, kt, c0:c0 + n], wp_[:, j, :EB], start=(kt == 0), stop=(kt == KT - 1), r=[xbufs[gi], bwp_], w=[bphh[gi]])
                                for _ in range(UQ):
                                    if nxt:
                                        run_unit(nxt.pop(0))
                            for gi, (c0, n) in enumerate(grp):
                                ai = rng["a"] % 2; rng["a"] += 1
                                A("activation", asb[ai][:n, :], phh[gi][:n, :EB], AF.Gelu, r=[bphh[gi]], w=[basb[ai]])
                                V("tensor_tensor", abf[ai][:n, :], asb[ai][:n, :], Gacc[par][gi][:n, :], ALU.mult, r=[basb[ai], bGacc[par][gi]], w=[babf[ai]])
                                pa, bpa = paT[gi % 2], bpaT[gi % 2]
                                for j in range(ET):
                                    T("transpose", pa[:, j * 128:j * 128 + n], abf[ai][:n, j * 128:(j + 1) * 128], identb[:n, :n], r=[babf[ai], b_identb], w=[bpa])
                                for j in range(ET):
                                    A("activation", aT[gi][:, j, :n], pa[:, j * 128:j * 128 + n], AF.Identity, r=[bpa], w=[baT[gi]])
                            for cb in range(NCB):
                                vr_, bvr_ = wnext()
                                for gi, (c0, n) in enumerate(grp):
                                    p_, bp = py[rng["py"] % NPY], bpy[rng["py"] % NPY]; rng["py"] += 1
                                    for j in range(ET):
                                        T("matmul", p_[:n, :CBS], aT[gi][:, j, :n], vr_[:, j, :CBS], start=(j == 0), stop=(j == ET - 1), r=[baT[gi], bvr_], w=[bp])
                                    ysl = yacc[:n, gi, cb * CBS:(cb + 1) * CBS]
                                    V("tensor_tensor", ysl, ysl, p_[:n, :CBS], ALU.add, r=[byacc[gi], bp], w=[byacc[gi]])
                                if nxt:
                                    run_unit(nxt.pop(0))
                            while nxt:
                                run_unit(nxt.pop(0))
                            flush_units()
                    P.barrier()
                    with ExitStack() as ph2:
                        g2b, bg2b = P.sb("g2b", [128, D], F32, ph2), P.buf("g2b"); load(g2b[:], g2b_d, bg2b)
                        b2b, bb2b = P.sb("b2b", [128, D], F32, ph2), P.buf("b2b"); load(b2b[:], b2b_d, bb2b)
                        pcs = pieces(D)
                        st = P.sb("l2st", [128, len(pcs) * 6 + 8], F32, ph2); bst = P.buf("l2st")
                        for gi, (c0, n) in enumerate(grp):
                            yv = yacc[:, gi, :]
                            ln_rows(yv, byacc[gi], n, st, bst, pcs)
                            V("tensor_tensor", yv[:n, :], yv[:n, :], g2b[:n, :], ALU.mult, r=[byacc[gi], bg2b], w=[byacc[gi]])
                            G("tensor_tensor", yv[:n, :], yv[:n, :], b2b[:n, :], ALU.add, r=[byacc[gi], bb2b], w=[byacc[gi]])
                            P.dma("sp", y_o[c0:c0 + n, :], yv[:n, :], byacc[gi], reads=[byacc[gi]])
                    P.barrier()

        stop = int(os.environ.get("KSTOP", "99"))
        if stop >= 1:
            phase_convbuf()
        if stop >= 2:
            phase_ln(pre_tiles)
        if stop >= 3:
            mixer_phase("pre")
        if stop >= 4:
            phase_ln(own_tiles)
        if stop >= 5:
            mixer_phase("own")
        if stop >= 6:
            phase_wout()
        if stop >= 7:
            phase_ln1()
        if stop >= 8:
            phase_peer()
        P.emit()
    return nc


ROPE_BASE = 10000.0
PAST_LEN = 16384


def col_layout(v, ntile):
    return np.ascontiguousarray(np.asarray(v, np.float32).reshape(ntile, 128).T)


def rope_tables(pos, H, dq_fn, dk_fn, idx):
    half = 128
    inv = (np.float32(ROPE_BASE) ** (-np.arange(half, dtype=np.float32) / np.float32(half))).astype(np.float32)
    ang = pos.astype(np.float32)[:, None] * inv[None]
    cos = np.cos(ang).astype(np.float32).T
    sin = np.sin(ang).astype(np.float32).T
    tq = np.empty((H, 2, 128, len(pos)), np.float32)
    tk = np.empty((H, 2, 128, len(pos)), np.float32)
    for h in range(H):
        dq = dq_fn(h, idx)[None, :]
        dk = dk_fn(h, idx)[None, :]
        tq[h, 0] = cos * dq; tq[h, 1] = sin * dq
        tk[h, 0] = cos * dk; tk[h, 1] = sin * dk
    return tq, tk


_NC_CACHE = {}


def kernel(x_prompt, x_sample, state_ret, state_conv, state_mlstm_c, state_mlstm_n, state_mlstm_m, p_prompt, p_sample,
           ln_emb_g, ln_emb_b, w_in, b_gate, conv_w, conv_b, g_ret_norm, g_ml_norm, w_out, ln1_g, ln1_b,
           w_peer_q, peer_sub_keys, peer_u, peer_v, w_ple_gate, w_ple_proj, ln2_g, ln2_b, _debug=None):
    f32 = np.float32
    B, SEQ, D = x_prompt.shape
    DB = x_sample.shape[0]
    H = state_ret.shape[2]
    cps = NCORES // B
    assert cps == 2
    cfg = Cfg()
    cfg.D = D; cfg.KT = D // 128; cfg.H = H
    cfg.NT = SEQ // cps; cfg.SD = DB // NCORES
    NT, SD, KT = cfg.NT, cfg.SD, cfg.KT
    NTO = NT + SD
    RW = H * 256
    NCT = 8 * RW // 128
    NMT = 2 * RW // 128
    cfg.NK = peer_sub_keys.shape[3]; cfg.PH = peer_sub_keys.shape[1]; cfg.PLE = p_prompt.shape[-1]
    cfg.ALPHA = float((2.0 * w_in.shape[0]) ** 0.25)
    ntiles = NT // 128 + (1 if SD > 0 else 0)
    cfg.TG = 3 if ntiles >= 3 and ntiles % 3 == 0 else 2
    cfg.TG = int(os.environ.get("KTG", cfg.TG))
    NK, PH, PLE = cfg.NK, cfg.PH, cfg.PLE
    NE = NK * NK
    KTM = 2 * RW // 128
    CBS = min(512, D); NCB = D // CBS
    KG = min(int(os.environ.get("KKG", "4")), KT); NKG = KT // KG
    QW = PH * 256; CQ = min(512, QW); NCQ = QW // CQ
    EB = min(512, NE); NEB = NE // EB; ET = EB // 128
    PT_ = PLE // 128
    key = (D, H, NT, SD, NK, PH, PLE)
    if key not in _NC_CACHE:
        _NC_CACHE[key] = build(cfg)
    nc = _NC_CACHE[key]

    w_in0 = np.asarray(w_in[0], f32)
    wm = np.ascontiguousarray(w_in0[:, :NCT * 128].reshape(KT, 128, NCT, 128).transpose(2, 1, 0, 3)).reshape(NCT, 128, KT * 128)
    wg = np.ascontiguousarray(w_in0[:, NCT * 128:].reshape(KT, 128, 2 * H).transpose(1, 0, 2)).reshape(128, KT * 2 * H)
    bg = np.ascontiguousarray(np.asarray(b_gate[0], f32).reshape(2, H).T)
    cw = np.asarray(conv_w[0], f32)
    convw = np.ascontiguousarray(cw.reshape(4, NMT, 128).transpose(2, 1, 0)).reshape(128, NMT * 4)
    convb = col_layout(conv_b[0], NMT)
    gnr = col_layout(np.asarray(g_ret_norm[0], f32).reshape(-1), 2 * H)
    gnm = col_layout(np.asarray(g_ml_norm[0], f32).reshape(-1), 2 * H)
    lng = col_layout(ln_emb_g, KT); lnb = col_layout(ln_emb_b, KT)
    ident = np.eye(128, dtype=f32)
    jj, ii = np.meshgrid(np.arange(128), np.arange(128), indexing="ij")
    mask01 = (jj <= ii).astype(f32)
    maskneg = np.where(jj <= ii, 0.0, -30000.0).astype(f32)
    sel = np.zeros((128, H, 128), f32)
    for h in range(H):
        sel[h, h, :] = 1.0
    sel = sel.reshape(128, H * 128)
    sdmask = np.zeros((128, SD, SD), f32)
    for s in range(SD):
        sdmask[:, s, s] = 1.0
    sdmask = sdmask.reshape(128, SD * SD)

    wo = np.ascontiguousarray(np.asarray(w_out[0], f32).reshape(KTM, 128, NCB, CBS).transpose(2, 1, 0, 3)).reshape(NCB, 128, KTM * CBS)
    bc = lambda v: np.ascontiguousarray(np.broadcast_to(np.asarray(v, f32).reshape(1, D), (128, D)))
    geb, beb, g1b, b1b, g2b, b2b = bc(ln_emb_g), bc(ln_emb_b), bc(ln1_g[0]), bc(ln1_b[0]), bc(ln2_g[0]), bc(ln2_b[0])
    wq = np.ascontiguousarray(np.asarray(w_peer_q[0], f32).reshape(NKG, KG, 128, NCQ, CQ).transpose(3, 0, 2, 1, 4)).reshape(NCQ, NKG, 128, KG * CQ)
    skT = np.ascontiguousarray(np.asarray(peer_sub_keys[0], f32).transpose(3, 0, 1, 2)).reshape(128, PH * 2 * NK)
    uT = np.ascontiguousarray(np.asarray(peer_u[0], f32).reshape(NEB, EB, NKG, KG, 128).transpose(0, 2, 4, 3, 1)).reshape(NEB, NKG, 128, KG * EB)
    vv = np.ascontiguousarray(np.asarray(peer_v[0], f32).reshape(NEB, ET, 128, NCB, CBS).transpose(0, 3, 2, 1, 4)).reshape(NEB, NCB, 128, ET * CBS)
    wpg = np.ascontiguousarray(np.asarray(w_ple_gate[0], f32).reshape(NKG, KG, 128, NCB, CBS).transpose(3, 0, 2, 1, 4)).reshape(NCB, NKG, 128, KG * CBS)
    wpp = np.ascontiguousarray(np.asarray(w_ple_proj[0], f32).reshape(PT_, 128, NCB, CBS).transpose(2, 1, 0, 3)).reshape(NCB, 128, PT_ * CBS)
    ps_all = np.asarray(p_sample[0], f32).reshape(DB, PLE)

    gam = [np.float64(1.0 - 2.0 ** (-5.0 - h)) for h in range(H)]
    idx_own = np.concatenate([np.arange(NT) % 128, np.zeros(SD, np.int64)])
    is_s = np.concatenate([np.zeros(NT, bool), np.ones(SD, bool)])

    def dq_own(h, idx):
        return np.where(is_s, 1.0, gam[h] ** (idx + 1.0)).astype(f32)

    def dk_own(h, idx):
        return np.where(is_s, 1.0 / 16.0, gam[h] ** (-(idx + 1.0)) / 16.0).astype(f32)

    def dk_pre(h, idx):
        return (gam[h] ** (-(idx + 1.0)) / 16.0).astype(f32)

    in_maps = []
    tabs = {}
    for half in range(cps):
        pos_own = np.concatenate([half * NT + np.arange(NT), np.full(SD, PAST_LEN)])
        tq, tk = rope_tables(pos_own, H, dq_own, dk_own, idx_own)
        pos_pre = np.arange(NT)
        _, tkp = rope_tables(pos_pre, H, dk_pre, dk_pre, np.arange(NT) % 128)
        tabs[half] = (tq, tk, tkp)
    xs_all = np.asarray(x_sample, f32).reshape(DB, D)
    for c in range(NCORES):
        b, half = c // cps, c % cps
        xb = np.asarray(x_prompt[b], f32)
        xo = xb[half * NT:(half + 1) * NT]
        xp = xb[0:NT] if half == 1 else xo
        s0 = c * SD
        tq, tk, tkp = tabs[half]
        in_maps.append(dict(
            xo=np.ascontiguousarray(xo), xp=np.ascontiguousarray(xp), xs=np.ascontiguousarray(xs_all[s0:s0 + SD]),
            flag=np.full((128, 1), float(half), f32), lng=lng, lnb=lnb, wm=wm, wg=wg, bg=bg, convw=convw, convb=convb,
            gnr=gnr, gnm=gnm, tq=tq, tk=tk, tkp=tkp, ident=ident, mask01=mask01, maskneg=maskneg, sel=sel, sdmask=sdmask,
            sret=np.ascontiguousarray(state_ret[0, s0:s0 + SD]), sconv=np.ascontiguousarray(np.asarray(state_conv[0, s0:s0 + SD], f32).reshape(SD * 3, 2 * RW)),
            smc=np.ascontiguousarray(state_mlstm_c[0, s0:s0 + SD]), smn=np.ascontiguousarray(state_mlstm_n[0, s0:s0 + SD]),
            smm=np.ascontiguousarray(state_mlstm_m[0, s0:s0 + SD]),
            wo=wo, geb=geb, beb=beb, g1b=g1b, b1b=b1b, g2b=g2b, b2b=b2b, wq=wq, skT=skT, uT=uT, vv=vv, wpg=wpg, wpp=wpp,
            po=np.ascontiguousarray(np.asarray(p_prompt[0, b], f32)[half * NT:(half + 1) * NT]), ps=np.ascontiguousarray(ps_all[s0:s0 + SD]),
        ))
    res = run_bass_kernel_spmd(nc, in_maps, core_ids=list(range(NCORES))).results

    last = [b * cps + cps - 1 for b in range(B)]
    ret_p = np.stack([res[c]["o_retp"] for c in last])[None]
    conv_p = np.stack([res[c]["o_convp"] for c in last])[None]
    c_p = np.stack([res[c]["o_cp"] for c in last])[None]
    n_p = np.stack([res[c]["o_np"] for c in last])[None]
    m_p = np.stack([res[c]["o_mp"][:, 0] for c in last])[None]
    ret_s = np.concatenate([res[c]["o_rets"] for c in range(NCORES)])[None]
    conv_s = np.concatenate([res[c]["o_convs"].reshape(SD, 3, 2 * RW) for c in range(NCORES)])[None]
    c_s = np.concatenate([res[c]["o_cs"] for c in range(NCORES)])[None]
    n_s = np.concatenate([res[c]["o_ns"] for c in range(NCORES)])[None]
    m_s = np.concatenate([res[c]["o_ms"] for c in range(NCORES)])[None]
    if _debug is not None:
        _debug["mix"] = [np.asarray(res[c]["o_mix"]) for c in range(NCORES)]
    y_prompt = np.zeros((B, SEQ, D), f32)
    y_sample = np.zeros((DB, 1, D), f32)
    for c in range(NCORES):
        b, half = c // cps, c % cps
        yo = np.asarray(res[c]["y_o"])
        y_prompt[b, half * NT:(half + 1) * NT] = yo[:NT]
        y_sample[c * SD:(c + 1) * SD, 0] = yo[NT:]
    return (y_prompt, y_sample, ret_p.astype(f32), conv_p.astype(f32), c_p.astype(f32), n_p.astype(f32), m_p.astype(f32),
            ret_s.astype(f32), conv_s.astype(f32), c_s.astype(f32), n_s.astype(f32), m_s.astype(f32))
```

```python
import math
import os
from contextlib import ExitStack
import numpy as np
import concourse.bass as bass
import concourse.mybir as mybir
from concourse.bass_utils import run_bass_kernel_spmd

F32 = mybir.dt.float32
BF16 = mybir.dt.bfloat16
ALU = mybir.AluOpType
AF = mybir.ActivationFunctionType
AX = mybir.AxisListType

SAME_ENGINE_SYNC = True
NCORES = 8
LN_EPS = 1e-5
LN16 = math.log(16.0)


class Buf:
    __slots__ = ("name", "w", "r")

    def __init__(self, name):
        self.name = name
        self.w = None
        self.r = {}


class Prog:
    ENGS = ("pe", "act", "dve", "pool", "sp")

    def __init__(self, nc, ctx):
        self.nc = nc
        self.ctx = ctx
        self.ops = {e: [] for e in self.ENGS}
        self.cnt = {}
        self.waited = {e: {} for e in self.ENGS}
        self.semh = {}
        self.nbuf = 0
        self.ntile = 0

    def buf(self, name=None):
        self.nbuf += 1
        return Buf(f"{name or 'b'}{self.nbuf}")

    def sb(self, name, shape, dtype, ctx=None):
        self.ntile += 1
        return (ctx or self.ctx).enter_context(self.nc.sbuf_tensor(f"{name}_{self.ntile}", list(shape), dtype))

    def ps(self, name, shape, dtype, ctx=None):
        self.ntile += 1
        return (ctx or self.ctx).enter_context(self.nc.psum_tensor(f"{name}_{self.ntile}", list(shape), dtype))

    def _sem(self, key):
        if key not in self.semh:
            nm = "s" + str(len(self.semh)) + "_" + "".join(ch for ch in str(key) if ch.isalnum())[:24]
            self.semh[key] = self.ctx.enter_context(self.nc.semaphore(nm))
            self.cnt[key] = 0
        return self.semh[key]

    def _collect(self, eng, reads, writes):
        waits = {}

        def need(k, v):
            if k[0] == "d":
                v = max(v, self.cnt[k])
            if k == eng and (eng == "pe" or not SAME_ENGINE_SYNC):
                return
            if self.waited[eng].get(k, 0) < v and waits.get(k, 0) < v:
                waits[k] = v

        for b in reads:
            if b.w is not None:
                need(*b.w)
        for b in writes:
            if b.w is not None:
                need(*b.w)
            for k, v in b.r.items():
                need(k, v)
        for k, v in waits.items():
            self.waited[eng][k] = v
        return list(waits.items())

    def op(self, eng, name, args, kw, reads=(), writes=()):
        self._sem(eng)
        waits = self._collect(eng, reads, writes)
        self.cnt[eng] += 1
        val = self.cnt[eng]
        self.ops[eng].append((waits, name, args, kw, eng, 1))
        for b in reads:
            b.r[eng] = val
        for b in writes:
            b.w = (eng, val)
            b.r = {}
        return val

    NDMASEM = 72

    def dma(self, q, out_ap, in_ap, tag, reads=(), writes=(), **kw):
        key0 = ("dw" if any(b is tag for b in writes) else "dr", tag.name)
        if not hasattr(self, "keymap"):
            self.keymap = {}
        if key0 not in self.keymap:
            assert len(self.keymap) < self.NDMASEM, "too many DMA buffers in one phase"
            self.keymap[key0] = ("d", len(self.keymap))
        key = self.keymap[key0]
        self._sem(key)
        waits = self._collect(q, reads, writes)
        self.cnt[key] += 16
        val = self.cnt[key]
        kw = dict(kw); kw["out"] = out_ap; kw["in_"] = in_ap
        self.ops[q].append((waits, "dma_start", (), kw, key, 16))
        for b in reads:
            b.r[key] = val
        for b in writes:
            b.w = (key, val)
            b.r = {}
        return val

    def barrier(self):
        for e in self.ENGS:
            waits = []
            for k, v in self.cnt.items():
                if k == e or v == 0:
                    continue
                if self.waited[e].get(k, 0) < v:
                    waits.append((k, v))
                    self.waited[e][k] = v
            if waits:
                self.ops[e].append((waits, None, None, None, None, 0))
        self.maxkeys = max(getattr(self, "maxkeys", 0), len(getattr(self, "keymap", {})))
        self.keymap = {}

    def emit(self):
        nc = self.nc
        self.barrier()
        with nc.Block() as block:
            def run(e, name):
                for waits, opn, args, kw, key, inc in self.ops[name]:
                    for k, v in waits:
                        e.wait_ge(self.semh[k], v)
                    if opn is not None:
                        getattr(e, opn)(*args, **kw).then_inc(self.semh[key], inc)

            @block.sync
            def _(e):
                run(e, "sp")

            @block.tensor
            def _(e):
                run(e, "pe")

            @block.scalar
            def _(e):
                run(e, "act")

            @block.vector
            def _(e):
                run(e, "dve")

            @block.gpsimd
            def _(e):
                run(e, "pool")


def pieces(n, m=512):
    k = (n + m - 1) // m
    base = (n + k - 1) // k
    out = []
    c = 0
    while c < n:
        out.append((c, min(n, c + base)))
        c += base
    return out


class Cfg:
    pass


def build(cfg):
    D, KT, H, NT, SD = cfg.D, cfg.KT, cfg.H, cfg.NT, cfg.SD
    NTO = NT + SD
    NCH = NT // 128
    RW = H * 256
    NCT = 8 * RW // 128
    NMT = 2 * RW // 128
    gam = [1.0 - 2.0 ** (-5.0 - h) for h in range(H)]
    NK, PH, PLE, ALPHA = cfg.NK, cfg.PH, cfg.PLE, cfg.ALPHA
    NE = NK * NK
    KTM = 2 * RW // 128
    CBS = min(512, D); NCB = D // CBS
    KG = min(int(os.environ.get("KKG", "4")), KT); NKG = KT // KG
    QW = PH * 256; CQ = min(512, QW); NCQ = QW // CQ
    EB = min(512, NE); NEB = NE // EB; ET = EB // 128
    NI = EB // NK
    PT_ = PLE // 128

    nc = bass.Bass("TRN2", target_bir_lowering=False)

    def din(name, shape, dt=F32):
        return nc.dram_tensor(name, list(shape), dt, kind="ExternalInput").ap()

    def dout(name, shape, dt=F32):
        return nc.dram_tensor(name, list(shape), dt, kind="ExternalOutput").ap()

    def dscr(name, shape, dt=F32):
        return nc.dram_tensor(name, list(shape), dt, kind="Internal").ap()

    xo = din("xo", [NT, D]); xp = din("xp", [NT, D]); xs = din("xs", [SD, D])
    flag_d = din("flag", [128, 1])
    lng_d = din("lng", [128, KT]); lnb_d = din("lnb", [128, KT])
    wm = din("wm", [NCT, 128, KT * 128])
    wg_d = din("wg", [128, KT * 2 * H])
    bg_d = din("bg", [H, 2])
    convw_d = din("convw", [128, NMT * 4]); convb_d = din("convb", [128, NMT])
    gnr_d = din("gnr", [128, 2 * H]); gnm_d = din("gnm", [128, 2 * H])
    tq_d = din("tq", [H, 2, 128, NTO]); tk_d = din("tk", [H, 2, 128, NTO]); tkp_d = din("tkp", [H, 2, 128, NT])
    ident_d = din("ident", [128, 128]); mask01_d = din("mask01", [128, 128]); maskneg_d = din("maskneg", [128, 128])
    sel_d = din("sel", [128, H * 128]); sdmask_d = din("sdmask", [128, SD * SD])
    sret = din("sret", [SD, H, 256, 256]); sconv = din("sconv", [SD * 3, 2 * RW])
    smc = din("smc", [SD, H, 256, 256]); smn = din("smn", [SD, H, 256]); smm = din("smm", [SD, H])

    o_retp = dout("o_retp", [H, 256, 256]); o_convp = dout("o_convp", [3, 2 * RW])
    o_cp = dout("o_cp", [H, 256, 256]); o_np = dout("o_np", [H, 256]); o_mp = dout("o_mp", [H, 1])
    o_rets = dout("o_rets", [SD, H, 256, 256]); o_convs = dout("o_convs", [SD * 3, 2 * RW])
    o_cs = dout("o_cs", [SD, H, 256, 256]); o_ns = dout("o_ns", [SD, H, 256]); o_ms = dout("o_ms", [SD, H])
    o_mix = dout("o_mix", [2 * RW, NTO], BF16)

    wo_d = din("wo", [NCB, 128, KTM * CBS])
    geb_d = din("geb", [128, D]); beb_d = din("beb", [128, D]); g1b_d = din("g1b", [128, D]); b1b_d = din("b1b", [128, D])
    g2b_d = din("g2b", [128, D]); b2b_d = din("b2b", [128, D])
    wq_d = din("wq", [NCQ, NKG, 128, KG * CQ]); skT_d = din("skT", [128, PH * 2 * NK])
    uT_d = din("uT", [NEB, NKG, 128, KG * EB]); v_d = din("vv", [NEB, NCB, 128, ET * CBS])
    wpg_d = din("wpg", [NCB, NKG, 128, KG * CBS]); wpp_d = din("wpp", [NCB, 128, PT_ * CBS])
    po = din("po", [NT, PLE]); ps_d = din("ps", [SD, PLE])
    y_o = dout("y_o", [NTO, D])
    y_d = dscr("y_d", [NTO, D]); x1_d = dscr("x1_d", [NTO, D])
    bstate_r = dscr("bstate_r", [H, 128, 512])
    bstate_c = dscr("bstate_c", [H, 128, 2 * 257])

    with ExitStack() as ctx:
        P = Prog(nc, ctx)

        def V(name, *a, r=(), w=(), **kw):
            P.op("dve", name, a, kw, r, w)

        def A(name, *a, r=(), w=(), **kw):
            P.op("act", name, a, kw, r, w)

        def G(name, *a, r=(), w=(), **kw):
            P.op("pool", name, a, kw, r, w)

        def T(name, *a, r=(), w=(), **kw):
            P.op("pe", name, a, kw, r, w)

        def load(dst, src, b, q="sp", **kw):
            P.dma(q, dst, src, b, writes=[b], **kw)

        def const(name, shape, src, dt=F32):
            t = P.sb(name, shape, dt); b = P.buf(name)
            load(t[:], src, b)
            return t, b

        ident, b_ident = const("ident", [128, 128], ident_d)
        mask01, b_m01 = const("mask01", [128, 128], mask01_d)
        maskneg, b_mneg = const("maskneg", [128, 128], maskneg_d)
        flag, b_flag = const("flag", [128, 1], flag_d)
        lng, b_lng = const("lng", [128, KT], lng_d)
        lnb, b_lnb = const("lnb", [128, KT], lnb_d)
        sel, b_sel = const("sel", [128, H * 128], sel_d)
        sdmask, b_sdm = const("sdmask", [128, SD * SD], sdmask_d)
        bg, b_bg = const("bg", [H, 2], bg_d)
        convw, b_cw = const("convw", [128, NMT * 4], convw_d)
        convb, b_cb = const("convb", [128, NMT], convb_d)
        gnr, b_gnr = const("gnr", [128, 2 * H], gnr_d)
        gnm, b_gnm = const("gnm", [128, 2 * H], gnm_d)
        identb = P.sb("identb", [128, 128], BF16); b_identb = P.buf("identb")
        V("tensor_copy", identb[:], ident[:], r=[b_ident], w=[b_identb])
        mhalf = P.sb("mhalf", [128, 1], F32); b_mhalf = P.buf("mhalf")
        V("memset", mhalf[:], -0.5, w=[b_mhalf])
        onesrow = P.sb("onesrow", [H, 128], F32); b_onesrow = P.buf("onesrow")
        V("memset", onesrow[:], 1.0, w=[b_onesrow])
        halo = P.sb("halo", [128, NMT * 3], F32); b_halo = P.buf("halo")
        mbound = P.sb("mbound", [H, 1], F32); b_mbound = P.buf("mbound")
        bufT = P.sb("bufT", [128, NMT, SD * 3], F32); b_bufT = P.buf("bufT")
        XT = P.sb("XT", [128, KT, NTO], BF16)
        b_XT = [P.buf("XT") for _ in range(NCH + 1)]

        def rstd_op(out_ap, var_ap, n, rb, wb):
            G("tensor_scalar", out_ap, var_ap, LN_EPS, None, ALU.add, r=rb, w=wb)
            G("tensor_tensor", out_ap, out_ap, mhalf[:n, :], ALU.pow, r=list(wb) + [b_mhalf], w=wb)

        def phase_ln(tiles):
            with ExitStack() as ph:
                xt = [P.sb("p1x", [128, D], F32, ph) for _ in range(2)]; bx = [P.buf("p1x") for _ in range(2)]
                pcs = pieces(D)
                nst = len(pcs)
                st = [P.sb("p1st", [128, nst * 6 + 8], F32, ph) for _ in range(2)]; bst = [P.buf("p1st") for _ in range(2)]
                pt = [P.ps("p1pt", [128, 4, 128], F32, ph) for _ in range(2)]; bpt = [P.buf("p1pt") for _ in range(2)]
                gi = 0
                for ti, (src, n, col0, xb) in enumerate(tiles):
                    x, b = xt[ti % 2], bx[ti % 2]
                    s, bs = st[ti % 2], bst[ti % 2]
                    load(x[:n, :], src, b)
                    for c, (c0, c1) in enumerate(pcs):
                        V("bn_stats", s[:n, c * 6:(c + 1) * 6], x[:n, c0:c1], r=[b], w=[bs])
                    o = nst * 6
                    V("bn_aggr", s[:n, o:o + 2], s[:n, 0:o], r=[bs], w=[bs])
                    rstd_op(s[:n, o + 2:o + 3], s[:n, o + 1:o + 2], n, [bs], [bs])
                    V("scalar_tensor_tensor", s[:n, o + 3:o + 4], s[:n, o:o + 1], -1.0, s[:n, o + 2:o + 3], ALU.mult, ALU.mult, r=[bs], w=[bs])
                    A("activation", x[:n, :], x[:n, :], AF.Identity, bias=s[:n, o + 3:o + 4], scale=s[:n, o + 2:o + 3], r=[b, bs], w=[b])
                    for k0 in range(0, KT, 4):
                        p_, bp = pt[gi % 2], bpt[gi % 2]; gi += 1
                        nk = min(4, KT - k0)
                        for j in range(nk):
                            kt = k0 + j
                            T("transpose", p_[:, j, :n], x[:n, kt * 128:(kt + 1) * 128], ident[:n, :n], r=[b, b_ident], w=[bp])
                        for j in range(nk):
                            kt = k0 + j
                            if j % 2 == 0:
                                V("tensor_scalar", XT[:, kt, col0:col0 + n], p_[:, j, :n], lng[:, kt:kt + 1], lnb[:, kt:kt + 1], ALU.mult, ALU.add,
                                  r=[bp, b_lng, b_lnb], w=[xb])
                            else:
                                A("activation", XT[:, kt, col0:col0 + n], p_[:, j, :n], AF.Identity, bias=lnb[:, kt:kt + 1], scale=lng[:, kt:kt + 1],
                                  r=[bp, b_lng, b_lnb], w=[xb])
            P.barrier()

        def phase_convbuf():
            if SD == 0:
                return
            with ExitStack() as ph:
                cvs = P.sb("cvs", [SD * 3, 2 * RW], F32, ph); bcvs = P.buf("cvs")
                pc = P.ps("pcv", [128, 512], F32, ph); bpc = P.buf("pcv")
                load(cvs[:, :], sconv, bcvs)
                for mti in range(NMT):
                    T("transpose", pc[:, 0:SD * 3], cvs[:, mti * 128:(mti + 1) * 128], ident[:SD * 3, :SD * 3], r=[bcvs, b_ident], w=[bpc])
                    A("activation", bufT[:, mti, :], pc[:, 0:SD * 3], AF.Identity, r=[bpc], w=[b_bufT])
                o3 = o_convs.rearrange("(s j) f -> s j f", j=3)
                c3 = cvs[:, :]
                for s in range(SD):
                    P.dma("sp", o_convs[s * 3:s * 3 + 2, :], cvs[s * 3 + 1:s * 3 + 3, :], bcvs, reads=[bcvs])
            P.barrier()

        def mixer_phase(mode):
            pre = mode == "pre"
            NC_ = NT if pre else NTO
            SDm = max(SD, 1)
            with ExitStack() as ph:
                NW = 3
                wt = [P.sb("wt", [128, KT, 128], BF16, ph) for _ in range(NW)]; bw = [P.buf("wt") for _ in range(NW)]
                pz = [P.ps("pz", [128, 512], F32, ph) for _ in range(2)]; bpz = [P.buf("pz") for _ in range(2)]
                pzst = {"i": 0}
                psc = P.ps("psc", [128, 4, 128], F32, ph); bsc = [P.buf("psc")] * 4
                pnd = P.ps("pnd", [128, 512], F32, ph); bnd = P.buf("pnd")
                pU = [P.ps("pU", [128, 512], F32, ph) for _ in range(2)]; bU = [P.buf("pU") for _ in range(2)]
                ptr = P.ps("ptr", [128, 1024], BF16, ph); btr = [P.buf("ptr")] * 4
                pms = P.ps("pms", [128, 512], F32, ph); bms = P.buf("pms")
                F = [P.sb("F", [128, NC_], BF16, ph) for _ in range(8)]; bF = [P.buf("F") for _ in range(8)]
                RA = P.sb("RA", [128, 3 + NC_], F32, ph); bRA = P.buf("RA")
                RB = P.sb("RB", [128, 3 + NC_], F32, ph); bRB = P.buf("RB")
                tmp = [P.sb("tmp", [128, 512], F32, ph) for _ in range(2)]; btmp = [P.buf("tmp") for _ in range(2)]
                tq = P.sb("tq", [128, 2, NC_], F32, ph); btq = P.buf("tq")
                tk = P.sb("tk", [128, 2, NC_], F32, ph); btk = P.buf("tk")
                RAW = [RA, RB]; bRAW = [bRA, bRB]
                CACC = tq[:, 0, :]; bCACC = btq
                PTt = P.sb("PT", [128, 128], BF16, ph); bPT = P.buf("PT")
                Et = P.sb("Et", [128, 128], F32, ph); bE = P.buf("Et")
                QP = P.sb("QP", [128, 2, 128], BF16, ph); bQP = P.buf("QP")
                vc = P.sb("vc", [128, 260], BF16, ph); bvc = P.buf("vc")
                kc = P.sb("kc", [128, 256], BF16, ph); bkc = P.buf("kc")
                on = P.sb("on", [128, 256], BF16, ph); bon = P.buf("on")
                hst = P.sb("hst", [128, 16], F32, ph); bhst = P.buf("hst")
                S32 = P.sb("S32", [128, 2, 257], F32, ph); bS32 = P.buf("S32")
                Sbf = P.sb("Sbf", [128, 2, 257], BF16, ph); bSbf = P.buf("Sbf")
                mixT = P.sb("mixT", [128, 2, NC_], BF16, ph); bmix = P.buf("mixT")
                SS = [P.sb("SS", [128, 2, 257], F32, ph) for _ in range(4)]; bSS = [P.buf("SS") for _ in range(4)]
                ksd = P.sb("ksd", [SDm, 256], BF16, ph); bksd = P.buf("ksd")
                kms = P.sb("kms", [SDm, 256], BF16, ph); bkms = P.buf("kms")
                vsd = P.sb("vsd", [SDm, 260], BF16, ph); bvsd = P.buf("vsd")
                qsd = P.sb("qsd", [128, 2, SDm], F32, ph); bqsd = P.buf("qsd")
                qms = P.sb("qms", [128, 2, SDm * SDm], F32, ph); bqms = P.buf("qms")
                nS = P.sb("nS", [128, 2, SDm], F32, ph); bnS = P.buf("nS")
                crw = [P.sb("crw", [SDm, 128], F32, ph) for _ in range(2)]; bcrw = [P.buf("crw") for _ in range(2)]
                crst = {"i": 0}
                NG = NC_
                grow = {nm: P.sb("g_" + nm, [128, NG], F32, ph) for nm in ("ig", "lf", "negM", "wp")}
                bgrow = {nm: P.buf("g_" + nm) for nm in grow}
                gt1 = P.sb("gt1", [H, 512], F32, ph); bgt1 = P.buf("gt1")
                gt2 = P.sb("gt2", [H, 512], F32, ph); bgt2 = P.buf("gt2")
                colA = P.sb("colA", [128, NCH, 3 * H], F32, ph); bcolA = P.buf("colA")
                wcrow = P.sb("wcrow", [128, NCH + 1], F32, ph); bwcrow = P.buf("wcrow")
                mrow = P.sb("mrow", [H, NCH + 2], F32, ph); bmrow = P.buf("mrow")
                wcbc = P.sb("wcbc", [128, H, NCH + 1], F32, ph); bwcbc = P.buf("wcbc")
                wgt = P.sb("wgt", [128, KT * 2 * H], BF16, ph); bwgt = P.buf("wgt")
                srow = {nm: P.sb("s_" + nm, [128, SDm], F32, ph) for nm in ("m", "mt", "w16", "wp", "emt", "t")}
                bsrow = P.buf("srow")
                scol = P.sb("scol", [SDm, 2 * H], F32, ph); bscol = P.buf("scol")
                wpbcS = P.sb("wpbcS", [128, H, SDm], F32, ph); bwpbcS = P.buf("wpbcS")
                convp = P.sb("convp", [128, 3, NMT], F32, ph); bconvp = P.buf("convp")

                xt_bufs = b_XT[:NCH] if pre else b_XT

                def head_cts(kind, h):
                    if kind == "ret":
                        bq, bk, bv, bgg = 2 * h, RW // 128 + 2 * h, 2 * RW // 128 + 2 * h, 3 * RW // 128 + 2 * h
                    else:
                        b0 = 4 * RW // 128
                        bq, bk, bv, bgg = b0 + 2 * h, b0 + RW // 128 + 2 * h, b0 + 2 * RW // 128 + 2 * h, b0 + 3 * RW // 128 + 2 * h
                    return bq, bk, bv, bgg

                order = []
                for h in range(H):
                    for kind in ("ret", "ml"):
                        bq, bk, bv, bgg = head_cts(kind, h)
                        if pre:
                            if kind == "ml":
                                order += [bq, bq + 1]
                            order += [bk, bk + 1, bv, bv + 1]
                        else:
                            order += [bq, bq + 1, bk, bk + 1, bv, bv + 1, bgg, bgg + 1]
                wst = {"issued": 0, "pos": 0}

                def wget(ct):
                    pos = wst["pos"]
                    assert order[pos] == ct, (order[pos], ct, pos)
                    while wst["issued"] <= min(pos + 2, len(order) - 1):
                        i = wst["issued"]
                        load(wt[i % NW][:].rearrange("p k c -> p (k c)"), wm[order[i]], bw[i % NW], q="pool")
                        wst["issued"] += 1
                    wst["pos"] += 1
                    return wt[pos % NW], bw[pos % NW]

                def ztile(ct, evac):
                    wtile, bwt = wget(ct)
                    for (c0, c1) in pieces(NC_):
                        i = pzst["i"] % 2; pzst["i"] += 1
                        p_, bp = pz[i], bpz[i]
                        n = c1 - c0
                        for kt in range(KT):
                            T("matmul", p_[:, :n], wtile[:, kt, :], XT[:, kt, c0:c1], start=(kt == 0), stop=(kt == KT - 1), r=[bwt] + xt_bufs, w=[bp])
                        evac(p_[:, :n], bp, c0, c1)

                def rope_pair(ct0, tab, btab, dst1, bd1, dst2, bd2):
                    def ev1(ps, bp, c0, c1):
                        V("tensor_tensor", RA[:, c0:c1], ps, tab[:, 0, c0:c1], ALU.mult, r=[bp, btab], w=[bRA])
                        V("tensor_tensor", RB[:, c0:c1], ps, tab[:, 1, c0:c1], ALU.mult, r=[bp, btab], w=[bRB])

                    def ev2(ps, bp, c0, c1):
                        n = c1 - c0
                        V("tensor_tensor", tmp[0][:, :n], ps, tab[:, 1, c0:c1], ALU.mult, r=[bp, btab], w=[btmp[0]])
                        V("tensor_tensor", tmp[1][:, :n], ps, tab[:, 0, c0:c1], ALU.mult, r=[bp, btab], w=[btmp[1]])
                        G("tensor_tensor", dst1[:, c0:c1], RA[:, c0:c1], tmp[0][:, :n], ALU.subtract, r=[bRA, btmp[0]], w=[bd1])
                        G("tensor_tensor", dst2[:, c0:c1], RB[:, c0:c1], tmp[1][:, :n], ALU.add, r=[bRB, btmp[1]], w=[bd2])
                    ztile(ct0, ev1)
                    ztile(ct0 + 1, ev2)

                def copy_tile(ct, dst, bd, func=AF.Identity):
                    def ev(ps, bp, c0, c1):
                        A("activation", dst[:, c0:c1], ps, func, r=[bp], w=[bd])
                    ztile(ct, ev)

                def tr_to_tokens(srcs, bsrcs, c0, n, slot, dst, bdst, scale=1.0, bscale=()):
                    for hh in range(2):
                        T("transpose", ptr[:n, slot * 256 + hh * 128: slot * 256 + (hh + 1) * 128], srcs[hh][:, c0:c0 + n], identb[:, :],
                          r=[bsrcs[hh], b_identb], w=[btr[slot]])
                    A("activation", dst[:n, 0:256], ptr[:n, slot * 256:(slot + 1) * 256], AF.Identity, scale=scale, r=[btr[slot]] + list(bscale), w=[bdst])

                def head_out(n, nd_ap, gcol, bg_, sg, bsg, c0, gi, r_ap=None):
                    V("bn_stats", hst[:n, 0:6], nd_ap, r=[bnd], w=[bhst])
                    V("bn_aggr", hst[:n, 6:8], hst[:n, 0:6], r=[bhst], w=[bhst])
                    if r_ap is not None:
                        V("tensor_tensor", hst[:n, 6:7], hst[:n, 6:7], r_ap, ALU.mult, r=[bhst], w=[bhst])
                        V("tensor_tensor", hst[:n, 7:8], hst[:n, 7:8], r_ap, ALU.mult, r=[bhst], w=[bhst])
                        V("tensor_tensor", hst[:n, 7:8], hst[:n, 7:8], r_ap, ALU.mult, r=[bhst], w=[bhst])
                    rstd_op(hst[:n, 8:9], hst[:n, 7:8], n, [bhst], [bhst])
                    V("scalar_tensor_tensor", hst[:n, 9:10], hst[:n, 6:7], -1.0, hst[:n, 8:9], ALU.mult, ALU.mult, r=[bhst], w=[bhst])
                    if r_ap is not None:
                        V("tensor_tensor", hst[:n, 8:9], hst[:n, 8:9], r_ap, ALU.mult, r=[bhst], w=[bhst])
                    A("activation", on[:n, :], nd_ap, AF.Identity, bias=hst[:n, 9:10], scale=hst[:n, 8:9], r=[bnd, bhst], w=[bon])
                    for hh in range(2):
                        T("transpose", ptr[:, 512 + hh * 128:512 + hh * 128 + n], on[:n, hh * 128:(hh + 1) * 128], identb[:n, :n], r=[bon, b_identb], w=[btr[2]])
                    for hh in range(2):
                        V("scalar_tensor_tensor", mixT[:, hh, c0:c0 + n], ptr[:, 512 + hh * 128:512 + hh * 128 + n], gcol[:, 2 * gi + hh:2 * gi + hh + 1],
                          sg[hh][:, c0:c0 + n], ALU.mult, ALU.mult, r=[btr[2], bg_, bsg[hh]], w=[bmix])

                def gate_stage():
                    ig, lf, negM, wpr = (grow[k] for k in ("ig", "lf", "negM", "wp"))
                    big, blf, bnegM, bwp = (bgrow[k] for k in ("ig", "lf", "negM", "wp"))
                    for k in grow:
                        G("memset", grow[k][:, :], 0.0, w=[bgrow[k]])
                    G("memset", wcrow[:, :], 0.0, w=[bwcrow])
                    load(wgt[:], wg_d, bwgt, q="pool")
                    wg3 = wgt[:].rearrange("p (k g) -> p k g", g=2 * H)
                    for (c0, c1) in pieces(NG):
                        n = c1 - c0
                        for gi, (dst, bd) in enumerate(((ig, big), (lf, blf))):
                            for kt in range(KT):
                                T("matmul", pms[:H, :n], wg3[:, kt, gi * H:(gi + 1) * H], XT[:, kt, c0:c1], start=(kt == 0), stop=(kt == KT - 1),
                                  r=[bwgt] + xt_bufs, w=[bms])
                            A("activation", dst[:H, c0:c1], pms[:H, :n], AF.Identity, bias=bg[:, gi:gi + 1], r=[bms, b_bg], w=[bd])
                        V("scalar_tensor_tensor", gt1[:, :n], lf[:H, c0:c1], -1.0, lf[:H, c0:c1], ALU.mult, ALU.min, r=[blf], w=[bgt1])
                        A("activation", gt1[:, :n], gt1[:, :n], AF.Exp, r=[bgt1], w=[bgt1])
                        A("activation", gt1[:, :n], gt1[:, :n], AF.Ln, bias=1.0, r=[bgt1], w=[bgt1])
                        V("tensor_scalar_min", lf[:H, c0:c1], lf[:H, c0:c1], 0.0, r=[blf], w=[blf])
                        V("tensor_tensor", lf[:H, c0:c1], lf[:H, c0:c1], gt1[:, :n], ALU.subtract, r=[blf, bgt1], w=[blf])
                    if pre:
                        V("memset", mrow[:, 0:1], 0.0, w=[bmrow])
                    else:
                        V("tensor_copy", mrow[:, 0:1], mbound[:, :], r=[b_mbound], w=[bmrow])
                    t1 = gt1[:, 0:128]; t2 = gt2[:, 0:128]; t3 = gt2[:, 128:256]
                    for c in range(NCH):
                        sl = slice(c * 128, (c + 1) * 128)
                        last = slice(c * 128 + 127, c * 128 + 128)
                        V("tensor_tensor_scan", t1, onesrow[:, :], lf[:H, sl], 0.0, ALU.mult, ALU.add, r=[blf, b_onesrow], w=[bgt1])
                        V("tensor_tensor", t2, ig[:H, sl], t1, ALU.subtract, r=[big, bgt1], w=[bgt2])
                        V("tensor_tensor_scan", negM[:H, sl], onesrow[:, :], t2, mrow[:, c:c + 1], ALU.mult, ALU.max, r=[bgt2, b_onesrow, bmrow], w=[bnegM])
                        V("tensor_tensor", mrow[:, c + 1:c + 2], gt1[:, 127:128], negM[:H, last], ALU.add, r=[bgt1, bnegM], w=[bmrow])
                        V("tensor_tensor", t1, t1, negM[:H, sl], ALU.add, r=[bgt1, bnegM], w=[bgt1])
                        A("activation", t1, t1, AF.Exp, scale=-1.0, r=[bgt1], w=[bgt1])
                        V("tensor_scalar", wcrow[:H, NCH:NCH + 1], negM[:H, last], -1.0, -LN16, ALU.mult, ALU.add, r=[bnegM], w=[bwcrow])
                        V("tensor_scalar_mul", negM[:H, sl], negM[:H, sl], -1.0, r=[bnegM], w=[bnegM])
                        A("activation", wpr[:H, sl], negM[:H, sl], AF.Exp, bias=mrow[:, c:c + 1], r=[bnegM, bmrow], w=[bwp])
                        V("tensor_copy", wcrow[:H, c:c + 1], wpr[:H, last], r=[bwp], w=[bwcrow])
                        A("activation", t3, t2, AF.Exp, bias=wcrow[:H, NCH:NCH + 1], r=[bgt2, bwcrow], w=[bgt2])
                        T("transpose", pms[:, 0:H], t2, ident[:H, :H], r=[bgt2, b_ident], w=[bms])
                        T("transpose", pms[:, H:2 * H], t3, ident[:H, :H], r=[bgt2, b_ident], w=[bms])
                        T("transpose", pms[:, 2 * H:3 * H], t1, ident[:H, :H], r=[bgt1, b_ident], w=[bms])
                        A("activation", colA[:, c, :], pms[:, 0:3 * H], AF.Identity, r=[bms], w=[bcolA])
                    for h in range(H):
                        T("matmul", pms[:, 0:NCH], sel[:, h * 128:(h + 1) * 128], wcrow[:, 0:NCH], start=True, stop=True, r=[b_sel, bwcrow], w=[bms])
                        A("activation", wcbc[:, h, 0:NCH], pms[:, 0:NCH], AF.Identity, r=[bms], w=[bwcbc])
                    if pre:
                        V("tensor_scalar_mul", mbound[:, :], mrow[:, NCH:NCH + 1], flag[:H, :], r=[bmrow, b_flag], w=[b_mbound])
                    elif SD > 0:
                        sm, smt, sw16, swp, semt, stt = (srow[k] for k in ("m", "mt", "w16", "wp", "emt", "t"))
                        for k in srow:
                            G("memset", srow[k][:, :], 0.0, w=[bsrow])
                        load(sm[:H, :], smm.rearrange("s h -> h s"), bsrow, allow_slow_non_contiguous=True)
                        cs = slice(NT, NTO)
                        V("tensor_tensor", stt[:H, :], lf[:H, cs], sm[:H, :], ALU.add, r=[blf, bsrow], w=[bsrow])
                        V("tensor_tensor", smt[:H, :], stt[:H, :], ig[:H, cs], ALU.max, r=[big, bsrow], w=[bsrow])
                        V("tensor_tensor", swp[:H, :], stt[:H, :], smt[:H, :], ALU.subtract, r=[bsrow], w=[bsrow])
                        A("activation", swp[:H, :], swp[:H, :], AF.Exp, r=[bsrow], w=[bsrow])
                        V("tensor_tensor", sw16[:H, :], ig[:H, cs], smt[:H, :], ALU.subtract, r=[big, bsrow], w=[bsrow])
                        V("tensor_scalar_add", sw16[:H, :], sw16[:H, :], -LN16, r=[bsrow], w=[bsrow])
                        A("activation", sw16[:H, :], sw16[:H, :], AF.Exp, r=[bsrow], w=[bsrow])
                        A("activation", semt[:H, :], smt[:H, :], AF.Exp, scale=-1.0, r=[bsrow], w=[bsrow])
                        P.dma("sp", o_ms.rearrange("s h -> h s"), smt[:H, :], bsrow, reads=[bsrow], allow_slow_non_contiguous=True)
                        T("transpose", pms[:SD, 0:H], sw16[:H, :], ident[:H, :H], r=[bsrow, b_ident], w=[bms])
                        T("transpose", pms[:SD, H:2 * H], semt[:H, :], ident[:H, :H], r=[bsrow, b_ident], w=[bms])
                        A("activation", scol[:SD, :], pms[:SD, 0:2 * H], AF.Identity, r=[bms], w=[bscol])
                        for h in range(H):
                            T("matmul", pms[:, 0:SD], sel[:, h * 128:(h + 1) * 128], swp[:, :], start=True, stop=True, r=[b_sel, bsrow], w=[bms])
                            A("activation", wpbcS[:, h, :], pms[:, 0:SD], AF.Identity, r=[bms], w=[bwpbcS])

                def run_head(kind, h):
                    ret = kind == "ret"
                    base_q, base_k, base_v, base_g = head_cts(kind, h)
                    q1, q2, k1, k2, v1, v2, g1, g2 = F
                    bq1, bq2, bk1, bk2, bv1, bv2, bg1, bg2 = bF
                    if ret:
                        if pre:
                            load(tk[:, :, :], tkp_d[h].rearrange("a p n -> p a n"), btk)
                        else:
                            load(tq[:, :, :], tq_d[h].rearrange("a p n -> p a n"), btq)
                            load(tk[:, :, :], tk_d[h].rearrange("a p n -> p a n"), btk)
                            rope_pair(base_q, tq, btq, q1, bq1, q2, bq2)
                        rope_pair(base_k, tk, btk, k1, bk1, k2, bk2)
                    else:
                        def conv_tile(ct, which, dst, bd):
                            mti = ct - 4 * RW // 128
                            R, bR = RAW[which % 2], bRAW[which % 2]

                            def ev(ps, bp, c0, c1):
                                A("activation", R[:, 3 + c0:3 + c1], ps, AF.Identity, r=[bp], w=[bR])
                            ztile(ct, ev)
                            if pre:
                                V("memset", R[:, 0:3], 0.0, w=[bR])
                                V("tensor_scalar_mul", halo[:, mti * 3:mti * 3 + 3], R[:, NT:NT + 3], flag[:, :], r=[bR, b_flag], w=[b_halo])
                            else:
                                V("tensor_copy", R[:, 0:3], halo[:, mti * 3:mti * 3 + 3], r=[b_halo], w=[bR])
                                G("tensor_copy", convp[:, :, mti], R[:, NT:NT + 3], r=[bR], w=[bconvp])
                            w4 = convw[:, mti * 4:mti * 4 + 4]
                            V("tensor_scalar", CACC[:, 0:NT], R[:, 3:3 + NT], w4[:, 3:4], convb[:, mti:mti + 1], ALU.mult, ALU.add, r=[bR, b_cw, b_cb], w=[bCACC])
                            for j in (2, 1, 0):
                                V("scalar_tensor_tensor", CACC[:, 0:NT], R[:, j:j + NT], w4[:, j:j + 1], CACC[:, 0:NT], ALU.mult, ALU.add, r=[bR, b_cw, bCACC], w=[bCACC])
                            if not pre and SD > 0:
                                V("tensor_scalar", CACC[:, NT:NTO], R[:, 3 + NT:3 + NTO], w4[:, 3:4], convb[:, mti:mti + 1], ALU.mult, ALU.add, r=[bR, b_cw, b_cb], w=[bCACC])
                                bt3 = bufT[:, mti, :].rearrange("p (s j) -> p s j", j=3)
                                for j in range(3):
                                    V("scalar_tensor_tensor", CACC[:, NT:NTO], bt3[:, :, j], w4[:, j:j + 1], CACC[:, NT:NTO], ALU.mult, ALU.add, r=[b_bufT, b_cw, bCACC], w=[bCACC])
                                i = crst["i"] % 2; crst["i"] += 1
                                T("transpose", pms[:SD, 0:128], R[:, 3 + NT:3 + NTO], ident[:, :], r=[bR, b_ident], w=[bms])
                                A("activation", crw[i][:SD, :], pms[:SD, 0:128], AF.Identity, r=[bms], w=[bcrw[i]])
                                o3 = o_convs.rearrange("(s j) f -> s j f", j=3)
                                P.dma("sp", o3[:, 2, mti * 128:(mti + 1) * 128], crw[i][:SD, :], bcrw[i], reads=[bcrw[i]])
                            A("activation", dst[:, :], CACC[:, 0:NC_], AF.Silu, r=[bCACC], w=[bd])
                        if not pre:
                            conv_tile(base_q, 0, q1, bq1)
                            conv_tile(base_q + 1, 1, q2, bq2)
                        else:
                            for ct in (base_q, base_q + 1):
                                mti = ct - 4 * RW // 128
                                wtile, bwt = wget(ct)
                                i = pzst["i"] % 2; pzst["i"] += 1
                                for kt in range(KT):
                                    T("matmul", pz[i][:, 0:3], wtile[:, kt, :], XT[:, kt, NT - 3:NT], start=(kt == 0), stop=(kt == KT - 1), r=[bwt] + xt_bufs, w=[bpz[i]])
                                V("tensor_scalar_mul", halo[:, mti * 3:mti * 3 + 3], pz[i][:, 0:3], flag[:, :], r=[bpz[i], b_flag], w=[b_halo])
                        conv_tile(base_k, 0, k1, bk1)
                        conv_tile(base_k + 1, 1, k2, bk2)
                    copy_tile(base_v, v1, bv1)
                    copy_tile(base_v + 1, v2, bv2)
                    if not pre:
                        gf = AF.Silu if ret else AF.Sigmoid
                        copy_tile(base_g, g1, bg1, gf)
                        copy_tile(base_g + 1, g2, bg2, gf)

                    W = 256 if ret else 257
                    if pre:
                        V("memset", S32[:, :, :], 0.0, w=[bS32])
                        V("memset", Sbf[:, :, :], 0.0, w=[bSbf])
                    else:
                        src = (bstate_r[h].rearrange("p (t e) -> p t e", t=2) if ret else bstate_c[h].rearrange("p (t e) -> p t e", t=2))
                        load(S32[:, :, 0:W], src, bS32)
                        A("activation", Sbf[:, :, 0:W], S32[:, :, 0:W], AF.Identity, r=[bS32], w=[bSbf])
                    if not ret:
                        V("memset", vc[:, 256:257], 1.0, w=[bvc])
                    g128 = gam[h] ** 128
                    selh = sel[:, h * 128:(h + 1) * 128]

                    for c in range(NCH):
                        c0 = c * 128
                        cs_ = slice(c0, c0 + 128)
                        if not pre:
                            T("matmul", psc[:, 0, :], k1[:, cs_], q1[:, cs_], start=True, stop=False, r=[bk1, bq1], w=[bsc[0]])
                            T("matmul", psc[:, 0, :], k2[:, cs_], q2[:, cs_], start=False, stop=True, r=[bk2, bq2], w=[bsc[0]])
                            if ret:
                                V("tensor_tensor", PTt[:, :], psc[:, 0, :], mask01[:, :], ALU.mult, r=[bsc[0], b_m01], w=[bPT])
                            else:
                                T("matmul", psc[:, 1, :], selh, grow["negM"][:, cs_], start=True, stop=False, r=[b_sel, bgrow["negM"]], w=[bsc[1]])
                                T("matmul", psc[:, 1, :], ident[:, :], maskneg[:, :], start=False, stop=True, r=[b_ident, b_mneg], w=[bsc[1]])
                                V("tensor_scalar", Et[:, :], psc[:, 1, :], colA[:, c, h:h + 1], 0.0, ALU.add, ALU.min, r=[bsc[1], bcolA], w=[bE])
                                A("activation", Et[:, :], Et[:, :], AF.Exp, r=[bE], w=[bE])
                                V("scalar_tensor_tensor", PTt[:, :], psc[:, 0, :], 1.0 / 16.0, Et[:, :], ALU.mult, ALU.mult, r=[bsc[0], bE], w=[bPT])
                                T("matmul", psc[:, 2, :], selh, grow["wp"][:, cs_], start=True, stop=True, r=[b_sel, bgrow["wp"]], w=[bsc[2]])
                                V("tensor_tensor", QP[:, 0, :], q1[:, cs_], psc[:, 2, :], ALU.mult, r=[bq1, bsc[2]], w=[bQP])
                                V("tensor_tensor", QP[:, 1, :], q2[:, cs_], psc[:, 2, :], ALU.mult, r=[bq2, bsc[2]], w=[bQP])
                        tr_to_tokens((v1, v2), (bv1, bv2), c0, 128, 0, vc, bvc)
                        if ret:
                            tr_to_tokens((k1, k2), (bk1, bk2), c0, 128, 1, kc, bkc, scale=g128)
                        else:
                            tr_to_tokens((k1, k2), (bk1, bk2), c0, 128, 1, kc, bkc, scale=colA[:, c, H + h:H + h + 1], bscale=[bcolA])
                        if not pre:
                            T("matmul", pnd[:, 0:W], PTt[:, :], vc[:, 0:W], start=True, stop=False, r=[bPT, bvc], w=[bnd])
                            if ret:
                                qa = (q1[:, cs_], q2[:, cs_]); bqa = (bq1, bq2)
                            else:
                                qa = (QP[:, 0, :], QP[:, 1, :]); bqa = (bQP, bQP)
                            for hh in range(2):
                                T("matmul", pnd[:, 0:W], qa[hh], Sbf[:, hh, 0:W], start=False, stop=(hh == 1), r=[bqa[hh], bSbf], w=[bnd])
                            if ret:
                                head_out(128, pnd[:, 0:256], gnr, b_gnr, (g1, g2), (bg1, bg2), c0, h)
                            else:
                                A("activation", hst[:, 10:11], pnd[:, 256:257], AF.Abs, r=[bnd], w=[bhst])
                                V("tensor_tensor", hst[:, 10:11], hst[:, 10:11], colA[:, c, 2 * H + h:2 * H + h + 1], ALU.max, r=[bhst, bcolA], w=[bhst])
                                V("reciprocal", hst[:, 11:12], hst[:, 10:11], r=[bhst], w=[bhst])
                                head_out(128, pnd[:, 0:256], gnm, b_gnm, (g1, g2), (bg1, bg2), c0, h, r_ap=hst[:, 11:12])
                        for hh in range(2):
                            T("matmul", pU[hh][:, 0:W], kc[:, hh * 128:(hh + 1) * 128], vc[:, 0:W], start=True, stop=True, r=[bkc, bvc], w=[bU[hh]])
                        for hh in range(2):
                            sc_ = g128 if ret else wcbc[:, h, c:c + 1]
                            V("scalar_tensor_tensor", S32[:, hh, 0:W], S32[:, hh, 0:W], sc_, pU[hh][:, 0:W], ALU.mult, ALU.add, r=[bS32, bU[hh], bwcbc], w=[bS32])
                        if c < NCH - 1 and not pre:
                            A("activation", Sbf[:, :, 0:W], S32[:, :, 0:W], AF.Identity, r=[bS32], w=[bSbf])
                    if pre:
                        V("tensor_scalar_mul", S32[:, :, 0:W], S32[:, :, 0:W], flag[:, :], r=[bS32, b_flag], w=[bS32])
                        dst = (bstate_r[h].rearrange("p (t e) -> p t e", t=2) if ret else bstate_c[h].rearrange("p (t e) -> p t e", t=2))
                        P.dma("sp", dst, S32[:, :, 0:W], bS32, reads=[bS32])
                        return
                    if ret:
                        P.dma("sp", o_retp[h].rearrange("(t p) e -> p t e", p=128), S32[:, :, 0:256], bS32, reads=[bS32])
                    else:
                        P.dma("sp", o_cp[h].rearrange("(t p) e -> p t e", p=128), S32[:, :, 0:256], bS32, reads=[bS32])
                        for t_ in range(2):
                            P.dma("sp", o_np[h:h + 1, t_ * 128:(t_ + 1) * 128].rearrange("o p -> p o"), S32[:, t_, 256:257], bS32, reads=[bS32], allow_slow_non_contiguous=True)
                    if SD > 0:
                        cs0 = NT
                        tr_to_tokens((v1, v2), (bv1, bv2), cs0, SD, 0, vsd, bvsd)
                        if ret:
                            tr_to_tokens((k1, k2), (bk1, bk2), cs0, SD, 1, ksd, bksd)
                        else:
                            V("memset", vsd[:SD, 256:257], 1.0, w=[bvsd])
                            tr_to_tokens((k1, k2), (bk1, bk2), cs0, SD, 1, ksd, bksd, scale=scol[:SD, h:h + 1], bscale=[bscol])
                            for t_ in range(2):
                                load(nS[:, t_, :], smn[:, h, t_ * 128:(t_ + 1) * 128].rearrange("s p -> p s"), bnS, allow_slow_non_contiguous=True)
                        V("tensor_copy", qsd[:, 0, :], q1[:, cs0:cs0 + SD], r=[bq1], w=[bqsd])
                        V("tensor_copy", qsd[:, 1, :], q2[:, cs0:cs0 + SD], r=[bq2], w=[bqsd])
                        for hh in range(2):
                            V("tensor_tensor", qms[:, hh, :].rearrange("p (s t) -> p s t", t=SD), qsd[:, hh, :].unsqueeze(2).to_broadcast([128, SD, SD]),
                              sdmask[:, :].rearrange("p (s t) -> p s t", t=SD), ALU.mult, r=[bqsd, b_sdm], w=[bqms])
                        def ld_state(s):
                            src_ = sret if ret else smc
                            load(SS[s % 4][:, :, 0:256], src_[s, h].rearrange("(t p) e -> p t e", p=128), bSS[s % 4])

                        for s in range(min(2, SD)):
                            ld_state(s)
                        for s in range(SD):
                            St, bSt = SS[s % 4], bSS[s % 4]
                            if s + 2 < SD:
                                ld_state(s + 2)
                            if not ret:
                                G("tensor_copy", St[:, :, 256], nS[:, :, s], r=[bnS], w=[bSt])
                            V("tensor_scalar_mul", kms[:SD, :], ksd[:SD, :], ident[:SD, s:s + 1], r=[bksd, b_ident], w=[bkms])
                            for hh in range(2):
                                T("matmul", pU[hh][:, 0:W], kms[:SD, hh * 128:(hh + 1) * 128], vsd[:SD, 0:W], start=True, stop=True, r=[bkms, bvsd], w=[bU[hh]])
                            for hh in range(2):
                                sc_ = gam[h] if ret else wpbcS[:, h, s:s + 1]
                                V("scalar_tensor_tensor", St[:, hh, 0:W], St[:, hh, 0:W], sc_, pU[hh][:, 0:W], ALU.mult, ALU.add, r=[bSt, bU[hh], bwpbcS], w=[bSt])
                            for hh in range(2):
                                T("matmul", pnd[:SD, 0:W], qms[:, hh, s * SD:(s + 1) * SD], St[:, hh, 0:W], start=(s == 0 and hh == 0), stop=(s == SD - 1 and hh == 1),
                                  r=[bqms, bSt], w=[bnd])
                            if ret:
                                P.dma("sp", o_rets[s, h].rearrange("(t p) e -> p t e", p=128), St[:, :, 0:256], bSt, reads=[bSt])
                            else:
                                P.dma("sp", o_cs[s, h].rearrange("(t p) e -> p t e", p=128), St[:, :, 0:256], bSt, reads=[bSt])
                                G("tensor_copy", nS[:, :, s], St[:, :, 256], r=[bSt], w=[bnS])
                        if ret:
                            head_out(SD, pnd[:SD, 0:256], gnr, b_gnr, (g1, g2), (bg1, bg2), cs0, h)
                        else:
                            for t_ in range(2):
                                P.dma("sp", o_ns[:, h, t_ * 128:(t_ + 1) * 128].rearrange("s p -> p s"), nS[:, t_, :], bnS, reads=[bnS], allow_slow_non_contiguous=True)
                            A("activation", hst[:SD, 10:11], pnd[:SD, 256:257], AF.Abs, r=[bnd], w=[bhst])
                            V("tensor_tensor", hst[:SD, 10:11], hst[:SD, 10:11], scol[:SD, H + h:H + h + 1], ALU.max, r=[bhst, bscol], w=[bhst])
                            V("reciprocal", hst[:SD, 11:12], hst[:SD, 10:11], r=[bhst], w=[bhst])
                            head_out(SD, pnd[:SD, 0:256], gnm, b_gnm, (g1, g2), (bg1, bg2), cs0, h, r_ap=hst[:SD, 11:12])
                    row0 = (0 if ret else RW) + h * 256
                    P.dma("sp", o_mix[row0:row0 + 256, :].rearrange("(t p) n -> p t n", p=128), mixT[:, :, :], bmix, reads=[bmix])

                gate_stage()
                for h in range(H):
                    run_head("ret", h)
                    run_head("ml", h)
                if not pre:
                    P.dma("sp", o_mp, mrow[:, NCH:NCH + 1], bmrow, reads=[bmrow])
                    for j_ in range(3):
                        P.dma("sp", o_convp[j_].rearrange("(t p) -> p t", p=128), convp[:, j_, :], bconvp, reads=[bconvp], allow_slow_non_contiguous=True)
            P.barrier()

        own_tiles = [(xo[c * 128:(c + 1) * 128, :], 128, c * 128, b_XT[c]) for c in range(NCH)]
        if SD > 0:
            own_tiles.append((xs[:, :], SD, NT, b_XT[NCH]))
        pre_tiles = [(xp[c * 128:(c + 1) * 128, :], 128, c * 128, b_XT[c]) for c in range(NCH)]
        all_tiles = [(c * 128, 128) for c in range(NCH)] + ([(NT, SD)] if SD > 0 else [])

        def phase_wout():
            with ExitStack() as ph:
                if KTM <= KT:
                    mixS = XT
                else:
                    mixS = P.sb("mixS", [128, KTM, NTO], BF16, ph)
                bmixS = P.buf("mixS")
                load(mixS[:, 0:KTM, :], o_mix.rearrange("(t p) n -> p t n", p=128), bmixS)
                wo = [P.sb("wo", [128, KTM, CBS], BF16, ph) for _ in range(2)]; bwo = [P.buf("wo") for _ in range(2)]
                pyo = [P.ps("pyo", [128, 512], F32, ph) for _ in range(2)]; bpyo = [P.buf("pyo") for _ in range(2)]
                ysb = [P.sb("ysb", [128, CBS], F32, ph) for _ in range(2)]; bysb = [P.buf("ysb") for _ in range(2)]
                it = 0
                for cb in range(NCB):
                    w_, bw_ = wo[cb % 2], bwo[cb % 2]
                    load(w_[:].rearrange("p k c -> p (k c)"), wo_d[cb], bw_, q="pool")
                    for (c0, n) in all_tiles:
                        p_, bp = pyo[it % 2], bpyo[it % 2]
                        y_, by = ysb[it % 2], bysb[it % 2]
                        it += 1
                        for kt in range(KTM):
                            T("matmul", p_[:n, :CBS], mixS[:, kt, c0:c0 + n], w_[:, kt, :], start=(kt == 0), stop=(kt == KTM - 1), r=[bmixS, bw_], w=[bp])
                        A("activation", y_[:n, :], p_[:n, :CBS], AF.Identity, r=[bp], w=[by])
                        P.dma("sp", y_d[c0:c0 + n, cb * CBS:(cb + 1) * CBS], y_[:n, :], by, reads=[by])
            P.barrier()

        def ln_rows(x, b, n, s, bs, pcs):
            nst = len(pcs)
            for c, (c0, c1) in enumerate(pcs):
                V("bn_stats", s[:n, c * 6:(c + 1) * 6], x[:n, c0:c1], r=[b], w=[bs])
            o = nst * 6
            V("bn_aggr", s[:n, o:o + 2], s[:n, 0:o], r=[bs], w=[bs])
            rstd_op(s[:n, o + 2:o + 3], s[:n, o + 1:o + 2], n, [bs], [bs])
            V("scalar_tensor_tensor", s[:n, o + 3:o + 4], s[:n, o:o + 1], -1.0, s[:n, o + 2:o + 3], ALU.mult, ALU.mult, r=[bs], w=[bs])
            A("activation", x[:n, :], x[:n, :], AF.Identity, bias=s[:n, o + 3:o + 4], scale=s[:n, o + 2:o + 3], r=[b, bs], w=[b])

        def phase_ln1():
            with ExitStack() as ph:
                geb, bgeb = P.sb("geb", [128, D], F32, ph), P.buf("geb"); load(geb[:], geb_d, bgeb)
                beb, bbeb = P.sb("beb", [128, D], F32, ph), P.buf("beb"); load(beb[:], beb_d, bbeb)
                g1b, bg1b = P.sb("g1b", [128, D], F32, ph), P.buf("g1b"); load(g1b[:], g1b_d, bg1b)
                b1b, bb1b = P.sb("b1b", [128, D], F32, ph), P.buf("b1b"); load(b1b[:], b1b_d, bb1b)
                xt_ = P.sb("l1x", [128, D], F32, ph); bx = P.buf("l1x")
                yt_ = P.sb("l1y", [128, D], F32, ph); by = P.buf("l1y")
                pcs = pieces(D)
                st = P.sb("l1st", [128, len(pcs) * 6 + 8], F32, ph); bst = P.buf("l1st")
                pt = [P.ps("l1pt", [128, 4, 128], F32, ph) for _ in range(2)]; bpt = [P.buf("l1pt") for _ in range(2)]
                gi = 0
                for ti, (c0, n) in enumerate(all_tiles):
                    src = xo[c0:c0 + n, :] if c0 < NT else xs[:, :]
                    load(xt_[:n, :], src, bx)
                    load(yt_[:n, :], y_d[c0:c0 + n, :], by)
                    ln_rows(xt_, bx, n, st, bst, pcs)
                    V("tensor_tensor", xt_[:n, :], xt_[:n, :], geb[:n, :], ALU.mult, r=[bx, bgeb], w=[bx])
                    G("tensor_tensor", xt_[:n, :], xt_[:n, :], beb[:n, :], ALU.add, r=[bx, bbeb], w=[bx])
                    V("scalar_tensor_tensor", yt_[:n, :], xt_[:n, :], ALPHA, yt_[:n, :], ALU.mult, ALU.add, r=[bx, by], w=[by])
                    ln_rows(yt_, by, n, st, bst, pcs)
                    V("tensor_tensor", yt_[:n, :], yt_[:n, :], g1b[:n, :], ALU.mult, r=[by, bg1b], w=[by])
                    G("tensor_tensor", yt_[:n, :], yt_[:n, :], b1b[:n, :], ALU.add, r=[by, bb1b], w=[by])
                    P.dma("sp", x1_d[c0:c0 + n, :], yt_[:n, :], by, reads=[by])
                    xb = b_XT[ti]
                    for k0 in range(0, KT, 4):
                        p_, bp = pt[gi % 2], bpt[gi % 2]; gi += 1
                        nk = min(4, KT - k0)
                        for j in range(nk):
                            T("transpose", p_[:, j, :n], yt_[:n, (k0 + j) * 128:(k0 + j + 1) * 128], ident[:n, :n], r=[by, b_ident], w=[bp])
                        if (k0 // 4) % 2 == 0:
                            A("activation", XT[:, k0:k0 + nk, c0:c0 + n], p_[:, 0:nk, :n], AF.Identity, r=[bp], w=[xb])
                        else:
                            V("tensor_copy", XT[:, k0:k0 + nk, c0:c0 + n], p_[:, 0:nk, :n], r=[bp], w=[xb])
            P.barrier()

        def phase_peer():
            TG = cfg.TG
            groups = [all_tiles[i:i + TG] for i in range(0, len(all_tiles), TG)]
            tile_idx = {c0: i for i, (c0, n) in enumerate(all_tiles)}
            HP = PH * 2
            for grp in groups:
                ng = len(grp)
                with ExitStack() as gph:
                    yacc = P.sb("yacc", [128, TG, D], F32, gph); byacc = [P.buf("yacc") for _ in range(TG)]
                    xbufs = [b_XT[tile_idx[c0]] for (c0, n) in grp]
                    s_ = P.sb("s_", [128, TG, HP * NK], F32, gph); bs_ = [P.buf("s_") for _ in range(TG)]
                    rt = P.sb("rt", [128, TG, 4 * PH], F32, gph); brt = [P.buf("rt") for _ in range(TG)]
                    with ExitStack() as ph:
                        phh = [P.ps("phh", [128, 512], F32, ph) for _ in range(TG)]; bphh = [P.buf("phh") for _ in range(TG)]
                        py = [P.ps("py", [128, 512], F32, ph) for _ in range(2)]; bpy = [P.buf("py") for _ in range(2)]
                        pmi = [P.ps("pmi", [128, 512], F32, ph) for _ in range(2)]; bpmi = [P.buf("pmi") for _ in range(2)]
                        wr = [P.sb("wr", [128, KG, 512], BF16, ph) for _ in range(4)]; bwr = [P.buf("wr") for _ in range(4)]
                        wrs = {"i": 0}
                        wpp = [P.sb("wpp", [128, PT_, CBS], BF16, ph) for _ in range(2)]; bwpp = [P.buf("wpp") for _ in range(2)]
                        skT, bskT = P.sb("skT", [128, HP * NK], F32, ph), P.buf("skT"); load(skT[:], skT_d, bskT)
                        sv = P.sb("sv", [128, HP * 16], F32, ph); bsv = P.buf("sv")
                        scr = P.sb("scr", [128, 512], F32, ph); bscr = P.buf("scr")
                        cand = [P.sb("cand", [128, 256], F32, ph) for _ in range(2)]; bcand = [P.buf("cand") for _ in range(2)]
                        tv = P.sb("tv", [128, PH * 24], F32, ph); btv = P.buf("tv")
                        qsb = P.sb("qsb", [128, 512], F32, ph); bqsb = P.buf("qsb")
                        qT = P.sb("qT", [128, 4, 128], F32, ph); bqT = P.buf("qT")
                        pTt = P.sb("pTt", [128, TG, PT_ * 128], BF16, ph); bpTt = [P.buf("pTt") for _ in range(TG)]
                        prow = P.sb("prow", [128, PLE], F32, ph); bprow = P.buf("prow")
                        sg = [P.sb("sg", [128, CBS], F32, ph) for _ in range(1)]; bsg = [P.buf("sg") for _ in range(1)]
                        rng = {"T": 0, "a": 0, "py": 0, "sg": 0, "mi": 0}

                        def wpiece(src_ap, ncols):
                            i = wrs["i"] % 4; wrs["i"] += 1
                            load(wr[i][:, :, 0:ncols], src_ap.rearrange("p (k c) -> p k c", c=ncols), bwr[i], q="pool")
                            return wr[i], bwr[i]

                        for gi, (c0, n) in enumerate(grp):
                            load(yacc[:n, gi, :], x1_d[c0:c0 + n, :], byacc[gi])
                            psrc = po[c0:c0 + n, :] if c0 < NT else ps_d[:, :]
                            load(prow[:n, :], psrc, bprow)
                            p_, bp = pmi[rng["mi"] % 2], bpmi[rng["mi"] % 2]; rng["mi"] += 1
                            for j in range(PT_):
                                T("transpose", p_[:, j * 128:j * 128 + n], prow[:n, j * 128:(j + 1) * 128], ident[:n, :n], r=[bprow, b_ident], w=[bp])
                            for j in range(PT_):
                                A("activation", pTt[:, gi, j * 128:j * 128 + n], p_[:, j * 128:j * 128 + n], AF.Identity, r=[bp], w=[bpTt[gi]])

                        for gi, (c0, n) in enumerate(grp):
                            pass
                        for cq in range(NCQ):
                            for kg in range(NKG):
                                wp_, bwp_ = wpiece(wq_d[cq, kg], CQ)
                                for gi, (c0, n) in enumerate(grp):
                                    for j in range(KG):
                                        kt = kg * KG + j
                                        T("matmul", phh[gi][:n, :CQ], XT[:, kt, c0:c0 + n], wp_[:, j, :CQ], start=(kt == 0), stop=(kt == KT - 1), r=[xbufs[gi], bwp_], w=[bphh[gi]])
                            for gi, (c0, n) in enumerate(grp):
                                A("activation", qsb[:n, :CQ], phh[gi][:n, :CQ], AF.Identity, r=[bphh[gi]], w=[bqsb])
                                nj = CQ // 128
                                p_, bp = pmi[rng["mi"] % 2], bpmi[rng["mi"] % 2]; rng["mi"] += 1
                                for j in range(nj):
                                    T("transpose", p_[:, j * 128:j * 128 + n], qsb[:n, j * 128:(j + 1) * 128], ident[:n, :n], r=[bqsb, b_ident], w=[bp])
                                for j in range(nj):
                                    A("activation", qT[:, j, :n], p_[:, j * 128:j * 128 + n], AF.Identity, r=[bp], w=[bqT])
                                hpb = 512 // NK if NK <= 512 else 1
                                for j0 in range(0, nj, hpb):
                                    p2, bp2 = pmi[rng["mi"] % 2], bpmi[rng["mi"] % 2]; rng["mi"] += 1
                                    jn = min(hpb, nj - j0)
                                    for j in range(j0, j0 + jn):
                                        hp = cq * nj + j
                                        T("matmul", p2[:n, (j - j0) * NK:(j - j0 + 1) * NK], qT[:, j, :n], skT[:, hp * NK:(hp + 1) * NK], start=True, stop=True, r=[bqT, bskT], w=[bp2])
                                    hp0 = cq * nj + j0
                                    A("activation", s_[:n, gi, hp0 * NK:(hp0 + jn) * NK], p2[:n, 0:jn * NK], AF.Identity, r=[bp2], w=[bs_[gi]])
                        for gi, (c0, n) in enumerate(grp):
                            s3 = s_[:, gi, :].rearrange("p (a k) -> p a k", k=NK)
                            sv3 = sv[:, :].rearrange("p (a k) -> p a k", k=16)
                            for hp in range(HP):
                                V("max", sv3[:n, hp, 0:8], s3[:n, hp, :], r=[bs_[gi]], w=[bsv])
                                V("match_replace", scr[:n, 0:NK], sv3[:n, hp, 0:8], s3[:n, hp, :], -1e30, r=[bs_[gi], bsv], w=[bscr])
                                V("max", sv3[:n, hp, 8:16], scr[:n, 0:NK], r=[bscr], w=[bsv])
                            tv3 = tv[:, :].rearrange("p (h k) -> p h k", k=24)
                            r4 = rt[:, gi, :].rearrange("p (a h) -> p a h", h=PH)
                            for h in range(PH):
                                V("tensor_tensor", cand[0][:n, :].rearrange("p (a b) -> p a b", b=16), sv3[:n, 2 * h, :].unsqueeze(2).to_broadcast([n, 16, 16]),
                                  sv3[:n, 2 * h + 1, :].unsqueeze(1).to_broadcast([n, 16, 16]), ALU.add, r=[bsv], w=[bcand[0]])
                                V("max", tv3[:n, h, 0:8], cand[0][:n, :], r=[bcand[0]], w=[btv])
                                V("match_replace", cand[1][:n, :], tv3[:n, h, 0:8], cand[0][:n, :], -1e30, r=[bcand[0], btv], w=[bcand[1]])
                                V("max", tv3[:n, h, 8:16], cand[1][:n, :], r=[bcand[1]], w=[btv])
                                V("match_replace", cand[0][:n, :], tv3[:n, h, 8:16], cand[1][:n, :], -1e30, r=[bcand[1], btv], w=[bcand[0]])
                                V("max", tv3[:n, h, 16:24], cand[0][:n, :], r=[bcand[0]], w=[btv])
                            V("tensor_tensor", r4[:n, 0, :], tv3[:n, :, 15], tv3[:n, :, 16], ALU.add, r=[btv], w=[brt[gi]])
                            V("tensor_scalar_mul", r4[:n, 0, :], r4[:n, 0, :], 0.5, r=[brt[gi]], w=[brt[gi]])
                            V("tensor_scalar_mul", r4[:n, 1, :], tv3[:n, :, 0], -1.0, r=[btv], w=[brt[gi]])
                            for h in range(PH):
                                A("activation", scr[:n, 0:16], tv3[:n, h, 0:16], AF.Exp, bias=r4[:n, 1, h:h + 1], accum_out=r4[:n, 2, h:h + 1], r=[btv, brt[gi]], w=[bscr, brt[gi]])
                            A("activation", r4[:n, 2, :], r4[:n, 2, :], AF.Ln, r=[brt[gi]], w=[brt[gi]])
                            V("tensor_tensor", r4[:n, 3, :], r4[:n, 1, :], r4[:n, 2, :], ALU.subtract, r=[brt[gi]], w=[brt[gi]])

                        for cb in range(NCB):
                            wq_, bwq_ = wpp[cb % 2], bwpp[cb % 2]
                            load(wq_[:].rearrange("p k c -> p (k c)"), wpp_d[cb], bwq_, q="pool")
                            for kg in range(NKG):
                                wp_, bwp_ = wpiece(wpg_d[cb, kg], CBS)
                                for gi, (c0, n) in enumerate(grp):
                                    for j in range(KG):
                                        kt = kg * KG + j
                                        T("matmul", phh[gi][:n, :CBS], XT[:, kt, c0:c0 + n], wp_[:, j, :CBS], start=(kt == 0), stop=(kt == KT - 1), r=[xbufs[gi], bwp_], w=[bphh[gi]])
                            for gi, (c0, n) in enumerate(grp):
                                sg_, bsg_ = sg[0], bsg[0]
                                p_, bp = py[rng["py"] % 2], bpy[rng["py"] % 2]; rng["py"] += 1
                                A("activation", sg_[:n, :], phh[gi][:n, :CBS], AF.Sigmoid, r=[bphh[gi]], w=[bsg_])
                                for j in range(PT_):
                                    T("matmul", p_[:n, :CBS], pTt[:, gi, j * 128:j * 128 + n], wq_[:, j, :], start=(j == 0), stop=(j == PT_ - 1), r=[bpTt[gi], bwq_], w=[bp])
                                V("tensor_tensor", sg_[:n, :], sg_[:n, :], p_[:n, :CBS], ALU.mult, r=[bsg_, bp], w=[bsg_])
                                ysl = yacc[:n, gi, cb * CBS:(cb + 1) * CBS]
                                V("scalar_tensor_tensor", ysl, ysl, ALPHA, sg_[:n, :], ALU.mult, ALU.add, r=[byacc[gi], bsg_], w=[byacc[gi]])

                    P.barrier()
                    with ExitStack() as ph:
                        NPY = max(2, min(3, 8 - TG - 2))
                        phh = [P.ps("phh", [128, 512], F32, ph) for _ in range(TG)]; bphh = [P.buf("phh") for _ in range(TG)]
                        py = [P.ps("py", [128, 512], F32, ph) for _ in range(NPY)]; bpy = [P.buf("py") for _ in range(NPY)]
                        paT = [P.ps("paT", [128, 1024], BF16, ph) for _ in range(2)]; bpaT = [P.buf("paT") for _ in range(2)]
                        NR = 8
                        ring = [P.sb("wring", [128, max(KG, ET), 512], BF16, ph) for _ in range(NR)]; bring = [P.buf("wring") for _ in range(NR)]
                        Tt = [P.sb("Tt", [128, EB], F32, ph) for _ in range(2)]; bTt = [P.buf("Tt") for _ in range(2)]
                        Ex = [P.sb("Ex", [128, EB], BF16, ph) for _ in range(2)]; bEx = [P.buf("Ex") for _ in range(2)]
                        Mm = [P.sb("Mm", [128, EB], BF16, ph) for _ in range(2)]; bMm = [P.buf("Mm") for _ in range(2)]
                        Gacc = [[P.sb("Gacc", [128, EB], BF16, ph) for _ in range(TG)] for _ in range(2)]
                        bGacc = [[P.buf("Gacc") for _ in range(TG)] for _ in range(2)]
                        asb = [P.sb("asb", [128, EB], BF16, ph) for _ in range(2)]; basb = [P.buf("asb") for _ in range(2)]
                        abf = [P.sb("abf", [128, EB], BF16, ph) for _ in range(2)]; babf = [P.buf("abf") for _ in range(2)]
                        aT = [P.sb("aT", [128, ET, 128], BF16, ph) for _ in range(TG)]; baT = [P.buf("aT") for _ in range(TG)]
                        rng = {"T": 0, "a": 0, "py": 0, "sg": 0, "mi": 0}

                        seq = []
                        for eb in range(NEB):
                            seq += [("u", eb, kg) for kg in range(NKG)]
                            seq += [("v", eb, cb) for cb in range(NCB)]
                        wst = {"pos": 0}

                        def wnext():
                            j = wst["pos"]; wst["pos"] += 1
                            kind, eb_, x = seq[j]
                            t_, b_ = ring[j % NR], bring[j % NR]
                            if kind == "u":
                                load(t_[:, 0:KG, 0:EB], uT_d[eb_, x].rearrange("p (k c) -> p k c", c=EB), b_, q="pool")
                            else:
                                load(t_[:, 0:ET, 0:CBS], v_d[eb_, x].rearrange("p (k c) -> p k c", c=CBS), b_, q="pool")
                            return t_, b_

                        def gate_units(eb, par):
                            units = []
                            i0 = eb * NI
                            for gi, (c0, n) in enumerate(grp):
                                for h in range(PH):
                                    st = {}

                                    def unit_a(gi=gi, n=n, h=h, st=st):
                                        s3 = s_[:, gi, :].rearrange("p (a k) -> p a k", k=NK)
                                        r4 = rt[:, gi, :].rearrange("p (a h) -> p a h", h=PH)
                                        ti = rng["T"] % 2; rng["T"] += 1
                                        st["ti"] = ti
                                        V("tensor_tensor", Tt[ti][:n, :].rearrange("p (i k) -> p i k", k=NK), s3[:n, 2 * h, i0:i0 + NI].unsqueeze(2).to_broadcast([n, NI, NK]),
                                          s3[:n, 2 * h + 1, :].unsqueeze(1).to_broadcast([n, NI, NK]), ALU.add, r=[bs_[gi]], w=[bTt[ti]])
                                        A("activation", Ex[ti][:n, :], Tt[ti][:n, :], AF.Exp, bias=r4[:n, 3, h:h + 1], r=[bTt[ti], brt[gi]], w=[bEx[ti]])

                                    def unit_b(gi=gi, n=n, h=h, st=st):
                                        r4 = rt[:, gi, :].rearrange("p (a h) -> p a h", h=PH)
                                        GA, bGA = Gacc[par][gi], bGacc[par][gi]
                                        ti = st["ti"]
                                        dst, bdst = (GA, bGA) if h == 0 else (Mm[ti], bMm[ti])
                                        V("scalar_tensor_tensor", dst[:n, :], Tt[ti][:n, :], r4[:n, 0, h:h + 1], Ex[ti][:n, :], ALU.is_ge, ALU.mult, r=[bTt[ti], bEx[ti], brt[gi]], w=[bdst])
                                        if h > 0:
                                            V("tensor_tensor", GA[:n, :], GA[:n, :], Mm[ti][:n, :], ALU.add, r=[bGA, bMm[ti]], w=[bGA])
                                    units.append((unit_a, unit_b))
                            return units

                        pend = {"b": None}

                        def run_unit(u_):
                            u_[0]()
                            if pend["b"] is not None:
                                pend["b"]()
                            pend["b"] = u_[1]

                        def flush_units():
                            if pend["b"] is not None:
                                pend["b"]()
                                pend["b"] = None

                        for u_ in gate_units(0, 0):
                            run_unit(u_)
                        flush_units()
                        for eb in range(NEB):
                            par = eb % 2
                            nxt = gate_units(eb + 1, 1 - par) if eb + 1 < NEB else []
                            UQ = max(0, -(-(len(nxt) - NCB) // NKG))
                            for kg in range(NKG):
                                wp_, bwp_ = wnext()
                                for gi, (c0, n) in enumerate(grp):
                                    for j in range(KG):
                                        kt = kg * KG + j
                                        T("matmul", phh[gi][:n, :EB], XT[:, kt, c0:c0 + n], wp_[:, j, :EB], start=(kt == 0), stop=(kt == KT - 1), r=[xbufs[gi], bwp_], w=[bphh[gi]])
                                for _ in range(UQ):
                                    if nxt:
                                        run_unit(nxt.pop(0))
                            for gi, (c0, n) in enumerate(grp):
                                ai = rng["a"] % 2; rng["a"] += 1
                                A("activation", asb[ai][:n, :], phh[gi][:n, :EB], AF.Gelu, r=[bphh[gi]], w=[basb[ai]])
                                V("tensor_tensor", abf[ai][:n, :], asb[ai][:n, :], Gacc[par][gi][:n, :], ALU.mult, r=[basb[ai], bGacc[par][gi]], w=[babf[ai]])
                                pa, bpa = paT[gi % 2], bpaT[gi % 2]
                                for j in range(ET):
                                    T("transpose", pa[:, j * 128:j * 128 + n], abf[ai][:n, j * 128:(j + 1) * 128], identb[:n, :n], r=[babf[ai], b_identb], w=[bpa])
                                for j in range(ET):
                                    A("activation", aT[gi][:, j, :n], pa[:, j * 128:j * 128 + n], AF.Identity, r=[bpa], w=[baT[gi]])
                            for cb in range(NCB):
                                vr_, bvr_ = wnext()
                                for gi, (c0, n) in enumerate(grp):
                                    p_, bp = py[rng["py"] % NPY], bpy[rng["py"] % NPY]; rng["py"] += 1
                                    for j in range(ET):
                                        T("matmul", p_[:n, :CBS], aT[gi][:, j, :n], vr_[:, j, :CBS], start=(j == 0), stop=(j == ET - 1), r=[baT[gi], bvr_], w=[bp])
                                    ysl = yacc[:n, gi, cb * CBS:(cb + 1) * CBS]
                                    V("tensor_tensor", ysl, ysl, p_[:n, :CBS], ALU.add, r=[byacc[gi], bp], w=[byacc[gi]])
                                if nxt:
                                    run_unit(nxt.pop(0))
                            while nxt:
                                run_unit(nxt.pop(0))
                            flush_units()
                    P.barrier()
                    with ExitStack() as ph2:
                        g2b, bg2b = P.sb("g2b", [128, D], F32, ph2), P.buf("g2b"); load(g2b[:], g2b_d, bg2b)
                        b2b, bb2b = P.sb("b2b", [128, D], F32, ph2), P.buf("b2b"); load(b2b[:], b2b_d, bb2b)
                        pcs = pieces(D)
                        st = P.sb("l2st", [128, len(pcs) * 6 + 8], F32, ph2); bst = P.buf("l2st")
                        for gi, (c0, n) in enumerate(grp):
                            yv = yacc[:, gi, :]
                            ln_rows(yv, byacc[gi], n, st, bst, pcs)
                            V("tensor_tensor", yv[:n, :], yv[:n, :], g2b[:n, :], ALU.mult, r=[byacc[gi], bg2b], w=[byacc[gi]])
                            G("tensor_tensor", yv[:n, :], yv[:n, :], b2b[:n, :], ALU.add, r=[byacc[gi], bb2b], w=[byacc[gi]])
                            P.dma("sp", y_o[c0:c0 + n, :], yv[:n, :], byacc[gi], reads=[byacc[gi]])
                    P.barrier()

        stop = int(os.environ.get("KSTOP", "99"))
        if stop >= 1:
            phase_convbuf()
        if stop >= 2:
            phase_ln(pre_tiles)
        if stop >= 3:
            mixer_phase("pre")
        if stop >= 4:
            phase_ln(own_tiles)
        if stop >= 5:
            mixer_phase("own")
        if stop >= 6:
            phase_wout()
        if stop >= 7:
            phase_ln1()
        if stop >= 8:
            phase_peer()
        P.emit()
    return nc


ROPE_BASE = 10000.0
PAST_LEN = 16384


def col_layout(v, ntile):
    return np.ascontiguousarray(np.asarray(v, np.float32).reshape(ntile, 128).T)


def rope_tables(pos, H, dq_fn, dk_fn, idx):
    half = 128
    inv = (np.float32(ROPE_BASE) ** (-np.arange(half, dtype=np.float32) / np.float32(half))).astype(np.float32)
    ang = pos.astype(np.float32)[:, None] * inv[None]
    cos = np.cos(ang).astype(np.float32).T
    sin = np.sin(ang).astype(np.float32).T
    tq = np.empty((H, 2, 128, len(pos)), np.float32)
    tk = np.empty((H, 2, 128, len(pos)), np.float32)
    for h in range(H):
        dq = dq_fn(h, idx)[None, :]
        dk = dk_fn(h, idx)[None, :]
        tq[h, 0] = cos * dq; tq[h, 1] = sin * dq
        tk[h, 0] = cos * dk; tk[h, 1] = sin * dk
    return tq, tk


_NC_CACHE = {}


def kernel(x_prompt, x_sample, state_ret, state_conv, state_mlstm_c, state_mlstm_n, state_mlstm_m, p_prompt, p_sample,
           ln_emb_g, ln_emb_b, w_in, b_gate, conv_w, conv_b, g_ret_norm, g_ml_norm, w_out, ln1_g, ln1_b,
           w_peer_q, peer_sub_keys, peer_u, peer_v, w_ple_gate, w_ple_proj, ln2_g, ln2_b, _debug=None):
    f32 = np.float32
    B, SEQ, D = x_prompt.shape
    DB = x_sample.shape[0]
    H = state_ret.shape[2]
    cps = NCORES // B
    assert cps == 2
    cfg = Cfg()
    cfg.D = D; cfg.KT = D // 128; cfg.H = H
    cfg.NT = SEQ // cps; cfg.SD = DB // NCORES
    NT, SD, KT = cfg.NT, cfg.SD, cfg.KT
    NTO = NT + SD
    RW = H * 256
    NCT = 8 * RW // 128
    NMT = 2 * RW // 128
    cfg.NK = peer_sub_keys.shape[3]; cfg.PH = peer_sub_keys.shape[1]; cfg.PLE = p_prompt.shape[-1]
    cfg.ALPHA = float((2.0 * w_in.shape[0]) ** 0.25)
    ntiles = NT // 128 + (1 if SD > 0 else 0)
    cfg.TG = 3 if ntiles >= 3 and ntiles % 3 == 0 else 2
    cfg.TG = int(os.environ.get("KTG", cfg.TG))
    NK, PH, PLE = cfg.NK, cfg.PH, cfg.PLE
    NE = NK * NK
    KTM = 2 * RW // 128
    CBS = min(512, D); NCB = D // CBS
    KG = min(int(os.environ.get("KKG", "4")), KT); NKG = KT // KG
    QW = PH * 256; CQ = min(512, QW); NCQ = QW // CQ
    EB = min(512, NE); NEB = NE // EB; ET = EB // 128
    PT_ = PLE // 128
    key = (D, H, NT, SD, NK, PH, PLE)
    if key not in _NC_CACHE:
        _NC_CACHE[key] = build(cfg)
    nc = _NC_CACHE[key]

    w_in0 = np.asarray(w_in[0], f32)
    wm = np.ascontiguousarray(w_in0[:, :NCT * 128].reshape(KT, 128, NCT, 128).transpose(2, 1, 0, 3)).reshape(NCT, 128, KT * 128)
    wg = np.ascontiguousarray(w_in0[:, NCT * 128:].reshape(KT, 128, 2 * H).transpose(1, 0, 2)).reshape(128, KT * 2 * H)
    bg = np.ascontiguousarray(np.asarray(b_gate[0], f32).reshape(2, H).T)
    cw = np.asarray(conv_w[0], f32)
    convw = np.ascontiguousarray(cw.reshape(4, NMT, 128).transpose(2, 1, 0)).reshape(128, NMT * 4)
    convb = col_layout(conv_b[0], NMT)
    gnr = col_layout(np.asarray(g_ret_norm[0], f32).reshape(-1), 2 * H)
    gnm = col_layout(np.asarray(g_ml_norm[0], f32).reshape(-1), 2 * H)
    lng = col_layout(ln_emb_g, KT); lnb = col_layout(ln_emb_b, KT)
    ident = np.eye(128, dtype=f32)
    jj, ii = np.meshgrid(np.arange(128), np.arange(128), indexing="ij")
    mask01 = (jj <= ii).astype(f32)
    maskneg = np.where(jj <= ii, 0.0, -30000.0).astype(f32)
    sel = np.zeros((128, H, 128), f32)
    for h in range(H):
        sel[h, h, :] = 1.0
    sel = sel.reshape(128, H * 128)
    sdmask = np.zeros((128, SD, SD), f32)
    for s in range(SD):
        sdmask[:, s, s] = 1.0
    sdmask = sdmask.reshape(128, SD * SD)

    wo = np.ascontiguousarray(np.asarray(w_out[0], f32).reshape(KTM, 128, NCB, CBS).transpose(2, 1, 0, 3)).reshape(NCB, 128, KTM * CBS)
    bc = lambda v: np.ascontiguousarray(np.broadcast_to(np.asarray(v, f32).reshape(1, D), (128, D)))
    geb, beb, g1b, b1b, g2b, b2b = bc(ln_emb_g), bc(ln_emb_b), bc(ln1_g[0]), bc(ln1_b[0]), bc(ln2_g[0]), bc(ln2_b[0])
    wq = np.ascontiguousarray(np.asarray(w_peer_q[0], f32).reshape(NKG, KG, 128, NCQ, CQ).transpose(3, 0, 2, 1, 4)).reshape(NCQ, NKG, 128, KG * CQ)
    skT = np.ascontiguousarray(np.asarray(peer_sub_keys[0], f32).transpose(3, 0, 1, 2)).reshape(128, PH * 2 * NK)
    uT = np.ascontiguousarray(np.asarray(peer_u[0], f32).reshape(NEB, EB, NKG, KG, 128).transpose(0, 2, 4, 3, 1)).reshape(NEB, NKG, 128, KG * EB)
    vv = np.ascontiguousarray(np.asarray(peer_v[0], f32).reshape(NEB, ET, 128, NCB, CBS).transpose(0, 3, 2, 1, 4)).reshape(NEB, NCB, 128, ET * CBS)
    wpg = np.ascontiguousarray(np.asarray(w_ple_gate[0], f32).reshape(NKG, KG, 128, NCB, CBS).transpose(3, 0, 2, 1, 4)).reshape(NCB, NKG, 128, KG * CBS)
    wpp = np.ascontiguousarray(np.asarray(w_ple_proj[0], f32).reshape(PT_, 128, NCB, CBS).transpose(2, 1, 0, 3)).reshape(NCB, 128, PT_ * CBS)
    ps_all = np.asarray(p_sample[0], f32).reshape(DB, PLE)

    gam = [np.float64(1.0 - 2.0 ** (-5.0 - h)) for h in range(H)]
    idx_own = np.concatenate([np.arange(NT) % 128, np.zeros(SD, np.int64)])
    is_s = np.concatenate([np.zeros(NT, bool), np.ones(SD, bool)])

    def dq_own(h, idx):
        return np.where(is_s, 1.0, gam[h] ** (idx + 1.0)).astype(f32)

    def dk_own(h, idx):
        return np.where(is_s, 1.0 / 16.0, gam[h] ** (-(idx + 1.0)) / 16.0).astype(f32)

    def dk_pre(h, idx):
        return (gam[h] ** (-(idx + 1.0)) / 16.0).astype(f32)

    in_maps = []
    tabs = {}
    for half in range(cps):
        pos_own = np.concatenate([half * NT + np.arange(NT), np.full(SD, PAST_LEN)])
        tq, tk = rope_tables(pos_own, H, dq_own, dk_own, idx_own)
        pos_pre = np.arange(NT)
        _, tkp = rope_tables(pos_pre, H, dk_pre, dk_pre, np.arange(NT) % 128)
        tabs[half] = (tq, tk, tkp)
    xs_all = np.asarray(x_sample, f32).reshape(DB, D)
    for c in range(NCORES):
        b, half = c // cps, c % cps
        xb = np.asarray(x_prompt[b], f32)
        xo = xb[half * NT:(half + 1) * NT]
        xp = xb[0:NT] if half == 1 else xo
        s0 = c * SD
        tq, tk, tkp = tabs[half]
        in_maps.append(dict(
            xo=np.ascontiguousarray(xo), xp=np.ascontiguousarray(xp), xs=np.ascontiguousarray(xs_all[s0:s0 + SD]),
            flag=np.full((128, 1), float(half), f32), lng=lng, lnb=lnb, wm=wm, wg=wg, bg=bg, convw=convw, convb=convb,
            gnr=gnr, gnm=gnm, tq=tq, tk=tk, tkp=tkp, ident=ident, mask01=mask01, maskneg=maskneg, sel=sel, sdmask=sdmask,
            sret=np.ascontiguousarray(state_ret[0, s0:s0 + SD]), sconv=np.ascontiguousarray(np.asarray(state_conv[0, s0:s0 + SD], f32).reshape(SD * 3, 2 * RW)),
            smc=np.ascontiguousarray(state_mlstm_c[0, s0:s0 + SD]), smn=np.ascontiguousarray(state_mlstm_n[0, s0:s0 + SD]),
            smm=np.ascontiguousarray(state_mlstm_m[0, s0:s0 + SD]),
            wo=wo, geb=geb, beb=beb, g1b=g1b, b1b=b1b, g2b=g2b, b2b=b2b, wq=wq, skT=skT, uT=uT, vv=vv, wpg=wpg, wpp=wpp,
            po=np.ascontiguousarray(np.asarray(p_prompt[0, b], f32)[half * NT:(half + 1) * NT]), ps=np.ascontiguousarray(ps_all[s0:s0 + SD]),
        ))
    res = run_bass_kernel_spmd(nc, in_maps, core_ids=list(range(NCORES))).results

    last = [b * cps + cps - 1 for b in range(B)]
    ret_p = np.stack([res[c]["o_retp"] for c in last])[None]
    conv_p = np.stack([res[c]["o_convp"] for c in last])[None]
    c_p = np.stack([res[c]["o_cp"] for c in last])[None]
    n_p = np.stack([res[c]["o_np"] for c in last])[None]
    m_p = np.stack([res[c]["o_mp"][:, 0] for c in last])[None]
    ret_s = np.concatenate([res[c]["o_rets"] for c in range(NCORES)])[None]
    conv_s = np.concatenate([res[c]["o_convs"].reshape(SD, 3, 2 * RW) for c in range(NCORES)])[None]
    c_s = np.concatenate([res[c]["o_cs"] for c in range(NCORES)])[None]
    n_s = np.concatenate([res[c]["o_ns"] for c in range(NCORES)])[None]
    m_s = np.concatenate([res[c]["o_ms"] for c in range(NCORES)])[None]
    if _debug is not None:
        _debug["mix"] = [np.asarray(res[c]["o_mix"]) for c in range(NCORES)]
    y_prompt = np.zeros((B, SEQ, D), f32)
    y_sample = np.zeros((DB, 1, D), f32)
    for c in range(NCORES):
        b, half = c // cps, c % cps
        yo = np.asarray(res[c]["y_o"])
        y_prompt[b, half * NT:(half + 1) * NT] = yo[:NT]
        y_sample[c * SD:(c + 1) * SD, 0] = yo[NT:]
    return (y_prompt, y_sample, ret_p.astype(f32), conv_p.astype(f32), c_p.astype(f32), n_p.astype(f32), m_p.astype(f32),
            ret_s.astype(f32), conv_s.astype(f32), c_s.astype(f32), n_s.astype(f32), m_s.astype(f32))
```
